# Optimizing a Trainium2 kernel written in Bass

```python
import jax, jax.numpy as jnp
from jax import lax
import numpy as np


D_MODEL = 1024
BATCH = 16
SEQ = 256
DEPTH = 1
DEC_BATCH = 8
DEC_SEQ = 1024
PAST_LEN = 512

GRID_W = 64
CHUNK = 128
N_DIR = 2
H_RET = 4
DK_RET = 128
DV_RET = 256
H_M = 4
DH_M = 128
CONV_W = 3
D_FF = 2816
ROPE_BASE = 10000.0
EPS = 1e-6
GN_EPS = 1e-5
RET_QK = H_RET * DK_RET
RET_V = H_RET * DV_RET
M_W = H_M * DH_M
D_IN = 2 * RET_QK + 2 * RET_V + 4 * M_W + 2 * N_DIR * H_M + N_DIR * D_MODEL

kernel_name = 'bidir_retention_mlstm_diffusion_step'


def _rmsnorm(x, g):
    xf = x.astype(jnp.float32)
    y = xf * lax.rsqrt(jnp.mean(xf * xf, axis=-1, keepdims=True) + EPS)
    return (y * g.astype(jnp.float32)).astype(x.dtype)


def _modulate(u, shift, scale):
    return u * (1 + scale) + shift


def _swiglu(u, w1, w3, w2):
    return (jax.nn.silu(u @ w1) * (u @ w3)) @ w2


def _head_norm(h, g):
    mu = jnp.mean(h, axis=-1, keepdims=True)
    hc = h - mu
    y = hc * lax.rsqrt(jnp.mean(hc * hc, axis=-1, keepdims=True) + GN_EPS)
    B, L, H, d = h.shape
    return y.reshape(B, L, H * d) * g.astype(jnp.float32)


def _short_conv(x, w, b):
    L = x.shape[1]
    pad = CONV_W // 2
    xp = jnp.pad(x, ((0, 0), (pad, CONV_W - 1 - pad), (0, 0)))
    y = b
    for j in range(CONV_W):
        y = y + w[j] * xp[:, j:j + L]
    return y


def _axial_rope(L):
    rows = L // GRID_W
    r = jnp.repeat(jnp.arange(rows, dtype=jnp.float32), GRID_W)
    col = (jnp.arange(L) % GRID_W).astype(jnp.float32)
    n_f = DK_RET // 4
    freqs = ROPE_BASE ** (-jnp.arange(n_f, dtype=jnp.float32) / n_f)
    ang = jnp.concatenate([r[:, None] * freqs, col[:, None] * freqs], axis=-1)
    return jnp.cos(ang), jnp.sin(ang)


def _apply_rope(x, cos, sin):
    half = x.shape[-1] // 2
    x1, x2 = x[..., :half], x[..., half:]
    c = cos[None, :, None, :]
    s = sin[None, :, None, :]
    return jnp.concatenate([x1 * c - x2 * s, x2 * c + x1 * s], axis=-1)


def _split_cols(proj):
    sizes = [RET_QK, RET_QK, RET_V, RET_V, M_W, M_W, M_W, M_W, N_DIR * H_M, N_DIR * H_M, N_DIR * D_MODEL]
    offs = [int(o) for o in np.cumsum(sizes)[:-1]]
    return jnp.split(proj, offs, axis=-1)


def _flip(a, d):
    return jnp.flip(a, axis=1) if d == 1 else a


def _retention_dir(q, k, v, lg, s0):
    B, L, H, dk = q.shape
    dv = v.shape[-1]
    N = L // CHUNK
    qc = q.reshape(B, N, CHUNK, H, dk)
    kc = k.reshape(B, N, CHUNK, H, dk)
    vc = v.reshape(B, N, CHUNK, H, dv)
    pos = jnp.arange(CHUNK, dtype=jnp.float32)
    diff = pos[:, None] - pos[None, :]
    mask = diff >= 0
    e = diff[None] * lg[:, None, None]
    dmat = jnp.where(mask, jnp.exp(jnp.where(mask, e, 0.0)), 0.0)
    scores = jnp.einsum('bnihd,bnjhd->bnhij', qc, kc) * dmat
    intra = jnp.einsum('bnhij,bnjhe->bnihe', scores, vc)
    wk = jnp.exp((CHUNK - 1 - pos)[:, None] * lg[None, :])
    kv = jnp.einsum('bnjhd,jh,bnjhe->bnhde', kc, wk, vc)
    g_chunk = jnp.exp(CHUNK * lg)[:, None, None]

    def step(s, kv_n):
        return g_chunk * s + kv_n, s

    s_fin, s_prev = lax.scan(step, s0, jnp.moveaxis(kv, 1, 0))
    s_prev = jnp.moveaxis(s_prev, 0, 1)
    wq = jnp.exp((pos + 1)[:, None] * lg[None, :])
    cross = jnp.einsum('bnihd,ih,bnhde->bnihe', qc, wq, s_prev)
    return (intra + cross).reshape(B, L, H, dv), s_fin


def _mlstm_dir(q, k, v, ig, lf, c0, n0, m0):
    B, L, H, d = q.shape
    N = L // CHUNK
    qc = q.reshape(B, N, CHUNK, H, d)
    kc = k.reshape(B, N, CHUNK, H, d)
    vc = v.reshape(B, N, CHUNK, H, d)
    igc = jnp.swapaxes(ig.reshape(B, N, CHUNK, H), 2, 3)
    b = jnp.cumsum(jnp.swapaxes(lf.reshape(B, N, CHUNK, H), 2, 3), axis=-1)
    b_last = b[..., -1]
    pos = jnp.arange(CHUNK)
    mask = pos[:, None] >= pos[None, :]
    dlog = jnp.where(mask, b[..., :, None] - b[..., None, :] + igc[..., None, :], -jnp.inf)
    g = b_last[..., None] - b + igc
    m_loc = jnp.max(g, axis=-1)
    w = jnp.exp(g - m_loc[..., None])
    c_loc = jnp.einsum('bnhj,bnjhv,bnjhk->bnhvk', w, vc, kc)
    n_loc = jnp.einsum('bnhj,bnjhk->bnhk', w, kc)

    def step(carry, xs):
        cs, ns, ms = carry
        bl, ml, cl, nl = xs
        m_new = jnp.maximum(bl + ms, ml)
        a1 = jnp.exp(bl + ms - m_new)
        a2 = jnp.exp(ml - m_new)
        c_new = a1[..., None, None] * cs + a2[..., None, None] * cl
        n_new = a1[..., None] * ns + a2[..., None] * nl
        return (c_new, n_new, m_new), (cs, ns, ms)

    xs = (jnp.moveaxis(b_last, 1, 0), jnp.moveaxis(m_loc, 1, 0), jnp.moveaxis(c_loc, 1, 0), jnp.moveaxis(n_loc, 1, 0))
    fin, prev = lax.scan(step, (c0, n0, m0), xs)
    c_prev, n_prev, m_prev = (jnp.moveaxis(a, 0, 1) for a in prev)
    lin = b + m_prev[..., None]
    m_i = jnp.maximum(jnp.max(dlog, axis=-1), lin)
    s = jnp.einsum('bnihk,bnjhk->bnhij', qc, kc) * jnp.exp(dlog - m_i[..., None])
    w_int = jnp.exp(lin - m_i)
    num = jnp.einsum('bnhij,bnjhv->bnihv', s, vc) + jnp.einsum('bnihk,bnhvk->bnihv', qc, c_prev) * jnp.swapaxes(w_int, 2, 3)[..., None]
    den = jnp.sum(s, axis=-1) + jnp.einsum('bnihk,bnhk->bnhi', qc, n_prev) * w_int
    denom = jnp.maximum(jnp.abs(den), jnp.exp(-m_i))
    h = num / jnp.swapaxes(denom, 2, 3)[..., None]
    return h.reshape(B, L, H, d), fin


def _mixer(u, p, s0, rope):
    f32 = jnp.float32
    B, L, _ = u.shape
    dt = u.dtype
    rq, rk, rv, rg, mq, mk, mv, mo, mi, mf, bg = _split_cols(u @ p['w_in'])
    ret_s0, c0, n0, m0 = s0
    rq = rq.astype(f32).reshape(B, L, H_RET, DK_RET)
    rk = rk.astype(f32).reshape(B, L, H_RET, DK_RET) * DK_RET ** -0.5
    rv = rv.astype(f32).reshape(B, L, H_RET, DV_RET)
    if rope is not None:
        rq = _apply_rope(rq, rope[0], rope[1])
        rk = _apply_rope(rk, rope[0], rope[1])
    lg = jax.nn.log_sigmoid(p['ret_decay_logit'].astype(f32))
    r_outs, r_fins = [], []
    for d in range(N_DIR):
        o, sf = _retention_dir(_flip(rq, d), _flip(rk, d), _flip(rv, d), lg[d], ret_s0[:, d])
        r_outs.append(_flip(o, d))
        r_fins.append(sf)
    r = _head_norm(r_outs[0] + r_outs[1], p['ret_gn']) * jax.nn.silu(rg.astype(f32))
    ret_branch = r.astype(dt) @ p['w_ret_up']
    qk = jax.nn.silu(_short_conv(jnp.concatenate([mq, mk], axis=-1), p['conv_w'], p['conv_b']))
    mq = qk[..., :M_W].astype(f32).reshape(B, L, H_M, DH_M)
    mk = qk[..., M_W:].astype(f32).reshape(B, L, H_M, DH_M) * DH_M ** -0.5
    mv = mv.astype(f32).reshape(B, L, H_M, DH_M)
    ig = (mi + p['b_igate']).astype(f32).reshape(B, L, N_DIR, H_M)
    lf = jax.nn.log_sigmoid((mf + p['b_fgate']).astype(f32)).reshape(B, L, N_DIR, H_M)
    m_outs, c_fins, n_fins, m_fins = [], [], [], []
    for d in range(N_DIR):
        h, (cf, nf, mfin) = _mlstm_dir(_flip(mq, d), _flip(mk, d), _flip(mv, d), _flip(ig[:, :, d], d),
                                       _flip(lf[:, :, d], d), c0[:, d], n0[:, d], m0[:, d])
        m_outs.append(_flip(h, d))
        c_fins.append(cf)
        n_fins.append(nf)
        m_fins.append(mfin)
    hm = jax.nn.sigmoid(mo.astype(f32)).reshape(B, L, H_M, DH_M) * (m_outs[0] + m_outs[1])
    m_branch = _head_norm(hm, p['m_gn']).astype(dt) @ p['w_m_up']
    gates = jax.nn.sigmoid(bg.astype(f32)).reshape(B, L, N_DIR, D_MODEL)
    merged = gates[:, :, 0] * ret_branch.astype(f32) + gates[:, :, 1] * m_branch.astype(f32)
    fin = (jnp.stack(r_fins, axis=1), jnp.stack(c_fins, axis=1), jnp.stack(n_fins, axis=1), jnp.stack(m_fins, axis=1))
    return merged.astype(dt) @ p['w_out'], fin


def _layer(x, mod, p, s0, rope):
    sh1, sc1, g1, sh2, sc2, g2, sh3, sc3, g3 = jnp.split(mod, 9, axis=-1)
    x = x + 0.5 * g1 * _swiglu(_modulate(_rmsnorm(x, p['norm_ffn1']), sh1, sc1), p['w1_ffn1'], p['w3_ffn1'], p['w2_ffn1'])
    y, fin = _mixer(_modulate(_rmsnorm(x, p['norm_mix']), sh2, sc2), p, s0, rope)
    x = x + g2 * y
    x = x + 0.5 * g3 * _swiglu(_modulate(_rmsnorm(x, p['norm_ffn2']), sh3, sc3), p['w1_ffn2'], p['w3_ffn2'], p['w2_ffn2'])
    return x, fin


def setup_inputs(seed: int = 0) -> dict:
    key = jax.random.key(seed)
    ks = iter(jax.random.split(key, 40))
    f32 = jnp.float32

    def nrm(shape, scale):
        return scale * jax.random.normal(next(ks), shape, f32)

    gam = 1.0 - 2.0 ** (-5.0 - jnp.arange(H_RET, dtype=f32))
    ret_logit0 = jnp.log(gam) - jnp.log1p(-gam)
    fb = jnp.linspace(3.0, 6.0, H_M, dtype=f32)
    fbias0 = jnp.concatenate([fb, fb])
    return {
        'x_prompt': nrm((BATCH, SEQ, D_MODEL), 1.0),
        'x_sample': nrm((DEC_BATCH, DEC_SEQ, D_MODEL), 1.0),
        'c': nrm((DEC_BATCH, D_MODEL), 1.0),
        'state_ret': nrm((DEC_BATCH, DEPTH, N_DIR, H_RET, DK_RET, DV_RET), 0.3),
        'state_mlstm_C': nrm((DEC_BATCH, DEPTH, N_DIR, H_M, DH_M, DH_M), 0.3),
        'state_mlstm_n': nrm((DEC_BATCH, DEPTH, N_DIR, H_M, DH_M), 0.3),
        'state_mlstm_m': nrm((DEC_BATCH, DEPTH, N_DIR, H_M), 0.5),
        'c_ctx': nrm((D_MODEL,), 1.0),
        'w_ada': nrm((DEPTH, D_MODEL, 9 * D_MODEL), D_MODEL ** -0.5),
        'b_ada': nrm((DEPTH, 9 * D_MODEL), 0.02),
        'norm_ffn1': 1.0 + nrm((DEPTH, D_MODEL), 0.02),
        'w1_ffn1': nrm((DEPTH, D_MODEL, D_FF), D_MODEL ** -0.5),
        'w3_ffn1': nrm((DEPTH, D_MODEL, D_FF), D_MODEL ** -0.5),
        'w2_ffn1': nrm((DEPTH, D_FF, D_MODEL), D_FF ** -0.5),
        'norm_mix': 1.0 + nrm((DEPTH, D_MODEL), 0.02),
        'w_in': nrm((DEPTH, D_MODEL, D_IN), D_MODEL ** -0.5),
        'conv_w': nrm((DEPTH, CONV_W, 2 * M_W), CONV_W ** -0.5),
        'conv_b': nrm((DEPTH, 2 * M_W), 0.02),
        'ret_decay_logit': ret_logit0[None, None, :] + nrm((DEPTH, N_DIR, H_RET), 0.01),
        'b_igate': nrm((DEPTH, N_DIR * H_M), 0.1),
        'b_fgate': fbias0[None, :] + nrm((DEPTH, N_DIR * H_M), 0.1),
        'ret_gn': 1.0 + nrm((DEPTH, RET_V), 0.02),
        'm_gn': 1.0 + nrm((DEPTH, M_W), 0.02),
        'w_ret_up': nrm((DEPTH, RET_V, D_MODEL), RET_V ** -0.5),
        'w_m_up': nrm((DEPTH, M_W, D_MODEL), M_W ** -0.5),
        'w_out': nrm((DEPTH, D_MODEL, D_MODEL), D_MODEL ** -0.5),
        'norm_ffn2': 1.0 + nrm((DEPTH, D_MODEL), 0.02),
        'w1_ffn2': nrm((DEPTH, D_MODEL, D_FF), D_MODEL ** -0.5),
        'w3_ffn2': nrm((DEPTH, D_MODEL, D_FF), D_MODEL ** -0.5),
        'w2_ffn2': nrm((DEPTH, D_FF, D_MODEL), D_FF ** -0.5),
        'norm_final': 1.0 + nrm((D_MODEL,), 0.02),
    }


def reference(x_prompt, x_sample, c, state_ret, state_mlstm_C, state_mlstm_n, state_mlstm_m, c_ctx,
              w_ada, b_ada, norm_ffn1, w1_ffn1, w3_ffn1, w2_ffn1, norm_mix, w_in, conv_w, conv_b,
              ret_decay_logit, b_igate, b_fgate, ret_gn, m_gn, w_ret_up, w_m_up, w_out,
              norm_ffn2, w1_ffn2, w3_ffn2, w2_ffn2, norm_final):
    f32 = jnp.float32
    bp = x_prompt.shape[0]
    rope = _axial_rope(x_sample.shape[1])
    zero_state = (jnp.zeros((bp, N_DIR, H_RET, DK_RET, DV_RET), f32),
                  jnp.zeros((bp, N_DIR, H_M, DH_M, DH_M), f32),
                  jnp.zeros((bp, N_DIR, H_M, DH_M), f32),
                  jnp.zeros((bp, N_DIR, H_M), f32))
    xp, xs = x_prompt, x_sample
    fins = []
    for l in range(DEPTH):
        p = {'norm_ffn1': norm_ffn1[l], 'w1_ffn1': w1_ffn1[l], 'w3_ffn1': w3_ffn1[l], 'w2_ffn1': w2_ffn1[l],
             'norm_mix': norm_mix[l], 'w_in': w_in[l], 'conv_w': conv_w[l], 'conv_b': conv_b[l],
             'ret_decay_logit': ret_decay_logit[l], 'b_igate': b_igate[l], 'b_fgate': b_fgate[l],
             'ret_gn': ret_gn[l], 'm_gn': m_gn[l], 'w_ret_up': w_ret_up[l], 'w_m_up': w_m_up[l], 'w_out': w_out[l],
             'norm_ffn2': norm_ffn2[l], 'w1_ffn2': w1_ffn2[l], 'w3_ffn2': w3_ffn2[l], 'w2_ffn2': w2_ffn2[l]}
        mod_ctx = (jax.nn.silu(c_ctx) @ w_ada[l] + b_ada[l])[None, None, :]
        mod_lat = (jax.nn.silu(c) @ w_ada[l] + b_ada[l])[:, None, :]
        xp, fin = _layer(xp, mod_ctx, p, zero_state, None)
        lat_state = (state_ret[:, l].astype(f32), state_mlstm_C[:, l].astype(f32),
                     state_mlstm_n[:, l].astype(f32), state_mlstm_m[:, l].astype(f32))
        xs, _ = _layer(xs, mod_lat, p, lat_state, rope)
        fins.append(fin)
    dt = x_prompt.dtype
    y_prompt = _rmsnorm(xp, norm_final)
    y_sample = _rmsnorm(xs, norm_final)
    new_state_ret = jnp.stack([f[0] for f in fins], axis=1).astype(dt)
    new_state_mlstm_C = jnp.stack([f[1] for f in fins], axis=1).astype(dt)
    new_state_mlstm_n = jnp.stack([f[2] for f in fins], axis=1).astype(dt)
    new_state_mlstm_m = jnp.stack([f[3] for f in fins], axis=1).astype(dt)
    return (y_prompt, y_sample, new_state_ret, new_state_mlstm_C, new_state_mlstm_n, new_state_mlstm_m)
```

```python
import numpy as np
import ml_dtypes
from contextlib import ExitStack
import concourse.bass as bass
import concourse.mybir as mybir
from concourse.bass_utils import run_bass_kernel_spmd

F32 = mybir.dt.float32
BF16 = mybir.dt.bfloat16
AF = mybir.ActivationFunctionType
ALU = mybir.AluOpType
AX = mybir.AxisListType

N_DMA_SEMS = {'sp': 32, 'pool': 8}


class Sched:
    ENG = ('pe', 'dve', 'act', 'pool', 'sp')

    def __init__(self, nc, es):
        self.nc = nc
        self.es = es
        self.semh = {}
        for e in self.ENG:
            self.semh[e] = es.enter_context(nc.semaphore("s_" + e))
        for q, n in N_DMA_SEMS.items():
            for i in range(n):
                self.semh[('d', q, i)] = es.enter_context(nc.semaphore("s_d%s%d" % (q, i)))
        self.prog = {e: [] for e in self.ENG}
        self.cnt = {e: 0 for e in self.ENG}
        self.seen = {e: {} for e in self.ENG}
        self.lastw = {}
        self.reads = {}
        self.pend = {e: ([], []) for e in self.ENG}
        self.ndma = {q: 0 for q in N_DMA_SEMS}
        self.nps = 0

    def sb(self, name, shape, dtype):
        return self.es.enter_context(self.nc.sbuf_tensor("sb_" + name, shape, dtype))

    def ps(self, name):
        self.nps += 1
        return self.es.enter_context(self.nc.psum_tensor(name, [128, 512], F32))

    def _deps(self, eng, rd, wr):
        ev = []
        for k in rd:
            if k in self.lastw:
                ev.append(self.lastw[k])
        for k in wr:
            if k in self.lastw:
                ev.append(self.lastw[k])
            for r in self.reads.get(k, ()):
                ev.append(r)
        return ev

    def _filter(self, eng, evs):
        best = {}
        for (s, v) in evs:
            if s == 'pe' and eng == 'pe':
                continue
            if self.seen[eng].get(s, 0) >= v:
                continue
            if best.get(s, 0) < v:
                best[s] = v
        for s, v in best.items():
            self.seen[eng][s] = v
        return list(best.items())

    def _register(self, ev, rd, wr):
        for k in rd:
            self.reads.setdefault(k, []).append(ev)
        for k in wr:
            self.lastw[k] = ev
            self.reads[k] = []

    def op(self, eng, fn, rd=(), wr=(), inc=True):
        waits = self._filter(eng, self._deps(eng, rd, wr))
        if inc:
            self.cnt[eng] += 1
            ev = (eng, self.cnt[eng])
            prd, pwr = self.pend[eng]
            self._register(ev, list(prd) + list(rd), list(pwr) + list(wr))
            self.pend[eng] = ([], [])
            self.prog[eng].append((waits, fn, (eng, 1)))
            return ev
        else:
            self.pend[eng][0].extend(rd)
            self.pend[eng][1].extend(wr)
            self.prog[eng].append((waits, fn, None))
            return None

    def dma(self, q, out, in_, rd=(), wr=()):
        nq = N_DMA_SEMS[q]
        slot = self.ndma[q] % nq
        rnd = self.ndma[q] // nq
        self.ndma[q] += 1
        evs = self._deps(q, rd, wr)
        if rnd > 0:
            evs.append((('d', q, slot), 16 * rnd))
        waits = self._filter(q, evs)
        ev = (('d', q, slot), 16 * (rnd + 1))
        self._register(ev, rd, wr)
        self.prog[q].append((waits, lambda e: e.dma_start(out=out, in_=in_), (('d', q, slot), 16)))
        return ev

    def finish(self, out_keys):
        evs = [self.lastw[k] for k in out_keys if k in self.lastw]
        waits = self._filter('sp', evs)
        self.prog['sp'].append((waits, None, None))
        nc = self.nc

        def replay(name, eng):
            for waits, fn, inc in self.prog[name]:
                for s, v in waits:
                    eng.wait_ge(self.semh[s], v)
                if fn is None:
                    continue
                ins = fn(eng)
                if inc is not None:
                    ins.then_inc(self.semh[inc[0]], inc[1])

        with nc.Block() as block:
            @block.tensor
            def _(e):
                replay('pe', e)

            @block.vector
            def _(e):
                replay('dve', e)

            @block.scalar
            def _(e):
                replay('act', e)

            @block.gpsimd
            def _(e):
                replay('pool', e)

            @block.sync
            def _(e):
                replay('sp', e)

        for h in self.semh.values():
            nc.gpsimd.sem_clear(h)
        nc.all_engine_barrier()


def _barrier(S):
    evs = [(e, S.cnt[e]) for e in ('pe', 'dve', 'act') if S.cnt[e] > 0]
    for q, nq in N_DMA_SEMS.items():
        n = S.ndma[q]
        for slot in range(min(n, nq)):
            rounds = (n - 1 - slot) // nq + 1
            evs.append((('d', q, slot), 16 * rounds))
    for eng in ('pe', 'dve', 'act', 'sp'):
        waits = S._filter(eng, evs)
        if waits:
            S.prog[eng].append((waits, None, None))


def I(name, **kw):
    return lambda e: getattr(e, name)(**kw)


D = 1024
DFF = 2816
T = 1536
NT = 12
SEQS = [(0, 8, True), (8, 2, False), (10, 2, False)]
EPS = 1e-6
GN_EPS = 1e-5
NEG = -30000.0
SM = {}
_o = 0
for _n, _w in [('cT', 16), ('bada', 72), ('g1', 8), ('g2', 8), ('g3', 8), ('gf', 8), ('conv', 32), ('dlogit', 8),
               ('bi', 8), ('bf', 8), ('retgn', 8), ('mgn', 4), ('m0', 8)]:
    SM[_n] = (_o, _o + _w)
    _o += _w
SM_W = _o
CW = 7 * 128 + 2 * 512


def build_program(dbg=None):
    nc = bass.Bass("TRN2", target_bir_lowering=False)

    def din(name, shape):
        return nc.dram_tensor(name, list(shape), F32, kind="ExternalInput").ap()

    def dout(name, shape):
        return nc.dram_tensor(name, list(shape), F32, kind="ExternalOutput").ap()

    x_d = din("x", [T, D])
    smalls_d = din("smalls", [128, SM_W])
    consts_d = din("consts", [128, CW])
    sret_d = din("sret", [128, 8, 256])
    smc_d = din("smc", [128, 8, 129])
    w_ada = din("w_ada", [D, 9 * D]).rearrange("(kc p) n -> p kc n", p=128)
    w1a = din("w1_ffn1", [D, DFF]).rearrange("(kc p) n -> p kc n", p=128)
    w3a = din("w3_ffn1", [D, DFF]).rearrange("(kc p) n -> p kc n", p=128)
    w2a = din("w2_ffn1", [DFF, D]).rearrange("(kc p) n -> p kc n", p=128)
    w1b = din("w1_ffn2", [D, DFF]).rearrange("(kc p) n -> p kc n", p=128)
    w3b = din("w3_ffn2", [D, DFF]).rearrange("(kc p) n -> p kc n", p=128)
    w2b = din("w2_ffn2", [DFF, D]).rearrange("(kc p) n -> p kc n", p=128)
    w_in = din("w_in", [D, 7184]).rearrange("(kc p) n -> p kc n", p=128)
    w_ru = din("w_ret_up", [1024, D]).rearrange("(kc p) n -> p kc n", p=128)
    w_mu = din("w_m_up", [512, D]).rearrange("(kc p) n -> p kc n", p=128)
    w_o = din("w_out", [D, D]).rearrange("(kc p) n -> p kc n", p=128)
    y_d = dout("y", [T, D])
    osr_d = dout("o_sr", [128, 16, 256])
    osc_d = dout("o_sc", [128, 16, 129])
    osm_d = dout("o_sm", [1, 16])
    dbg_outs = {}

    with ExitStack() as es:
        es.enter_context(nc.allow_low_precision("bf16 matmul operands, fp32 accumulation"))
        S = Sched(nc, es)
        xT = S.sb("xT", [128, 8, T], F32)
        uT = S.sb("uT", [128, 8, T], BF16)
        ring = [S.sb("ring%d" % i, [128, 4096], BF16) for i in range(4)]
        consts = S.sb("consts", [128, CW], F32)
        smalls = S.sb("smalls", [128, SM_W], F32)
        modT = S.sb("modT", [128, 72, 2], F32)
        modA = S.sb("modA", [128, 3, 8, 2], F32)
        modG = S.sb("modG", [128, 3, 8, 2], F32)
        cv = S.sb("cv", [128, 4], F32)
        ones_ms = S.sb("ones_ms", [128, 128], F32)
        ones1 = S.sb("ones1", [128, 128], F32)
        identb = S.sb("identb", [128, 128], BF16)
        P = [S.ps("P%d" % i) for i in range(8)]
        Pb = [p[:].bitcast(BF16) for p in P]
        ident = consts[:, 0:128]
        Um = consts[:, 128:256]
        Lm = consts[:, 256:384]
        negU = consts[:, 384:512]
        negL = consts[:, 512:640]
        iota_row = consts[:, 640:768]
        diffm = consts[:, 768:896]
        cos_t = consts[:, 896:896 + 512].rearrange("p (t f) -> p t f", t=8)
        sin_t = consts[:, 896 + 512:896 + 1024].rearrange("p (t f) -> p t f", t=8)

        def sm(name, a=None, b=None):
            lo, hi = SM[name]
            if a is None:
                return smalls[:, lo:hi]
            return smalls[:, lo + a:lo + b]

        ring_i = [0]

        def ring_slot():
            i = ring_i[0] % 4
            ring_i[0] += 1
            return i

        def dump(name, ap, shape, keys):
            if dbg is None or name not in dbg:
                return
            d = dout("dbg_" + name, shape)
            dbg_outs[name] = d
            S.dma('sp', d, ap, rd=keys, wr=[('dbgo', name)])

        S.dma('sp', smalls[:], smalls_d, wr=['smalls'])
        S.dma('sp', consts[:], consts_d, wr=['consts'])
        S.op('dve', I('memset', ap=cv[:, 0:1], constant=EPS), wr=['cv0'])
        S.op('dve', I('memset', ap=cv[:, 1:2], constant=GN_EPS), wr=['cv1'])
        S.op('dve', I('memset', ap=cv[:, 2:3], constant=1.0), wr=['cv2'])
        S.op('dve', I('memset', ap=cv[:, 3:4], constant=0.0), wr=['cv'])
        S.op('dve', I('memset', ap=ones_ms[:], constant=1.0 / 1024.0), wr=['ones_ms'])
        S.op('dve', I('memset', ap=ones1[:], constant=1.0), wr=['ones1'])
        S.op('dve', I('tensor_copy', out=identb[:], in_=ident), rd=['consts'], wr=['identb'])
        CVK = ['cv0', 'cv1', 'cv2', 'cv']

        ph0 = es.enter_context(ExitStack())
        scT = ph0.enter_context(nc.sbuf_tensor("scT", [128, 8, 2], BF16))
        S.op('act', I('activation', out=scT[:].rearrange("p a b -> p (a b)"), in_=sm('cT'), func=AF.Silu),
             rd=['smalls'], wr=['scT'])
        xin = [ph0.enter_context(nc.sbuf_tensor("xin%d" % i, [128, D], F32)) for i in range(2)]

        adaw = [ph0.enter_context(nc.sbuf_tensor("adaw%d" % i, [128, 4096], BF16)) for i in range(2)]

        def mod_block(blk):
            sl = blk % 2
            wv_ = adaw[sl][:].rearrange("p (k n) -> p k n", k=8)
            S.dma('pool', wv_, w_ada[:, :, blk * 512:(blk + 1) * 512], wr=[('adaw', sl)])
            for q in range(4):
                ch = blk * 4 + q
                for kc in range(8):
                    S.op('pe', I('matmul', out=P[7][:, ch * 2:ch * 2 + 2], lhsT=wv_[:, kc, q * 128:(q + 1) * 128],
                                 rhs=scT[:, kc, :], start=(kc == 0), stop=(kc == 7)),
                         rd=[('adaw', sl), 'scT'], wr=[('P', 7)], inc=(kc == 7 and q == 3))

        junkx = ph0.enter_context(nc.sbuf_tensor("junkx", [128, D], F32))
        ssc = ph0.enter_context(nc.sbuf_tensor("ssc", [128, NT], F32))
        dgx = [ph0.enter_context(nc.sbuf_tensor("dgx%d" % i, [128, 128], F32)) for i in range(2)]

        def x_tile(t):
            xs = xin[t % 2]
            S.dma('sp', xs[:], x_d[t * 128:(t + 1) * 128, :], wr=[('xin', t % 2)])
            S.op('dve', I('tensor_tensor', out=junkx[:], in0=xs[:], in1=xs[:], op=ALU.mult), rd=[('xin', t % 2)], wr=['junkx'])
            S.op('dve', I('tensor_reduce', out=ssc[:, t:t + 1], in_=junkx[:], axis=AX.X, op=ALU.add), rd=['junkx'], wr=[('ssc', t)])
            S.op('dve', I('tensor_scalar', out=dgx[t % 2][:], in0=ident, scalar1=ssc[:, t:t + 1], scalar2=None, op0=ALU.mult),
                 rd=[('ssc', t), 'consts'], wr=[('dgx', t % 2)])
            S.op('pe', I('matmul', out=P[4 + t // 4][:, (t % 4) * 128:(t % 4 + 1) * 128], lhsT=ones_ms[:], rhs=dgx[t % 2][:], start=True, stop=True),
                 rd=[('dgx', t % 2), 'ones_ms'], wr=[('P', 4 + t // 4)])
            for half in range(2):
                for q in range(4):
                    dc = half * 4 + q
                    S.op('pe', I('transpose', out=P[half][:, q * 128:(q + 1) * 128],
                                 in_=xs[:, dc * 128:(dc + 1) * 128], identity=ident),
                         rd=[('xin', t % 2), 'consts'], wr=[('P', half)], inc=(q == 3))
                dst = xT[:, half * 4:(half + 1) * 4, t * 128:(t + 1) * 128]
                src = P[half][:].rearrange("p (a b) -> p a b", a=4)
                wk = [('x', half * 4 + q, t // 4) for q in range(4)]
                if half == 0:
                    S.op('dve', I('tensor_copy', out=dst, in_=src), rd=[('P', 0)], wr=wk)
                else:
                    S.op('act', I('activation', out=dst, in_=src, func=AF.Copy), rd=[('P', 1)], wr=wk)

        def mod_finish(n0, n1):
            c0, c1 = 24 * n0, 24 * n1
            S.op('dve', I('tensor_tensor', out=modT[:, c0:c1, :], in0=P[7][:, 2 * c0:2 * c1].rearrange("p (c s) -> p c s", s=2),
                          in1=sm('bada', c0, c1).unsqueeze(2).to_broadcast([128, c1 - c0, 2]), op=ALU.add),
                 rd=[('P', 7), 'smalls'], wr=['modT'])
            for n in range(n0, n1):
                gname = ['g1', 'g2', 'g3'][n]
                S.op('dve', I('tensor_scalar', out=modA[:, n], in0=modT[:, (3 * n + 1) * 8:(3 * n + 2) * 8, :],
                              scalar1=1.0, scalar2=None, op0=ALU.add), rd=['modT'], wr=['modA'])
                S.op('dve', I('tensor_tensor', out=modA[:, n], in0=modA[:, n],
                              in1=sm(gname).unsqueeze(2).to_broadcast([128, 8, 2]), op=ALU.mult),
                     rd=['modA', 'smalls'], wr=['modA'])
                S.op('dve', I('tensor_scalar', out=modG[:, n], in0=modT[:, (3 * n + 2) * 8:(3 * n + 3) * 8, :],
                              scalar1=(1.0 if n == 1 else 0.5), scalar2=None, op0=ALU.mult), rd=['modT'], wr=['modG'])

        def mod_part(c0, c1):
            S.op('dve', I('tensor_tensor', out=modT[:, c0:c1, :], in0=P[7][:, 2 * c0:2 * c1].rearrange("p (c s) -> p c s", s=2),
                          in1=sm('bada', c0, c1).unsqueeze(2).to_broadcast([128, c1 - c0, 2]), op=ALU.add),
                 rd=[('P', 7), 'smalls'], wr=['modT'])

        for blk in range(4):
            mod_block(blk)
            x_tile(3 * blk)
            x_tile(3 * blk + 1)
            x_tile(3 * blk + 2)
        mod_part(0, 16)
        S.op('dve', I('tensor_scalar', out=modA[:, 0], in0=modT[:, 8:16, :], scalar1=1.0, scalar2=None, op0=ALU.add), rd=['modT'], wr=['modA'])
        S.op('dve', I('tensor_tensor', out=modA[:, 0], in0=modA[:, 0], in1=sm('g1').unsqueeze(2).to_broadcast([128, 8, 2]), op=ALU.mult),
             rd=['modA', 'smalls'], wr=['modA'])

        def mod_gate0():
            mod_block(5)
            mod_part(16, 24)
            S.op('dve', I('tensor_scalar', out=modG[:, 0], in0=modT[:, 16:24, :], scalar1=0.5, scalar2=None, op0=ALU.mult), rd=['modT'], wr=['modG'])

        def mod_ss1():
            mod_block(9)
            mod_part(24, 40)
            S.op('dve', I('tensor_scalar', out=modA[:, 1], in0=modT[:, 32:40, :], scalar1=1.0, scalar2=None, op0=ALU.add), rd=['modT'], wr=['modA'])
            S.op('dve', I('tensor_tensor', out=modA[:, 1], in0=modA[:, 1], in1=sm('g2').unsqueeze(2).to_broadcast([128, 8, 2]), op=ALU.mult),
                 rd=['modA', 'smalls'], wr=['modA'])

        def mod_rest():
            mod_part(40, 72)
            S.op('dve', I('tensor_scalar', out=modG[:, 1], in0=modT[:, 40:48, :], scalar1=1.0, scalar2=None, op0=ALU.mult), rd=['modT'], wr=['modG'])
            S.op('dve', I('tensor_scalar', out=modA[:, 2], in0=modT[:, 56:64, :], scalar1=1.0, scalar2=None, op0=ALU.add), rd=['modT'], wr=['modA'])
            S.op('dve', I('tensor_tensor', out=modA[:, 2], in0=modA[:, 2], in1=sm('g3').unsqueeze(2).to_broadcast([128, 8, 2]), op=ALU.mult),
                 rd=['modA', 'smalls'], wr=['modA'])
            S.op('dve', I('tensor_scalar', out=modG[:, 2], in0=modT[:, 64:72, :], scalar1=0.5, scalar2=None, op0=ALU.mult), rd=['modT'], wr=['modG'])

        mod_extra = ([(lambda: mod_block(4)), mod_gate0] + [(lambda bb=bb: mod_block(bb)) for bb in range(6, 9)] + [mod_ss1]
                     + [(lambda bb=bb: mod_block(bb)) for bb in range(10, 18)])
        MODK = ['modT', 'modA', 'modG']

        def norm_phase(ph, n, final_cb=None, banks=(5, 6, 7), pre=False):
            if final_cb is None:
                SGS = [(0, 1024, 0), (1024, 512, 1)]
            else:
                SGS = [(0, 512, 0), (512, 512, 0), (1024, 512, 1)]
            sq = [ph.enter_context(nc.sbuf_tensor("sq%d_%d" % (n, i), [128, 1024], F32)) for i in range(2)] if not pre else None
            rstd = ph.enter_context(nc.sbuf_tensor("rstd%d" % n, [128, T], F32))
            tm = [ph.enter_context(nc.sbuf_tensor("tm%d_%d" % (n, i), [128, 1024], F32)) for i in range(2)]
            kq = [0]

            def stats(o, w):
                nch = w // 512
                g0 = o // 512
                for dc in range(8 if not pre else 0):
                    s = kq[0] % 2
                    kq[0] += 1
                    S.op('act', I('activation', out=sq[s][:, 0:w], in_=xT[:, dc, o:o + w], func=AF.Square),
                         rd=[('x', dc, g0 + c) for c in range(nch)], wr=[('sq', s)])
                    for c in range(nch):
                        pb = banks[(g0 + c) % 3]
                        S.op('pe', I('matmul', out=P[pb][:], lhsT=ones_ms[:], rhs=sq[s][:, c * 512:(c + 1) * 512], start=(dc == 0), stop=(dc == 7)),
                             rd=[('sq', s), 'ones_ms'], wr=[('P', pb)], inc=True)
                for c in range(nch):
                    pb = banks[(g0 + c) % 3]
                    S.op('act', I('activation', out=rstd[:, o + c * 512:o + (c + 1) * 512], in_=P[pb][:], func=AF.Ln, bias=cv[:, 0:1], scale=1.0),
                         rd=[('P', pb)] + CVK, wr=[('rstd', g0 + c)])
                S.op('act', I('activation', out=rstd[:, o:o + w], in_=rstd[:, o:o + w], func=AF.Exp, scale=-0.5), rd=[('rstd', g0 + c) for c in range(nch)],
                     wr=[('rstd', g0 + c) for c in range(nch)])

            def apply(o, w, st):
                nch = w // 512
                g0 = o // 512
                for dc in range(8):
                    s = kq[0] % 2
                    kq[0] += 1
                    S.op('dve', I('tensor_tensor', out=tm[s][:, 0:w], in0=xT[:, dc, o:o + w], in1=rstd[:, o:o + w], op=ALU.mult),
                         rd=[('x', dc, g0 + c) for c in range(nch)] + [('rstd', g0 + c) for c in range(nch)], wr=[('tm', s)])
                    if final_cb is None:
                        S.op('act', I('activation', out=uT[:, dc, o:o + w], in_=tm[s][:, 0:w], func=AF.Identity,
                                      bias=modT[:, 3 * n * 8 + dc, st:st + 1], scale=modA[:, n, dc, st:st + 1]),
                             rd=[('tm', s)] + MODK, wr=[('u', dc, g0 + c) for c in range(nch)])
                    else:
                        final_cb(g0, dc, tm[s][:, 0:512], ('tm', s))

            stats(SGS[0][0], SGS[0][1])
            for i_, (o, w, st) in enumerate(SGS):
                if i_ + 1 < len(SGS):
                    stats(SGS[i_ + 1][0], SGS[i_ + 1][1])
                apply(o, w, st)
            return rstd, tm

        stat_k = [0]

        def emit_stat(sqb, dc, g, banks, acc, after=None):
            gs = slice(g * 512, (g + 1) * 512)
            if dc == 0:
                S.op('act', I('activation', out=acc[g][:], in_=xT[:, dc, gs], func=AF.Square), rd=[('x', dc, g)], wr=[('acc', g)])
            else:
                s = stat_k[0] % 2
                stat_k[0] += 1
                S.op('act', I('activation', out=sqb[s][:], in_=xT[:, dc, gs], func=AF.Square), rd=[('x', dc, g)], wr=[('sqb', s)])
                S.op('dve', I('tensor_tensor', out=acc[g][:], in0=acc[g][:], in1=sqb[s][:], op=ALU.add), rd=[('acc', g), ('sqb', s)], wr=[('acc', g)])
            if dc == 7:
                S.op('pe', I('matmul', out=P[banks[g]][:], lhsT=ones_ms[:], rhs=acc[g][:], start=True, stop=True),
                     rd=[('acc', g), 'ones_ms'], wr=[('P', banks[g])], inc=True)
                if after is not None:
                    after(g)

        napk = [0]

        def norm_apply_group(n, g, bank, rstd, tm):
            gs = slice(g * 512, (g + 1) * 512)
            st = 0 if g < 2 else 1
            S.op('act', I('activation', out=rstd[:, gs], in_=P[bank][:], func=AF.Ln, bias=cv[:, 0:1], scale=1.0),
                 rd=[('P', bank)] + CVK, wr=[('rstd', g)])
            S.op('act', I('activation', out=rstd[:, gs], in_=rstd[:, gs], func=AF.Exp, scale=-0.5), rd=[('rstd', g)], wr=[('rstd', g)])
            for dc in range(8):
                s = napk[0] % 2
                napk[0] += 1
                S.op('dve', I('tensor_tensor', out=tm[s][:, 0:512], in0=xT[:, dc, gs], in1=rstd[:, gs], op=ALU.mult),
                     rd=[('x', dc, g), ('rstd', g)], wr=[('tm', s)])
                S.op('act', I('activation', out=uT[:, dc, gs], in_=tm[s][:, 0:512], func=AF.Identity,
                              bias=modT[:, 3 * n * 8 + dc, st:st + 1], scale=modA[:, n, dc, st:st + 1]),
                     rd=[('tm', s)] + MODK, wr=[('u', dc, g)])

        def ffn_phase(ph, n, w1, w3, w2, extra=(), stat_banks=None, next_norm=None, nbufs=None):
            extra = list(extra)
            after = None
            if next_norm is not None:
                rstdF, tmF = nbufs
                after = lambda g: norm_apply_group(next_norm, g, stat_banks[g], rstdF, tmF)
            sqb = [ph.enter_context(nc.sbuf_tensor("sqb%d_%d" % (n, i), [128, 512], F32)) for i in range(2)]
            pstat = []
            accb = [ph.enter_context(nc.sbuf_tensor("accb%d_%d" % (n, i), [128, 512], F32)) for i in range(3)] if stat_banks is not None else None
            hT = ph.enter_context(nc.sbuf_tensor("hT%d" % n, [128, 11, T], BF16))
            sa = [ph.enter_context(nc.sbuf_tensor("sa%d_%d" % (n, i), [128, 512], F32)) for i in range(2)]
            k = 0
            for half in range(2):
                for (off, wdt) in [(0, 512), (512, 512), (1024, 384)]:
                    c0 = half * 1408 + off
                    s1 = ring_slot()
                    v1 = ring[s1][:].rearrange("p (k n) -> p k n", k=8)
                    S.dma('pool', v1[:, :, 0:wdt], w1[:, :, c0:c0 + wdt], wr=[('ring', s1)])
                    s3 = ring_slot()
                    v3 = ring[s3][:].rearrange("p (k n) -> p k n", k=8)
                    S.dma('pool', v3[:, :, 0:wdt], w3[:, :, c0:c0 + wdt], wr=[('ring', s3)])
                    pending_extra = extra.pop(0) if extra else None
                    for q in range(wdt // 128):
                        fl = (off + q * 128) // 128
                        for g in range(3):
                            gs = slice(g * 512, (g + 1) * 512)
                            pi = 2 * (k % 2)
                            s = k % 2
                            k += 1
                            for kc in range(8):
                                S.op('pe', I('matmul', out=P[pi][:], lhsT=v1[:, kc, q * 128:(q + 1) * 128], rhs=uT[:, kc, gs],
                                             start=(kc == 0), stop=(kc == 7)),
                                     rd=[('ring', s1), ('u', kc, g)], wr=[('P', pi)], inc=(kc == 7))
                            for kc in range(8):
                                S.op('pe', I('matmul', out=P[pi + 1][:], lhsT=v3[:, kc, q * 128:(q + 1) * 128], rhs=uT[:, kc, gs],
                                             start=(kc == 0), stop=(kc == 7)),
                                     rd=[('ring', s3), ('u', kc, g)], wr=[('P', pi + 1)], inc=(kc == 7))
                            S.op('act', I('activation', out=sa[s][:], in_=P[pi][:], func=AF.Silu), rd=[('P', pi)], wr=[('sa', s)])
                            S.op('dve', I('tensor_tensor', out=hT[:, fl, gs], in0=sa[s][:], in1=P[pi + 1][:], op=ALU.mult),
                                 rd=[('sa', s), ('P', pi + 1)], wr=[('h', fl, g)])
                    if pending_extra is not None:
                        pending_extra()
                for cb in range(4):
                    sl = ring_slot()
                    v2 = ring[sl][:, 0:11 * 256].rearrange("p (k n) -> p k n", k=11)
                    S.dma('pool', v2, w2[:, half * 11:(half + 1) * 11, cb * 256:(cb + 1) * 256], wr=[('ring', sl)])
                    pend2 = extra.pop(0) if extra else None
                    for q in range(2):
                        dc = cb * 2 + q
                        for g in range(3):
                            gs = slice(g * 512, (g + 1) * 512)
                            st = 0 if g < 2 else 1
                            pi = 4 + (k % 2)
                            k += 1
                            for fl in range(11):
                                S.op('pe', I('matmul', out=P[pi][:], lhsT=v2[:, fl, q * 128:(q + 1) * 128], rhs=hT[:, fl, gs],
                                             start=(fl == 0), stop=(fl == 10)),
                                     rd=[('ring', sl), ('h', fl, g)], wr=[('P', pi)], inc=(fl == 10))
                            S.op('dve', I('scalar_tensor_tensor', out=xT[:, dc, gs], in0=P[pi][:], scalar=modG[:, n, dc, st:st + 1],
                                          in1=xT[:, dc, gs], op0=ALU.mult, op1=ALU.add),
                                 rd=[('P', pi), ('x', dc, g)] + MODK, wr=[('x', dc, g)])
                            if half == 1 and stat_banks is not None:
                                pstat.append((dc, g))
                                if len(pstat) > 3:
                                    emit_stat(sqb, *pstat.pop(0), stat_banks, accb, after)
                    if pend2 is not None:
                        pend2()
            while extra:
                extra.pop(0)()
            while pstat:
                emit_stat(sqb, *pstat.pop(0), stat_banks, accb, after)

        with ExitStack() as ph:
            nb0 = norm_phase(ph, 0, banks=(4, 5, 6), pre=True)
            ffn_phase(ph, 0, w1a, w3a, w2a, extra=mod_extra, stat_banks=(0, 1, 2), next_norm=1, nbufs=nb0)
            mod_rest()
            _barrier(S)
        ph0.close()

        SCK = 128.0 ** -0.5
        mxs = es.enter_context(ExitStack())
        rT = mxs.enter_context(nc.sbuf_tensor("rT", [128, 8, T], BF16))
        hmT = mxs.enter_context(nc.sbuf_tensor("hmT", [128, 4, T], BF16))

        tctr = [0]

        def alt():
            tctr[0] += 1
            return 'dve' if tctr[0] % 2 == 0 else 'act'

        def copy_op(eng, out, in_, rd, wr):
            if eng == 'dve':
                S.op('dve', I('tensor_copy', out=out, in_=in_), rd=rd, wr=wr)
            else:
                S.op('act', I('activation', out=out, in_=in_, func=AF.Copy), rd=rd, wr=wr)

        def group_norm_stats(st, src_ap, src_keys, width, junk, tag):
            inv = 1.0 / width
            S.op('dve', I('tensor_reduce', out=st[:, 0:1], in_=src_ap, axis=AX.X, op=ALU.add), rd=src_keys, wr=[(tag, 0)])
            S.op('act', I('activation', out=junk, in_=src_ap, func=AF.Square), rd=src_keys, wr=[(tag, 'junk')])
            S.op('dve', I('tensor_reduce', out=st[:, 1:2], in_=junk, axis=AX.X, op=ALU.add), rd=[(tag, 'junk')], wr=[(tag, 1)])
            S.op('dve', I('tensor_scalar', out=st[:, 2:3], in0=st[:, 0:1], scalar1=inv, scalar2=None, op0=ALU.mult),
                 rd=[(tag, 0)], wr=[(tag, 2)])
            S.op('dve', I('tensor_tensor', out=st[:, 3:4], in0=st[:, 2:3], in1=st[:, 2:3], op=ALU.mult), rd=[(tag, 2)], wr=[(tag, 3)])
            S.op('dve', I('scalar_tensor_tensor', out=st[:, 4:5], in0=st[:, 1:2], scalar=inv, in1=st[:, 3:4],
                          op0=ALU.mult, op1=ALU.subtract), rd=[(tag, 1), (tag, 3)], wr=[(tag, 4)])
            S.op('act', I('activation', out=st[:, 5:6], in_=st[:, 4:5], func=AF.Sqrt, bias=cv[:, 1:2], scale=1.0),
                 rd=[(tag, 4)] + CVK, wr=[(tag, 5)])
            S.op('dve', I('reciprocal', out=st[:, 5:6], in_=st[:, 5:6]), rd=[(tag, 5)], wr=[(tag, 5)])
            S.op('dve', I('scalar_tensor_tensor', out=st[:, 6:7], in0=st[:, 2:3], scalar=-1.0, in1=st[:, 5:6],
                          op0=ALU.mult, op1=ALU.mult), rd=[(tag, 2), (tag, 5)], wr=[(tag, 6)])

        with ExitStack() as ph:
            def sbt(name, shape, dt):
                return ph.enter_context(nc.sbuf_tensor(name, shape, dt))
            lg = sbt("r_lg", [128, 8], F32)
            nlg = sbt("r_nlg", [128, 8], F32)
            lg127 = sbt("r_lg127", [128, 8], F32)
            lg128 = sbt("r_lg128", [128, 8], F32)
            gch = sbt("r_gch", [128, 8], F32)
            wkt = sbt("r_wkt", [128, 8], F32)
            Mh = sbt("r_Mh", [128, 4, 128], F32)
            Wq = sbt("r_Wq", [128, 8, 128], F32)
            ta = sbt("r_ta", [128, 128], F32)
            tb = sbt("r_tb", [128, 128], F32)
            qktok = sbt("r_qktok", [128, NT, 2, 128], BF16)
            qkT = sbt("r_qkT", [128, 2, T], BF16)
            rv = sbt("r_rv", [128, NT, 256], BF16)
            rgs = sbt("r_rgs", [128, NT, 256], BF16)
            Sst = sbt("r_Sst", [128, 2, 2, 256], F32)
            Sbst = sbt("r_Sbst", [128, 8, 256], BF16)
            Sfb = [sbt("r_Sfb%d" % i, [128, 256], BF16) for i in range(2)]
            kw = [sbt("r_kw%d" % i, [128, 128], BF16) for i in range(2)]
            qf = [sbt("r_qf%d" % i, [128, 128], BF16) for i in range(2)]
            qb = [sbt("r_qb%d" % i, [128, 128], BF16) for i in range(2)]
            sTm = [sbt("r_sTm%d" % i, [128, 128], BF16) for i in range(2)]
            rt = [sbt("r_rt%d" % i, [128, 2, 64], F32) for i in range(4)]
            oall = sbt("r_oall", [128, 8, 256], F32)
            junk = sbt("r_junk", [128, 2, 256], F32)
            st = sbt("r_st", [128, 8, 8], F32)
            DK = ['lg', 'nlg', 'lg127', 'lg128', 'consts']

            def decay_tables():
                S.op('act', I('activation', out=lg[:], in_=sm('dlogit'), func=AF.Exp, scale=-1.0), rd=['smalls'], wr=['lg'])
                S.op('act', I('activation', out=lg[:], in_=lg[:], func=AF.Ln, bias=cv[:, 2:3], scale=1.0), rd=['lg'] + CVK, wr=['lg'])
                S.op('dve', I('tensor_scalar', out=nlg[:], in0=lg[:], scalar1=1.0, scalar2=None, op0=ALU.mult), rd=['lg'], wr=['nlg'])
                S.op('dve', I('tensor_scalar', out=lg[:], in0=nlg[:], scalar1=-1.0, scalar2=None, op0=ALU.mult), rd=['nlg'], wr=['lg'])
                S.op('dve', I('tensor_scalar', out=lg127[:], in0=lg[:], scalar1=127.0, scalar2=None, op0=ALU.mult), rd=['lg'], wr=['lg127'])
                S.op('dve', I('tensor_scalar', out=lg128[:], in0=lg[:], scalar1=128.0, scalar2=None, op0=ALU.mult), rd=['lg'], wr=['lg128'])
                S.op('act', I('activation', out=gch[:], in_=lg128[:], func=AF.Exp), rd=['lg128'], wr=['dec_g'])
                DK = ['lg', 'nlg', 'lg127', 'lg128', 'consts']
                for h in range(4):
                    S.op('dve', I('tensor_scalar', out=ta[:], in0=diffm, scalar1=lg[:, h:h + 1], scalar2=0.0, op0=ALU.mult, op1=ALU.min),
                         rd=DK, wr=['ta'])
                    S.op('act', I('activation', out=ta[:], in_=ta[:], func=AF.Exp), rd=['ta'], wr=['ta'])
                    S.op('dve', I('tensor_tensor', out=ta[:], in0=ta[:], in1=Um, op=ALU.mult), rd=['ta', 'consts'], wr=['ta'])
                    S.op('dve', I('tensor_scalar', out=tb[:], in0=diffm, scalar1=nlg[:, 4 + h:5 + h], scalar2=0.0, op0=ALU.mult, op1=ALU.min),
                         rd=DK, wr=['tb'])
                    S.op('act', I('activation', out=tb[:], in_=tb[:], func=AF.Exp), rd=['tb'], wr=['tb'])
                    S.op('dve', I('tensor_tensor', out=tb[:], in0=tb[:], in1=Lm, op=ALU.mult), rd=['tb', 'consts'], wr=['tb'])
                    S.op('dve', I('tensor_tensor', out=ta[:], in0=ta[:], in1=tb[:], op=ALU.add), rd=['ta', 'tb'], wr=['ta'])
                    S.op('dve', I('tensor_scalar', out=Mh[:, h, :], in0=ta[:], scalar1=SCK, scalar2=None, op0=ALU.mult), rd=['ta'], wr=['dec_M'])
                    S.op('act', I('activation', out=Wq[:, h, :], in_=iota_row, func=AF.Exp, bias=lg[:, h:h + 1], scale=lg[:, h:h + 1]),
                         rd=DK, wr=['dec_Wq'])
                    S.op('act', I('activation', out=Wq[:, 4 + h, :], in_=iota_row, func=AF.Exp, bias=lg128[:, 4 + h:5 + h], scale=nlg[:, 4 + h:5 + h]),
                         rd=DK, wr=['dec_Wq'])
                    S.op('act', I('activation', out=wkt[:, h:h + 1], in_=diffm[:, 0:1], func=AF.Exp, bias=lg127[:, h:h + 1], scale=lg[:, h:h + 1]),
                         rd=DK, wr=['dec_wk'])
                    S.op('act', I('activation', out=wkt[:, 4 + h:5 + h], in_=diffm[:, 0:1], func=AF.Exp, scale=nlg[:, 4 + h:5 + h]),
                         rd=DK, wr=['dec_wk'])
                S.op('dve', I('tensor_scalar', out=wkt[:], in0=wkt[:], scalar1=SCK, scalar2=None, op0=ALU.mult), rd=['dec_wk'], wr=['dec_wk'])

            DEC = ['dec_g', 'dec_M', 'dec_Wq', 'dec_wk']

            kk = [0]

            def proj_r(h):
                slA = ring_slot()
                vA = ring[slA][:, 0:2048].rearrange("p (k n) -> p k n", k=8)
                S.dma('pool', vA[:, :, 0:128], w_in[:, :, h * 128:(h + 1) * 128], wr=[('ring', slA)])
                S.dma('pool', vA[:, :, 128:256], w_in[:, :, 512 + h * 128:512 + (h + 1) * 128], wr=[('ring', slA)])
                slB = ring_slot()
                vB = ring[slB][:].rearrange("p (k n) -> p k n", k=8)
                S.dma('pool', vB[:, :, 0:256], w_in[:, :, 1024 + h * 256:1024 + (h + 1) * 256], wr=[('ring', slB)])
                S.dma('pool', vB[:, :, 256:512], w_in[:, :, 2048 + h * 256:2048 + (h + 1) * 256], wr=[('ring', slB)])
                for t in range(NT):
                    ts_ = slice(t * 128, (t + 1) * 128)
                    pq = t % 2
                    pv = 2 if t % 2 == 0 else 6
                    for kc in range(8):
                        S.op('pe', I('matmul', out=P[pq][:, 0:256], lhsT=uT[:, kc, ts_], rhs=vA[:, kc, :], start=(kc == 0), stop=(kc == 7)),
                             rd=[('ring', slA), ('u', kc, t // 4)], wr=[('P', pq)], inc=(kc == 7))
                    for kc in range(8):
                        S.op('pe', I('matmul', out=P[pv][:], lhsT=uT[:, kc, ts_], rhs=vB[:, kc, :], start=(kc == 0), stop=(kc == 7)),
                             rd=[('ring', slB), ('u', kc, t // 4)], wr=[('P', pv)], inc=(kc == 7))
                    if t < 8:
                        X = P[pq][:, 0:256].rearrange("p (a b) -> p a b", a=2)
                        x1 = X[:, :, 0:64]
                        x2 = X[:, :, 64:128]
                        cb_ = cos_t[:, t, :].unsqueeze(1).to_broadcast([128, 2, 64])
                        sb_ = sin_t[:, t, :].unsqueeze(1).to_broadcast([128, 2, 64])
                        S.op('dve', I('tensor_tensor', out=rt[0][:], in0=x1, in1=cb_, op=ALU.mult), rd=[('P', pq), 'consts'], wr=[('rt', 0)])
                        S.op('dve', I('tensor_tensor', out=rt[1][:], in0=x2, in1=sb_, op=ALU.mult), rd=[('P', pq), 'consts'], wr=[('rt', 1)])
                        S.op('dve', I('tensor_tensor', out=rt[2][:], in0=x2, in1=cb_, op=ALU.mult), rd=[('P', pq), 'consts'], wr=[('rt', 2)])
                        S.op('dve', I('tensor_tensor', out=rt[3][:], in0=x1, in1=sb_, op=ALU.mult), rd=[('P', pq), 'consts'], wr=[('rt', 3)])
                        S.op('dve', I('tensor_tensor', out=qktok[:, t, :, 0:64], in0=rt[0][:], in1=rt[1][:], op=ALU.subtract),
                             rd=[('rt', 0), ('rt', 1)], wr=[('qktok', t, 0)])
                        S.op('dve', I('tensor_tensor', out=qktok[:, t, :, 64:128], in0=rt[2][:], in1=rt[3][:], op=ALU.add),
                             rd=[('rt', 2), ('rt', 3)], wr=[('qktok', t, 1)])
                    else:
                        S.op('act', I('activation', out=qktok[:, t].rearrange("p a b -> p (a b)"), in_=P[pq][:, 0:256], func=AF.Copy),
                             rd=[('P', pq)], wr=[('qktok', t, 0), ('qktok', t, 1)])
                    S.op('act', I('activation', out=rv[:, t, :], in_=P[pv][:, 0:256], func=AF.Copy), rd=[('P', pv)], wr=[('rv', t)])
                    S.op('act', I('activation', out=rgs[:, t, :], in_=P[pv][:, 256:512], func=AF.Silu), rd=[('P', pv)], wr=[('rgs', t)])
                    pt = 3 if t % 2 == 0 else 7
                    for c in range(2):
                        S.op('pe', I('transpose', out=Pb[pt][:, c * 128:(c + 1) * 128], in_=qktok[:, t, c, :], identity=identb[:]),
                             rd=[('qktok', t, 0), ('qktok', t, 1), 'identb'], wr=[('P', pt)], inc=(c == 1))
                    copy_op('dve' if t % 2 == 0 else 'act', qkT[:, :, ts_], Pb[pt][:, 0:256].rearrange("p (a b) -> p a b", a=2), [('P', pt)], [('qkT', t)])

            def rest_r(h):
                for si, (t0, N, samp) in enumerate(SEQS):
                    p_ = si - 1
                    ebase = 0 if samp else 8
                    sver = [0, 0]
                    if samp:
                        S.dma('sp', Sst[:, 0, 0, :], sret_d[:, h, :], wr=[('S', 0, 0)])
                        S.dma('sp', Sst[:, 1, 0, :], sret_d[:, 4 + h, :], wr=[('S', 1, 0)])
                    else:
                        S.op('dve', I('memset', ap=Sst[:, 0, 0, :], constant=0.0), wr=[('S', 0, 0)])
                        S.op('dve', I('memset', ap=Sst[:, 1, 0, :], constant=0.0), wr=[('S', 1, 0)])

                    def kv_mm(t, d):
                        s = kk[0] % 2
                        kk[0] += 1
                        di = d * 4 + h
                        S.op('dve', I('tensor_scalar', out=kw[s][:], in0=qktok[:, t, 1, :], scalar1=wkt[:, di:di + 1], scalar2=None, op0=ALU.mult),
                             rd=[('qktok', t, 0), ('qktok', t, 1)] + DEC, wr=[('kw', s)])
                        S.op('pe', I('matmul', out=P[5 + s][:, 0:256], lhsT=kw[s][:], rhs=rv[:, t, :], start=True, stop=True),
                             rd=[('kw', s), ('rv', t)], wr=[('P', 5 + s)])
                        return s

                    def s_update(d, s):
                        di = d * 4 + h
                        cu = sver[d]
                        S.op('dve', I('scalar_tensor_tensor', out=Sst[:, d, 1 - cu, :], in0=Sst[:, d, cu, :], scalar=gch[:, di:di + 1], in1=P[5 + s][:, 0:256],
                                      op0=ALU.mult, op1=ALU.add), rd=[('S', d, cu), ('P', 5 + s)] + DEC, wr=[('S', d, 1 - cu)])
                        sver[d] = 1 - cu

                    order = list(reversed(range(N)))
                    pend = kv_mm(t0 + order[0], 1)
                    for oi, n in enumerate(order):
                        cur = pend
                        if oi + 1 < N:
                            pend = kv_mm(t0 + order[oi + 1], 1)
                        S.op('act', I('activation', out=Sbst[:, n, :], in_=Sst[:, 1, sver[1], :], func=AF.Copy), rd=[('S', 1, sver[1])], wr=[('Sbst', n)])
                        s_update(1, cur)
                    if not samp:
                        S.dma('sp', osr_d[:, p_ * 8 + 4 + h, :], Sst[:, 1, sver[1], :], rd=[('S', 1, sver[1])], wr=[('osr', p_, 1, h)])

                    def indep(n):
                        t = t0 + n
                        ts_ = slice(t * 128, (t + 1) * 128)
                        s = n % 2
                        S.op('dve', I('tensor_tensor', out=qf[s][:], in0=qkT[:, 0, ts_], in1=Wq[:, h, :], op=ALU.mult),
                             rd=[('qkT', t)] + DEC, wr=[('qf', s)])
                        S.op('dve', I('tensor_tensor', out=qb[s][:], in0=qkT[:, 0, ts_], in1=Wq[:, 4 + h, :], op=ALU.mult),
                             rd=[('qkT', t)] + DEC, wr=[('qb', s)])
                        S.op('pe', I('matmul', out=P[3 + s][:, 0:128], lhsT=qkT[:, 1, ts_], rhs=qkT[:, 0, ts_], start=True, stop=True),
                             rd=[('qkT', t)], wr=[('P', 3 + s)])
                        S.op('dve', I('tensor_tensor', out=sTm[s][:], in0=P[3 + s][:, 0:128], in1=Mh[:, h, :], op=ALU.mult),
                             rd=[('P', 3 + s)] + DEC, wr=[('sTm', s)])
                        return kv_mm(t, 0)

                    pend = indep(0)
                    for n in range(N):
                        t = t0 + n
                        s = n % 2
                        cur = pend
                        if n + 1 < N:
                            pend = indep(n + 1)
                        S.op('act', I('activation', out=Sfb[s][:], in_=Sst[:, 0, sver[0], :], func=AF.Copy), rd=[('S', 0, sver[0])], wr=[('Sfb', s)])
                        po = 0 if s == 0 else 7
                        pk = ('P', 0) if s == 0 else ('P', 7)
                        S.op('pe', I('matmul', out=P[po][:, 0:256], lhsT=qf[s][:], rhs=Sfb[s][:], start=True, stop=False),
                             rd=[('qf', s), ('Sfb', s)], wr=[pk], inc=False)
                        S.op('pe', I('matmul', out=P[po][:, 0:256], lhsT=qb[s][:], rhs=Sbst[:, n, :], start=False, stop=False),
                             rd=[('qb', s), ('Sbst', n)], wr=[pk], inc=False)
                        S.op('pe', I('matmul', out=P[po][:, 0:256], lhsT=sTm[s][:], rhs=rv[:, t, :], start=False, stop=True),
                             rd=[('sTm', s), ('rv', t)], wr=[pk], inc=True)
                        S.op('act', I('activation', out=oall[:, t - ebase, :], in_=P[po][:, 0:256], func=AF.Copy), rd=[pk], wr=[('oall', t - ebase)])
                        s_update(0, cur)
                    if not samp:
                        S.dma('sp', osr_d[:, p_ * 8 + h, :], Sst[:, 0, sver[0], :], rd=[('S', 0, sver[0])], wr=[('osr', p_, 0, h)])

                    if si == 1:
                        continue
                    ea, eb_ = (0, 8) if samp else (8, 12)
                    ne = eb_ - ea
                    OK_ = [('oall', j) for j in range(ne)]
                    QK_ = [('qktok', t, c) for t in range(ea, eb_) for c in range(2)]
                    inv = 1.0 / 256.0
                    S.op('dve', I('tensor_reduce', out=st[:, 0, 0:ne], in_=oall[:, 0:ne, :], axis=AX.X, op=ALU.add), rd=OK_, wr=[('rst', 0)])
                    for j in range(0, ne, 2):
                        S.op('act', I('activation', out=junk[:].rearrange("p a b -> p (a b)"), in_=oall[:, j:j + 2, :].rearrange("p a b -> p (a b)"), func=AF.Square),
                             rd=OK_, wr=['rjunk'])
                        S.op('dve', I('tensor_reduce', out=st[:, 1, j:j + 2], in_=junk[:], axis=AX.X, op=ALU.add), rd=['rjunk'], wr=[('rst', 1)])
                    S.op('dve', I('tensor_scalar', out=st[:, 2, 0:ne], in0=st[:, 0, 0:ne], scalar1=inv, scalar2=None, op0=ALU.mult), rd=[('rst', 0)], wr=[('rst', 2)])
                    S.op('dve', I('tensor_tensor', out=st[:, 3, 0:ne], in0=st[:, 2, 0:ne], in1=st[:, 2, 0:ne], op=ALU.mult), rd=[('rst', 2)], wr=[('rst', 3)])
                    S.op('dve', I('scalar_tensor_tensor', out=st[:, 4, 0:ne], in0=st[:, 1, 0:ne], scalar=inv, in1=st[:, 3, 0:ne], op0=ALU.mult, op1=ALU.subtract),
                         rd=[('rst', 1), ('rst', 3)], wr=[('rst', 4)])
                    S.op('act', I('activation', out=st[:, 5, 0:ne], in_=st[:, 4, 0:ne], func=AF.Sqrt, bias=cv[:, 1:2], scale=1.0), rd=[('rst', 4)] + CVK, wr=[('rst', 5)])
                    S.op('dve', I('reciprocal', out=st[:, 5, 0:ne], in_=st[:, 5, 0:ne]), rd=[('rst', 5)], wr=[('rst', 5)])
                    S.op('dve', I('scalar_tensor_tensor', out=st[:, 6, 0:ne], in0=st[:, 2, 0:ne], scalar=-1.0, in1=st[:, 5, 0:ne], op0=ALU.mult, op1=ALU.mult),
                         rd=[('rst', 2), ('rst', 5)], wr=[('rst', 6)])
                    for j in range(ne):
                        S.op('act', I('activation', out=oall[:, j, :], in_=oall[:, j, :], func=AF.Identity, bias=st[:, 6, j:j + 1], scale=st[:, 5, j:j + 1]),
                             rd=[('oall', j), ('rst', 5), ('rst', 6)], wr=[('oall', j)])
                    rbfv = qktok[:, ea:eb_].rearrange("p t c f -> p t (c f)")
                    S.op('dve', I('tensor_tensor', out=rbfv, in0=oall[:, 0:ne, :], in1=rgs[:, ea:eb_, :], op=ALU.mult),
                         rd=OK_ + [('rgs', t) for t in range(ea, eb_)] + QK_, wr=QK_)
                    for c in range(2):
                        pe_ = 1 + c
                        for t in range(ea, eb_):
                            S.op('pe', I('transpose', out=Pb[pe_][:, (t - ea) * 128:(t - ea + 1) * 128], in_=qktok[:, t, c, :], identity=identb[:]),
                                 rd=[('qktok', t, 0), ('qktok', t, 1), 'identb'], wr=[('P', pe_)], inc=(t == eb_ - 1))
                        S.op('act', I('activation', out=rT[:, 2 * h + c, ea * 128:eb_ * 128], in_=Pb[pe_][:, 0:ne * 128], func=AF.Identity,
                                      scale=sm('retgn', 2 * h + c, 2 * h + c + 1)),
                             rd=[('P', pe_), 'smalls'], wr=[('rT', 2 * h + c, 0), ('rT', 2 * h + c, 1), ('rT', 2 * h + c, 2)])
            proj_r(0)
            decay_tables()
            for h in range(4):
                rest_r(h)
                if h + 1 < 4:
                    proj_r(h + 1)
            _barrier(S)
        with ExitStack() as ph:
            def sbt(name, shape, dt):
                return ph.enter_context(nc.sbuf_tensor(name, shape, dt))
            wloc = sbt("m_wloc", [128, NT, 8], F32)
            wa2 = sbt("m_wa2", [128, NT, 8], F32)
            a1 = sbt("m_a1", [128, NT, 8], F32)
            a2 = sbt("m_a2", [128, NT, 8], F32)
            enm = sbt("m_enm", [128, NT, 8], F32)
            wint = sbt("m_wint", [128, NT, 8], F32)
            flr = sbt("m_flr", [128, NT, 8], F32)
            mfin = sbt("m_mfin", [128, 2, 8], F32)
            xpre = sbt("m_xpre", [128, T], F32)
            ycv = sbt("m_ycv", [128, T], F32)
            qmT = sbt("m_qmT", [128, T], BF16)
            kmT = sbt("m_kmT", [128, T], BF16)
            kmtok = sbt("m_kmtok", [128, NT, 128], BF16)
            vext = sbt("m_vext", [128, NT, 136], BF16)
            mos = sbt("m_mos", [128, 2, NT, 128], BF16)
            ndall = sbt("m_ndall", [128, NT, 2, 130], F32)
            Cst = sbt("m_Cst", [128, 2, 2, 130], F32)
            Cbf = sbt("m_Cbf", [128, 2, 2, 136], BF16)
            sTb = [sbt("m_sT%d" % i, [128, 128], BF16) for i in range(3)]
            eb = sbt("m_eb", [128, 1, NT, 2], F32)
            st = sbt("m_st", [128, 8, NT], F32)
            gp = ExitStack()

            def sbg(name, shape, dt):
                return gp.enter_context(nc.sbuf_tensor(name, shape, dt))
            ig = sbg("m_ig", [128, NT, 8], F32)
            lf = sbg("m_lf", [128, NT, 8], F32)
            btok = sbg("m_btok", [128, NT, 8], F32)
            tot = sbg("m_tot", [128, NT, 8], F32)
            atok = sbg("m_atok", [128, NT, 8], F32)
            amax = sbg("m_amax", [128, NT, 8], F32)
            ml = sbg("m_ml", [128, NT, 8], F32)
            mprev = sbg("m_mprev", [128, NT, 8], F32)
            mnew = sbg("m_mnew", [128, NT, 8], F32)
            pmtok = sbg("m_pmtok", [128, NT, 8], F32)
            mx = sbg("m_mx", [128, NT, 8], F32)
            aT = sbg("m_aT", [128, 128], F32)
            pfa = sbg("m_pfa", [128, 128], F32)
            pfb = sbg("m_pfb", [128, 128], F32)
            tmp4 = sbg("m_tmp4", [128, 4], F32)
            amaxc = sbg("m_amaxc", [128, 1], F32)
            diagA = sbg("m_diagA", [128, 96], F32)

            def f2(a):
                return a[:].rearrange("p t c -> p (t c)")
            GK = ['gates']

            def gate_prep():
                slG = ring_slot()
                vG = ring[slG][:, 0:128].rearrange("p (k n) -> p k n", k=8)
                S.dma('pool', vG, w_in[:, :, 5120:5136], wr=[('ring', slG)])
                for t in range(NT):
                    for kc in range(8):
                        S.op('pe', I('matmul', out=P[0][:, t * 16:(t + 1) * 16], lhsT=uT[:, kc, t * 128:(t + 1) * 128], rhs=vG[:, kc, :],
                                     start=(kc == 0), stop=(kc == 7)), rd=[('ring', slG), ('u', kc, t // 4)], wr=[('P', 0)], inc=(kc == 7 and t == NT - 1))
                G3 = P[0][:, 0:192].rearrange("p (t c) -> p t c", c=16)
                S.op('dve', I('tensor_tensor', out=ig[:], in0=G3[:, :, 0:8], in1=sm('bi').unsqueeze(1).to_broadcast([128, NT, 8]), op=ALU.add),
                     rd=[('P', 0), 'smalls'], wr=GK)
                S.op('dve', I('tensor_tensor', out=lf[:], in0=G3[:, :, 8:16], in1=sm('bf').unsqueeze(1).to_broadcast([128, NT, 8]), op=ALU.add),
                     rd=[('P', 0), 'smalls'], wr=GK)
                S.op('act', I('activation', out=f2(lf), in_=f2(lf), func=AF.Exp, scale=-1.0), rd=GK, wr=GK)
                S.op('act', I('activation', out=f2(lf), in_=f2(lf), func=AF.Ln, bias=cv[:, 2:3], scale=1.0), rd=GK + CVK, wr=GK)
                S.op('dve', I('tensor_scalar', out=f2(lf), in0=f2(lf), scalar1=-1.0, scalar2=None, op0=ALU.mult), rd=GK, wr=GK)
                S.op('pe', I('matmul', out=P[1][:, 0:96], lhsT=Um, rhs=f2(lf), start=True, stop=True), rd=GK + ['consts'], wr=[('P', 1)])
                S.op('pe', I('matmul', out=P[1][:, 96:192], lhsT=Lm, rhs=f2(lf), start=True, stop=True), rd=GK + ['consts'], wr=[('P', 1)])
                S.op('pe', I('matmul', out=P[1][:, 192:288], lhsT=ones1[:], rhs=f2(lf), start=True, stop=True), rd=GK + ['ones1'], wr=[('P', 1)])
                cF = P[1][:, 0:96].rearrange("p (t c) -> p t c", c=8)
                cB = P[1][:, 96:192].rearrange("p (t c) -> p t c", c=8)
                S.op('dve', I('tensor_copy', out=btok[:, :, 0:4], in_=cF[:, :, 0:4]), rd=[('P', 1)], wr=GK)
                S.op('dve', I('tensor_copy', out=btok[:, :, 4:8], in_=cB[:, :, 4:8]), rd=[('P', 1)], wr=GK)
                S.op('dve', I('tensor_copy', out=f2(tot), in_=P[1][:, 192:288]), rd=[('P', 1)], wr=GK)
                S.op('dve', I('tensor_tensor', out=atok[:], in0=ig[:], in1=btok[:], op=ALU.subtract), rd=GK, wr=GK)
                S.op('pe', I('transpose', out=P[2][0:96, 0:128], in_=f2(atok), identity=ident), rd=GK + ['consts'], wr=[('P', 2)])
                S.op('dve', I('tensor_reduce', out=amaxc[0:96, :], in_=P[2][0:96, 0:128], axis=AX.X, op=ALU.max), rd=[('P', 2)], wr=GK)
                S.op('dve', I('tensor_scalar', out=diagA[0:96, :], in0=consts[0:96, 0:96], scalar1=amaxc[0:96, 0:1], scalar2=None, op0=ALU.mult),
                     rd=GK + ['consts'], wr=GK)
                S.op('pe', I('matmul', out=P[2][:, 128:224], lhsT=ones1[0:96, :], rhs=diagA[0:96, :], start=True, stop=True),
                     rd=GK + ['ones1'], wr=[('P', 2)])
                S.op('dve', I('tensor_copy', out=aT[0:96, :], in_=P[2][0:96, 0:128]), rd=[('P', 2)], wr=['aT'])
                for dirn in range(2):
                    cur, ck = aT, 'aT'
                    for si_, sh in enumerate([1, 2, 4, 8, 16, 32, 64]):
                        nxt, nk = (pfa, 'pfa') if si_ % 2 == 0 else (pfb, 'pfb')
                        if dirn == 0:
                            S.op('dve', I('tensor_tensor', out=nxt[0:96, sh:128], in0=cur[0:96, sh:128], in1=cur[0:96, 0:128 - sh], op=ALU.max), rd=[ck], wr=[nk])
                            S.op('dve', I('tensor_copy', out=nxt[0:96, 0:sh], in_=cur[0:96, 0:sh]), rd=[ck], wr=[nk])
                        else:
                            S.op('dve', I('tensor_tensor', out=nxt[0:96, 0:128 - sh], in0=cur[0:96, 0:128 - sh], in1=cur[0:96, sh:128], op=ALU.max), rd=[ck], wr=[nk])
                            S.op('dve', I('tensor_copy', out=nxt[0:96, 128 - sh:128], in_=cur[0:96, 128 - sh:128]), rd=[ck], wr=[nk])
                        cur, ck = nxt, nk
                    S.op('pe', I('transpose', out=P[1][:, 288 + dirn * 96:288 + (dirn + 1) * 96], in_=cur[0:96, :], identity=consts[0:96, 0:96]),
                         rd=[ck, 'consts'], wr=[('P', 1)])
                    pv_ = P[1][:, 288 + dirn * 96:288 + (dirn + 1) * 96].rearrange("p (t c) -> p t c", c=8)
                    S.op('dve', I('tensor_copy', out=pmtok[:, :, dirn * 4:dirn * 4 + 4], in_=pv_[:, :, dirn * 4:dirn * 4 + 4]), rd=[('P', 1)], wr=GK)
                S.op('dve', I('tensor_copy', out=f2(amax), in_=P[2][:, 128:224]), rd=[('P', 2)], wr=GK)
                S.op('dve', I('tensor_tensor', out=ml[:], in0=tot[:], in1=amax[:], op=ALU.add), rd=GK, wr=GK)
                S.op('dve', I('tensor_tensor', out=wloc[:], in0=atok[:], in1=amax[:], op=ALU.subtract), rd=GK, wr=GK)
                S.op('act', I('activation', out=f2(wloc), in_=f2(wloc), func=AF.Exp), rd=GK, wr=GK)
                for si, (t0, N, samp) in enumerate(SEQS):
                    for d in range(2):
                        cs = slice(d * 4, d * 4 + 4)
                        order = list(range(N)) if d == 0 else list(reversed(range(N)))
                        for oi, n in enumerate(order):
                            t = t0 + n
                            if oi == 0:
                                if samp:
                                    S.op('dve', I('tensor_copy', out=mprev[:, t, cs], in_=sm('m0', d * 4, d * 4 + 4)), rd=GK + ['smalls'], wr=GK)
                                else:
                                    S.op('dve', I('memset', ap=mprev[:, t, cs], constant=0.0), rd=GK, wr=GK)
                            S.op('dve', I('tensor_tensor', out=tmp4[:], in0=tot[:, t, cs], in1=mprev[:, t, cs], op=ALU.add), rd=GK, wr=GK)
                            S.op('dve', I('tensor_tensor', out=mnew[:, t, cs], in0=tmp4[:], in1=ml[:, t, cs], op=ALU.max), rd=GK, wr=GK)
                            if oi + 1 < N:
                                S.op('dve', I('tensor_copy', out=mprev[:, t0 + order[oi + 1], cs], in_=mnew[:, t, cs]), rd=GK, wr=GK)
                            elif not samp:
                                S.op('dve', I('tensor_copy', out=mfin[:, si - 1, cs], in_=mnew[:, t, cs]), rd=GK, wr=GK)
                S.op('dve', I('tensor_tensor', out=a1[:], in0=tot[:], in1=mprev[:], op=ALU.add), rd=GK, wr=GK)
                S.op('dve', I('tensor_tensor', out=a1[:], in0=a1[:], in1=mnew[:], op=ALU.subtract), rd=GK, wr=GK)
                S.op('act', I('activation', out=f2(a1), in_=f2(a1), func=AF.Exp), rd=GK, wr=GK)
                S.op('dve', I('tensor_tensor', out=a2[:], in0=ml[:], in1=mnew[:], op=ALU.subtract), rd=GK, wr=GK)
                S.op('act', I('activation', out=f2(a2), in_=f2(a2), func=AF.Exp), rd=GK, wr=GK)
                S.op('dve', I('tensor_tensor', out=mx[:], in0=pmtok[:], in1=mprev[:], op=ALU.max), rd=GK, wr=GK)
                S.op('dve', I('tensor_tensor', out=enm[:], in0=amax[:], in1=mx[:], op=ALU.subtract), rd=GK, wr=GK)
                S.op('act', I('activation', out=f2(enm), in_=f2(enm), func=AF.Exp), rd=GK, wr=GK)
                S.op('dve', I('tensor_tensor', out=wint[:], in0=mprev[:], in1=mx[:], op=ALU.subtract), rd=GK, wr=GK)
                S.op('act', I('activation', out=f2(wint), in_=f2(wint), func=AF.Exp), rd=GK, wr=GK)
                S.op('dve', I('tensor_tensor', out=flr[:], in0=btok[:], in1=mx[:], op=ALU.add), rd=GK, wr=GK)
                S.op('act', I('activation', out=f2(flr), in_=f2(flr), func=AF.Exp, scale=-1.0), rd=GK, wr=GK)
                S.op('dve', I('tensor_tensor', out=wa2[:], in0=wloc[:], in1=a2[:], op=ALU.mult), rd=GK, wr=GK)
                S.dma('sp', osm_d, mfin[0:1].rearrange("p a b -> p (a b)"), rd=GK, wr=['osm'])
                S.op('dve', I('memset', ap=vext[:, :, 128:129], constant=1.0), wr=['vone'])
                S.op('dve', I('memset', ap=vext[:, :, 129:130], constant=0.0), wr=['vzero'])


            SEGS = [(0, 1024), (1024, 1280), (1280, 1536)]
            def proj(h):
                slQ = ring_slot()
                vQ = ring[slQ][:, 0:2048].rearrange("p (k n) -> p k n", k=8)
                S.dma('pool', vQ[:, :, 0:128], w_in[:, :, 3072 + h * 128:3072 + (h + 1) * 128], wr=[('ring', slQ)])
                S.dma('pool', vQ[:, :, 128:256], w_in[:, :, 3584 + h * 128:3584 + (h + 1) * 128], wr=[('ring', slQ)])
                slV = ring_slot()
                vV = ring[slV][:, 0:2048].rearrange("p (k n) -> p k n", k=8)
                S.dma('pool', vV[:, :, 0:128], w_in[:, :, 4096 + h * 128:4096 + (h + 1) * 128], wr=[('ring', slV)])
                S.dma('pool', vV[:, :, 128:256], w_in[:, :, 4608 + h * 128:4608 + (h + 1) * 128], wr=[('ring', slV)])
                for c in range(2):
                    for g in range(3):
                        gs = slice(g * 512, (g + 1) * 512)
                        for kc in range(8):
                            S.op('pe', I('matmul', out=P[g % 2][:], lhsT=vQ[:, kc, c * 128:(c + 1) * 128], rhs=uT[:, kc, gs], start=(kc == 0), stop=(kc == 7)),
                                 rd=[('ring', slQ), ('u', kc, g)], wr=[('P', g % 2)], inc=(kc == 7))
                        copy_op('act' if g % 2 == 0 else 'dve', xpre[:, gs], P[g % 2][:], [('P', g % 2)], ['xpre'])
                    ch = c * 4 + h
                    cw = lambda j: sm('conv', ch * 4 + j, ch * 4 + j + 1)
                    for (s0, e0) in SEGS:
                        S.op('act', I('activation', out=ycv[:, s0:e0], in_=xpre[:, s0:e0], func=AF.Identity, bias=cw(3), scale=cw(1)),
                             rd=['xpre', 'smalls'], wr=['ycv'])
                        S.op('dve', I('scalar_tensor_tensor', out=ycv[:, s0 + 1:e0], in0=xpre[:, s0:e0 - 1], scalar=cw(0), in1=ycv[:, s0 + 1:e0],
                                      op0=ALU.mult, op1=ALU.add), rd=['xpre', 'ycv', 'smalls'], wr=['ycv'])
                        S.op('dve', I('scalar_tensor_tensor', out=ycv[:, s0:e0 - 1], in0=xpre[:, s0 + 1:e0], scalar=cw(2), in1=ycv[:, s0:e0 - 1],
                                      op0=ALU.mult, op1=ALU.add), rd=['xpre', 'ycv', 'smalls'], wr=['ycv'])
                    if c == 0:
                        S.op('act', I('activation', out=qmT[:], in_=ycv[:], func=AF.Silu), rd=['ycv'], wr=['qmT'])
                    else:
                        S.op('act', I('activation', out=ycv[:], in_=ycv[:], func=AF.Silu), rd=['ycv'], wr=['ycv'])
                        S.op('dve', I('tensor_scalar', out=kmT[:], in0=ycv[:], scalar1=SCK, scalar2=None, op0=ALU.mult), rd=['ycv'], wr=['kmT'])
                for (ta_, tb_) in [(0, 8), (8, 12)]:
                    for t in range(ta_, tb_):
                        S.op('pe', I('transpose', out=Pb[2][:, (t - ta_) * 128:(t - ta_ + 1) * 128], in_=kmT[:, t * 128:(t + 1) * 128], identity=identb[:]),
                             rd=['kmT', 'identb'], wr=[('P', 2)], inc=(t == tb_ - 1))
                    S.op('dve', I('tensor_copy', out=kmtok[:, ta_:tb_, :], in_=Pb[2][:, 0:(tb_ - ta_) * 128].rearrange("p (a b) -> p a b", b=128)),
                         rd=[('P', 2)], wr=['kmtok'])
                for tp in range(NT // 2):
                    pvo = [3, 7, 4, 5][tp % 4]
                    for tt_ in range(2):
                        t = tp * 2 + tt_
                        for kc in range(8):
                            S.op('pe', I('matmul', out=P[pvo][:, tt_ * 256:(tt_ + 1) * 256], lhsT=uT[:, kc, t * 128:(t + 1) * 128], rhs=vV[:, kc, :], start=(kc == 0), stop=(kc == 7)),
                                 rd=[('ring', slV), ('u', kc, t // 4)], wr=[('P', pvo)], inc=(kc == 7 and tt_ == 1))
                    pv4 = P[pvo][:].rearrange("p (a b c) -> p a b c", a=2, b=2)
                    S.op('act', I('activation', out=vext[:, tp * 2:tp * 2 + 2, 0:128], in_=pv4[:, :, 0, :], func=AF.Copy), rd=[('P', pvo)],
                         wr=[('vext', tp * 2), ('vext', tp * 2 + 1)])
                    S.op('act', I('activation', out=mos[:, h % 2, tp * 2:tp * 2 + 2, :], in_=pv4[:, :, 1, :], func=AF.Sigmoid), rd=[('P', pvo)],
                         wr=[('mos', h % 2, tp * 2), ('mos', h % 2, tp * 2 + 1)])

            def loops(h):
                for si, (t0, N, samp) in enumerate(SEQS):
                    p_ = si - 1
                    cver = [0, 0]
                    for d in (1, 0):
                        dh = d * 4 + h
                        if samp:
                            S.op('dve', I('memset', ap=Cst[:, d, 0, 129:130], constant=0.0), wr=[('C', d, 0)])
                            S.dma('sp', Cst[:, d, 0, 0:129], smc_d[:, dh, :], rd=[('C', d, 0)], wr=[('C', d, 0)])
                        else:
                            S.op('dve', I('memset', ap=Cst[:, d, 0, :], constant=0.0), wr=[('C', d, 0)])
                        S.op('act', I('activation', out=Cbf[:, d, 0, 0:130], in_=Cst[:, d, 0, :], func=AF.Copy), rd=[('C', d, 0)], wr=[('Cbf', d, 0)])
                    ordB = [(1, n) for n in reversed(range(N))]
                    ordF = [(0, n) for n in range(N)]
                    steps = [x for pr in zip(ordB, ordF) for x in pr]
                    ns = len(steps)
                    for j, (d, n) in enumerate(steps):
                        t = t0 + n
                        dh = d * 4 + h
                        ts_ = slice(t * 128, (t + 1) * 128)
                        s3 = j % 3
                        s2 = j % 2
                        maskm = Lm if d == 1 else Um
                        qb_ = [0, 1, 4][s3]
                        ib_ = [2, 3, 5][s3]
                        S.op('pe', I('matmul', out=P[qb_][:, 0:128], lhsT=kmT[:, ts_], rhs=qmT[:, ts_], start=True, stop=True),
                             rd=['kmT', 'qmT'], wr=[('P', qb_)])
                        S.op('dve', I('scalar_tensor_tensor', out=sTb[s3][:], in0=P[qb_][:, 0:128], scalar=wloc[:, t, dh:dh + 1], in1=maskm,
                                      op0=ALU.mult, op1=ALU.mult), rd=[('P', qb_), 'consts'] + GK, wr=[('sTb', s3)])
                        S.op('pe', I('matmul', out=P[ib_][:, 0:130], lhsT=sTb[s3][:], rhs=vext[:, t, 0:130], start=True, stop=True),
                             rd=[('sTb', s3), ('vext', t), 'vone', 'vzero'], wr=[('P', ib_)])
                        S.op('act', I('activation', out=ndall[:, t, d, :], in_=P[ib_][:, 0:130], func=AF.Identity, scale=enm[:, t, dh:dh + 1]),
                             rd=[('P', ib_)] + GK, wr=[('nd', t, d)])
                        S.op('dve', I('tensor_scalar', out=wvst[:, j, 0:130], in0=vext[:, t, 0:130], scalar1=wa2[:, t, dh:dh + 1], scalar2=None, op0=ALU.mult),
                             rd=[('vext', t), 'vone', 'vzero'] + GK, wr=[('wvst', j)])

                    def pe_cloc(j):
                        d, n = steps[j]
                        S.op('pe', I('matmul', out=P[4 + j % 2][:, 0:130], lhsT=kmtok[:, t0 + n, :], rhs=wvst[:, j, 0:130], start=True, stop=True),
                             rd=['kmtok', ('wvst', j)], wr=[('P', 4 + j % 2)])

                    CRB = [6, 7, 1]

                    def pe_cross(j):
                        d, n = steps[j]
                        t = t0 + n
                        cb_ = CRB[j % 3]
                        S.op('pe', I('matmul', out=P[cb_][:, 0:130], lhsT=qmT[:, t * 128:(t + 1) * 128], rhs=Cbf[:, d, cver[d], 0:130], start=True, stop=True),
                             rd=['qmT', ('Cbf', d, cver[d])], wr=[('P', cb_)])

                    pe_cloc(0)
                    pe_cross(0)
                    if ns > 1:
                        pe_cross(1)
                    for j, (d, n) in enumerate(steps):
                        t = t0 + n
                        dh = d * 4 + h
                        if j + 1 < ns:
                            pe_cloc(j + 1)
                        cu = cver[d]
                        S.op('dve', I('scalar_tensor_tensor', out=Cst[:, d, 1 - cu, :], in0=Cst[:, d, cu, :], scalar=a1[:, t, dh:dh + 1], in1=P[4 + j % 2][:, 0:130],
                                      op0=ALU.mult, op1=ALU.add), rd=[('P', 4 + j % 2), ('C', d, cu)] + GK, wr=[('C', d, 1 - cu)])
                        S.op('act', I('activation', out=Cbf[:, d, 1 - cu, 0:130], in_=Cst[:, d, 1 - cu, :], func=AF.Copy), rd=[('C', d, 1 - cu)], wr=[('Cbf', d, 1 - cu)])
                        cver[d] = 1 - cu
                        if j + 2 < ns:
                            pe_cross(j + 2)
                        cb_ = CRB[j % 3]
                        S.op('dve', I('scalar_tensor_tensor', out=ndall[:, t, d, :], in0=P[cb_][:, 0:130], scalar=wint[:, t, dh:dh + 1], in1=ndall[:, t, d, :],
                                      op0=ALU.mult, op1=ALU.add), rd=[('P', cb_), ('nd', t, d)] + GK, wr=[('nd', t, d)])
                    if not samp:
                        for d in range(2):
                            S.dma('sp', osc_d[:, p_ * 8 + d * 4 + h, :], Cst[:, d, cver[d], 0:129], rd=[('C', d, cver[d])], wr=[('osc', p_, d * 4 + h)])

            def epilogue(h):
                junk = ndall[:, :, 1, 0:128]
                hmf = ndall[:, :, 0, 0:128]
                hbf = wvst[:, 0:NT, 0:128]
                WVK = [('wvst', j) for j in range(16)]
                NDK = [('nd', t, d) for t in range(NT) for d in range(2)]
                inv = 1.0 / 128.0
                den = ndall[:, :, :, 128]
                S.op('dve', I('scalar_tensor_tensor', out=eb[:, 0], in0=den, scalar=-1.0, in1=den, op0=ALU.mult, op1=ALU.max), rd=NDK, wr=[('eb', 0)])
                fl2 = flr[:].rearrange("p t (d q) -> p t d q", d=2)[:, :, :, h]
                S.op('dve', I('tensor_tensor', out=eb[:, 0], in0=eb[:, 0], in1=fl2, op=ALU.max), rd=[('eb', 0)] + GK, wr=[('eb', 0)])
                S.op('dve', I('reciprocal', out=eb[:, 0], in_=eb[:, 0]), rd=[('eb', 0)], wr=[('eb', 0)])
                for d in range(2):
                    S.op('dve', I('tensor_tensor', out=ndall[:, :, d, 0:128], in0=ndall[:, :, d, 0:128],
                                  in1=eb[:, 0, :, d].unsqueeze(2).to_broadcast([128, NT, 128]), op=ALU.mult), rd=NDK + [('eb', 0)], wr=NDK)
                S.op('dve', I('tensor_tensor', out=hmf, in0=ndall[:, :, 0, 0:128], in1=ndall[:, :, 1, 0:128], op=ALU.add), rd=NDK, wr=NDK)
                S.op('dve', I('tensor_tensor', out=hmf, in0=hmf, in1=mos[:, h % 2], op=ALU.mult), rd=NDK + [('mos', h % 2, t) for t in range(NT)], wr=NDK)
                S.op('dve', I('tensor_reduce', out=st[:, 0, :], in_=hmf, axis=AX.X, op=ALU.add), rd=NDK, wr=[('mst', 0)])
                S.op('act', I('activation', out=junk, in_=hmf, func=AF.Square), rd=NDK, wr=NDK)
                S.op('dve', I('tensor_reduce', out=st[:, 1, :], in_=junk, axis=AX.X, op=ALU.add), rd=NDK, wr=[('mst', 1)])
                S.op('dve', I('tensor_scalar', out=st[:, 2, :], in0=st[:, 0, :], scalar1=inv, scalar2=None, op0=ALU.mult), rd=[('mst', 0)], wr=[('mst', 2)])
                S.op('dve', I('tensor_tensor', out=st[:, 3, :], in0=st[:, 2, :], in1=st[:, 2, :], op=ALU.mult), rd=[('mst', 2)], wr=[('mst', 3)])
                S.op('dve', I('scalar_tensor_tensor', out=st[:, 4, :], in0=st[:, 1, :], scalar=inv, in1=st[:, 3, :], op0=ALU.mult, op1=ALU.subtract),
                     rd=[('mst', 1), ('mst', 3)], wr=[('mst', 4)])
                S.op('act', I('activation', out=st[:, 5, :], in_=st[:, 4, :], func=AF.Sqrt, bias=cv[:, 1:2], scale=1.0), rd=[('mst', 4)] + CVK, wr=[('mst', 5)])
                S.op('dve', I('reciprocal', out=st[:, 5, :], in_=st[:, 5, :]), rd=[('mst', 5)], wr=[('mst', 5)])
                S.op('dve', I('scalar_tensor_tensor', out=st[:, 6, :], in0=st[:, 2, :], scalar=-1.0, in1=st[:, 5, :], op0=ALU.mult, op1=ALU.mult),
                     rd=[('mst', 2), ('mst', 5)], wr=[('mst', 6)])
                for t in range(NT):
                    S.op('act', I('activation', out=hbf[:, t, :], in_=hmf[:, t, :], func=AF.Identity, bias=st[:, 6, t:t + 1], scale=st[:, 5, t:t + 1]),
                         rd=NDK + [('mst', 5), ('mst', 6)], wr=WVK)
                for bi_, (ta_, tb_) in enumerate([(0, 8), (8, 12)]):
                    pe_ = bi_
                    for t in range(ta_, tb_):
                        S.op('pe', I('transpose', out=Pb[pe_][:, (t - ta_) * 128:(t - ta_ + 1) * 128], in_=hbf[:, t, :], identity=identb[:]),
                             rd=WVK + ['identb'], wr=[('P', pe_)], inc=(t == tb_ - 1))
                    S.op('act', I('activation', out=hmT[:, h, ta_ * 128:tb_ * 128], in_=Pb[pe_][:, 0:(tb_ - ta_) * 128], func=AF.Identity,
                                  scale=sm('mgn', h, h + 1)), rd=[('P', pe_), 'smalls'], wr=[('hmT', h, 0), ('hmT', h, 1), ('hmT', h, 2)])
            proj(0)
            gate_prep()
            _barrier(S)
            gp.close()
            wvst = sbt("m_wvst", [128, 16, 136], BF16)
            for h in range(4):
                loops(h)
                if h + 1 < 4:
                    proj(h + 1)
                epilogue(h)
            _barrier(S)

        with ExitStack() as ph:
            merged = ph.enter_context(nc.sbuf_tensor("merged", [128, 8, T], BF16))
            s0t = [ph.enter_context(nc.sbuf_tensor("s0t%d" % i, [128, 512], F32)) for i in range(2)]
            s1t = [ph.enter_context(nc.sbuf_tensor("s1t%d" % i, [128, 512], F32)) for i in range(2)]
            k = 0
            for dc in range(8):
                sl = ring_slot()
                vru = ring[sl][:, 0:1024].rearrange("p (k n) -> p k n", k=8)
                vmu = ring[sl][:, 1024:1536].rearrange("p (k n) -> p k n", k=4)
                vb0 = ring[sl][:, 1536:2560].rearrange("p (k n) -> p k n", k=8)
                vb1 = ring[sl][:, 2560:3584].rearrange("p (k n) -> p k n", k=8)
                cs_ = slice(dc * 128, (dc + 1) * 128)
                S.dma('pool', vru, w_ru[:, :, cs_], wr=[('ring', sl)])
                S.dma('pool', vmu, w_mu[:, :, cs_], wr=[('ring', sl)])
                S.dma('pool', vb0, w_in[:, :, 5136 + dc * 128:5136 + (dc + 1) * 128], wr=[('ring', sl)])
                S.dma('pool', vb1, w_in[:, :, 6160 + dc * 128:6160 + (dc + 1) * 128], wr=[('ring', sl)])
                for g in range(3):
                    gs = slice(g * 512, (g + 1) * 512)
                    pb = 4 * (k % 2)
                    s = k % 2
                    k += 1
                    for kc in range(8):
                        S.op('pe', I('matmul', out=P[pb][:], lhsT=vru[:, kc, :], rhs=rT[:, kc, gs], start=(kc == 0), stop=(kc == 7)),
                             rd=[('ring', sl), ('rT', kc, g)], wr=[('P', pb)], inc=(kc == 7))
                    for kc in range(4):
                        S.op('pe', I('matmul', out=P[pb + 1][:], lhsT=vmu[:, kc, :], rhs=hmT[:, kc, gs], start=(kc == 0), stop=(kc == 3)),
                             rd=[('ring', sl), ('hmT', kc, g)], wr=[('P', pb + 1)], inc=(kc == 3))
                    for kc in range(8):
                        S.op('pe', I('matmul', out=P[pb + 2][:], lhsT=vb0[:, kc, :], rhs=uT[:, kc, gs], start=(kc == 0), stop=(kc == 7)),
                             rd=[('ring', sl), ('u', kc, g)], wr=[('P', pb + 2)], inc=(kc == 7))
                    for kc in range(8):
                        S.op('pe', I('matmul', out=P[pb + 3][:], lhsT=vb1[:, kc, :], rhs=uT[:, kc, gs], start=(kc == 0), stop=(kc == 7)),
                             rd=[('ring', sl), ('u', kc, g)], wr=[('P', pb + 3)], inc=(kc == 7))
                    S.op('act', I('activation', out=s0t[s][:], in_=P[pb + 2][:], func=AF.Sigmoid), rd=[('P', pb + 2)], wr=[('s0t', s)])
                    S.op('act', I('activation', out=s1t[s][:], in_=P[pb + 3][:], func=AF.Sigmoid), rd=[('P', pb + 3)], wr=[('s1t', s)])
                    S.op('dve', I('tensor_tensor', out=s0t[s][:], in0=s0t[s][:], in1=P[pb][:], op=ALU.mult), rd=[('s0t', s), ('P', pb)], wr=[('s0t', s)])
                    S.op('dve', I('tensor_tensor', out=s1t[s][:], in0=s1t[s][:], in1=P[pb + 1][:], op=ALU.mult), rd=[('s1t', s), ('P', pb + 1)], wr=[('s1t', s)])
                    S.op('dve', I('tensor_tensor', out=merged[:, dc, gs], in0=s0t[s][:], in1=s1t[s][:], op=ALU.add),
                         rd=[('s0t', s), ('s1t', s)], wr=[('mg', dc, g)])
            pstat_m = []
            accm = [ph.enter_context(nc.sbuf_tensor("accm%d" % i, [128, 512], F32)) for i in range(3)]
            rstdM = ph.enter_context(nc.sbuf_tensor("rstdM", [128, T], F32))
            tmM = [ph.enter_context(nc.sbuf_tensor("tmM%d" % i, [128, 512], F32)) for i in range(2)]
            afterM = lambda g: norm_apply_group(2, g, (5, 6, 7)[g], rstdM, tmM)
            vo_h = []
            for hh in range(2):
                sl = ring_slot()
                vo = ring[sl][:].rearrange("p (k n) -> p k n", k=8)
                S.dma('pool', vo, w_o[:, :, hh * 512:(hh + 1) * 512], wr=[('ring', sl)])
                vo_h.append((vo, sl))
            for g in range(3):
                gs = slice(g * 512, (g + 1) * 512)
                st_ = 0 if g < 2 else 1
                for dc in range(8):
                    vo, sl = vo_h[dc // 4]
                    cq = (dc % 4) * 128
                    pi = k % 2
                    k += 1
                    for kc in range(8):
                        S.op('pe', I('matmul', out=P[pi][:], lhsT=vo[:, kc, cq:cq + 128], rhs=merged[:, kc, gs], start=(kc == 0), stop=(kc == 7)),
                             rd=[('ring', sl), ('mg', kc, g)], wr=[('P', pi)], inc=(kc == 7))
                    S.op('dve', I('scalar_tensor_tensor', out=xT[:, dc, gs], in0=P[pi][:], scalar=modG[:, 1, dc, st_:st_ + 1], in1=xT[:, dc, gs],
                                  op0=ALU.mult, op1=ALU.add), rd=[('P', pi), ('x', dc, g)] + MODK, wr=[('x', dc, g)])
                    pstat_m.append((dc, g))
                    if len(pstat_m) > 3:
                        emit_stat(s0t, *pstat_m.pop(0), (5, 6, 7), accm, afterM)
            while pstat_m:
                emit_stat(s0t, *pstat_m.pop(0), (5, 6, 7), accm, afterM)
            _barrier(S)
        mxs.close()

        with ExitStack() as ph:
            ffn_phase(ph, 2, w1b, w3b, w2b, stat_banks=(2, 3, 6))
            _barrier(S)

        with ExitStack() as ph:
            yT = [ph.enter_context(nc.sbuf_tensor("yT%d" % i, [128, 8, 512], F32)) for i in range(2)]
            yo = [ph.enter_context(nc.sbuf_tensor("yo%d" % i, [128, D], F32)) for i in range(3)]
            cnt = [0]
            pend_tr = []

            def trans_group(g):
                yb = yT[g % 2]
                for tt in range(4):
                    t = g * 4 + tt
                    o = yo[cnt[0] % 3]
                    ok = ('yo', cnt[0] % 3)
                    pb0 = 0 if cnt[0] % 2 == 0 else 4
                    cnt[0] += 1
                    for half in range(2):
                        pbk = pb0 + half
                        for q in range(4):
                            d2 = half * 4 + q
                            S.op('pe', I('transpose', out=P[pbk][:, q * 128:(q + 1) * 128],
                                         in_=yb[:, d2, tt * 128:(tt + 1) * 128], identity=ident),
                                 rd=[('yT', g % 2, d2), 'consts'], wr=[('P', pbk)], inc=(q == 3))
                        if half == 0:
                            S.op('dve', I('tensor_copy', out=o[:, 0:512], in_=P[pbk][:]), rd=[('P', pbk)], wr=[ok + (0,)])
                        else:
                            S.op('act', I('activation', out=o[:, 512:1024], in_=P[pbk][:], func=AF.Copy), rd=[('P', pbk)], wr=[ok + (1,)])
                    S.dma('sp', y_d[t * 128:(t + 1) * 128, :], o[:], rd=[ok + (0,), ok + (1,)], wr=[('y', t)])

            def fin_cb(g, dc, tmb, tmk):
                S.op('act', I('activation', out=yT[g % 2][:, dc, :], in_=tmb, func=AF.Copy, scale=sm('gf', dc, dc + 1)),
                     rd=[tmk, 'smalls'], wr=[('yT', g % 2, dc)])
                if dc == 7:
                    pend_tr.append(g)
                    if len(pend_tr) > 1:
                        trans_group(pend_tr.pop(0))

            norm_phase(ph, 3, final_cb=fin_cb, banks=(2, 3, 6), pre=True)
            while pend_tr:
                trans_group(pend_tr.pop(0))
            outs = [('y', t) for t in range(NT)] + [('dbgo', k) for k in dbg_outs] + ['osm'] + [('osr', p, d, h) for p in range(2) for d in range(2) for h in range(4)] + [('osc', p, dh) for p in range(2) for dh in range(8)]
            S.finish(outs)
    return nc, list(dbg_outs.keys())


def _consts():
    a = np.arange(128)
    ident = np.eye(128, dtype=np.float32)
    U = (a[None, :] >= a[:, None]).astype(np.float32)
    L = (a[None, :] <= a[:, None]).astype(np.float32)
    negU = np.where(U > 0, 0.0, NEG).astype(np.float32)
    negL = np.where(L > 0, 0.0, NEG).astype(np.float32)
    iota_row = np.broadcast_to(a[None, :].astype(np.float32), (128, 128))
    diff = (a[None, :] - a[:, None]).astype(np.float32)
    Lq = 1024
    r = np.repeat(np.arange(Lq // 64, dtype=np.float32), 64)
    col = (np.arange(Lq) % 64).astype(np.float32)
    n_f = 32
    freqs = (np.float32(10000.0) ** (-np.arange(n_f, dtype=np.float32) / np.float32(n_f))).astype(np.float32)
    ang = np.concatenate([r[:, None] * freqs, col[:, None] * freqs], axis=-1).astype(np.float32)
    cos = np.cos(ang).astype(np.float32).reshape(8, 128, 64).transpose(1, 0, 2).reshape(128, 512)
    sin = np.sin(ang).astype(np.float32).reshape(8, 128, 64).transpose(1, 0, 2).reshape(128, 512)
    return np.ascontiguousarray(np.concatenate([ident, U, L, negU, negL, iota_row, diff, cos, sin], axis=1), dtype=np.float32)


def _pp(v, nch):
    return np.ascontiguousarray(np.asarray(v, dtype=np.float32).reshape(nch, 128).T)


def _bc(v):
    v = np.asarray(v, dtype=np.float32).reshape(-1)
    return np.ascontiguousarray(np.broadcast_to(v[None, :], (128, v.size)))


_PROG = {}


def kernel(**inputs):
    f = {k: np.asarray(v) for k, v in inputs.items()}
    if 'prog' not in _PROG:
        _PROG['prog'] = build_program()
    nc, _ = _PROG['prog']
    consts = _consts()
    shared = {
        "consts": consts,
        "w_ada": np.ascontiguousarray(f['w_ada'][0]),
        "w1_ffn1": np.ascontiguousarray(f['w1_ffn1'][0]), "w3_ffn1": np.ascontiguousarray(f['w3_ffn1'][0]),
        "w2_ffn1": np.ascontiguousarray(f['w2_ffn1'][0]),
        "w1_ffn2": np.ascontiguousarray(f['w1_ffn2'][0]), "w3_ffn2": np.ascontiguousarray(f['w3_ffn2'][0]),
        "w2_ffn2": np.ascontiguousarray(f['w2_ffn2'][0]),
        "w_in": np.ascontiguousarray(f['w_in'][0]), "w_ret_up": np.ascontiguousarray(f['w_ret_up'][0]),
        "w_m_up": np.ascontiguousarray(f['w_m_up'][0]), "w_out": np.ascontiguousarray(f['w_out'][0]),
    }
    conv = np.concatenate([f['conv_w'][0], f['conv_b'][0][None, :]], axis=0)
    conv_pp = np.ascontiguousarray(conv.reshape(4, 8, 128).transpose(2, 1, 0).reshape(128, 32))
    in_maps = []
    for b in range(8):
        x = np.concatenate([f['x_sample'][b], f['x_prompt'][2 * b], f['x_prompt'][2 * b + 1]], axis=0)
        cT = np.stack([_pp(f['c'][b], 8), _pp(f['c_ctx'], 8)], axis=2).reshape(128, 16)
        smalls = np.concatenate([
            cT, _pp(f['b_ada'][0], 72), _pp(f['norm_ffn1'][0], 8), _pp(f['norm_mix'][0], 8), _pp(f['norm_ffn2'][0], 8),
            _pp(f['norm_final'], 8), conv_pp, _bc(f['ret_decay_logit'][0]), _bc(f['b_igate'][0]), _bc(f['b_fgate'][0]),
            _pp(f['ret_gn'][0], 8), _pp(f['m_gn'][0], 4), _bc(f['state_mlstm_m'][b, 0])], axis=1)
        sret = f['state_ret'][b, 0].reshape(8, 128, 256).transpose(1, 0, 2)
        smc = np.concatenate([f['state_mlstm_C'][b, 0].reshape(8, 128, 128).transpose(2, 0, 1),
                              f['state_mlstm_n'][b, 0].reshape(8, 128).T[:, :, None]], axis=2)
        m = dict(shared)
        m.update({"x": np.ascontiguousarray(x, dtype=np.float32), "smalls": np.ascontiguousarray(smalls, dtype=np.float32),
                  "sret": np.ascontiguousarray(sret, dtype=np.float32), "smc": np.ascontiguousarray(smc, dtype=np.float32)})
        in_maps.append(m)
    res = run_bass_kernel_spmd(nc, in_maps, core_ids=list(range(8)))
    _PROG['last'] = res
    y_s = np.zeros((8, 1024, 1024), np.float32)
    y_p = np.zeros((16, 256, 1024), np.float32)
    n_sr = np.zeros((16, 1, 2, 4, 128, 256), np.float32)
    n_C = np.zeros((16, 1, 2, 4, 128, 128), np.float32)
    n_n = np.zeros((16, 1, 2, 4, 128), np.float32)
    n_m = np.zeros((16, 1, 2, 4), np.float32)
    for b in range(8):
        r = res.results[b]
        y = r["y"]
        y_s[b] = y[0:1024]
        y_p[2 * b] = y[1024:1280]
        y_p[2 * b + 1] = y[1280:1536]
        osr = r["o_sr"].reshape(128, 2, 2, 4, 256)
        osc = r["o_sc"].reshape(128, 2, 2, 4, 129)
        osm = r["o_sm"].reshape(2, 2, 4)
        for p in range(2):
            n_sr[2 * b + p, 0] = osr[:, p].transpose(1, 2, 0, 3)
            n_C[2 * b + p, 0] = osc[:, p, :, :, 0:128].transpose(1, 2, 3, 0)
            n_n[2 * b + p, 0] = osc[:, p, :, :, 128].transpose(1, 2, 0)
            n_m[2 * b + p, 0] = osm[p]
    return (y_p, y_s, n_sr, n_C, n_n, n_m)
```

```python
import numpy as np
import ml_dtypes
from contextlib import ExitStack
import concourse.bass as bass
import concourse.mybir as mybir
from concourse.bass_utils import run_bass_kernel_spmd

F32 = mybir.dt.float32
BF16 = mybir.dt.bfloat16
AF = mybir.ActivationFunctionType
ALU = mybir.AluOpType
AX = mybir.AxisListType

N_DMA_SEMS = {'sp': 32, 'pool': 8}


class Sched:
    ENG = ('pe', 'dve', 'act', 'pool', 'sp')

    def __init__(self, nc, es):
        self.nc = nc
        self.es = es
        self.semh = {}
        for e in self.ENG:
            self.semh[e] = es.enter_context(nc.semaphore("s_" + e))
        for q, n in N_DMA_SEMS.items():
            for i in range(n):
                self.semh[('d', q, i)] = es.enter_context(nc.semaphore("s_d%s%d" % (q, i)))
        self.prog = {e: [] for e in self.ENG}
        self.cnt = {e: 0 for e in self.ENG}
        self.seen = {e: {} for e in self.ENG}
        self.lastw = {}
        self.reads = {}
        self.pend = {e: ([], []) for e in self.ENG}
        self.ndma = {q: 0 for q in N_DMA_SEMS}
        self.nps = 0

    def sb(self, name, shape, dtype):
        return self.es.enter_context(self.nc.sbuf_tensor("sb_" + name, shape, dtype))

    def ps(self, name):
        self.nps += 1
        return self.es.enter_context(self.nc.psum_tensor(name, [128, 512], F32))

    def _deps(self, eng, rd, wr):
        ev = []
        for k in rd:
            if k in self.lastw:
                ev.append(self.lastw[k])
        for k in wr:
            if k in self.lastw:
                ev.append(self.lastw[k])
            for r in self.reads.get(k, ()):
                ev.append(r)
        return ev

    def _filter(self, eng, evs):
        best = {}
        for (s, v) in evs:
            if s == 'pe' and eng == 'pe':
                continue
            if self.seen[eng].get(s, 0) >= v:
                continue
            if best.get(s, 0) < v:
                best[s] = v
        for s, v in best.items():
            self.seen[eng][s] = v
        return list(best.items())

    def _register(self, ev, rd, wr):
        for k in rd:
            self.reads.setdefault(k, []).append(ev)
        for k in wr:
            self.lastw[k] = ev
            self.reads[k] = []

    def op(self, eng, fn, rd=(), wr=(), inc=True):
        waits = self._filter(eng, self._deps(eng, rd, wr))
        if inc:
            self.cnt[eng] += 1
            ev = (eng, self.cnt[eng])
            prd, pwr = self.pend[eng]
            self._register(ev, list(prd) + list(rd), list(pwr) + list(wr))
            self.pend[eng] = ([], [])
            self.prog[eng].append((waits, fn, (eng, 1)))
            return ev
        else:
            self.pend[eng][0].extend(rd)
            self.pend[eng][1].extend(wr)
            self.prog[eng].append((waits, fn, None))
            return None

    def dma(self, q, out, in_, rd=(), wr=()):
        nq = N_DMA_SEMS[q]
        slot = self.ndma[q] % nq
        rnd = self.ndma[q] // nq
        self.ndma[q] += 1
        evs = self._deps(q, rd, wr)
        if rnd > 0:
            evs.append((('d', q, slot), 16 * rnd))
        waits = self._filter(q, evs)
        ev = (('d', q, slot), 16 * (rnd + 1))
        self._register(ev, rd, wr)
        self.prog[q].append((waits, lambda e: e.dma_start(out=out, in_=in_), (('d', q, slot), 16)))
        return ev

    def finish(self, out_keys):
        evs = [self.lastw[k] for k in out_keys if k in self.lastw]
        waits = self._filter('sp', evs)
        self.prog['sp'].append((waits, None, None))
        nc = self.nc

        def replay(name, eng):
            for waits, fn, inc in self.prog[name]:
                for s, v in waits:
                    eng.wait_ge(self.semh[s], v)
                if fn is None:
                    continue
                ins = fn(eng)
                if inc is not None:
                    ins.then_inc(self.semh[inc[0]], inc[1])

        with nc.Block() as block:
            @block.tensor
            def _(e):
                replay('pe', e)

            @block.vector
            def _(e):
                replay('dve', e)

            @block.scalar
            def _(e):
                replay('act', e)

            @block.gpsimd
            def _(e):
                replay('pool', e)

            @block.sync
            def _(e):
                replay('sp', e)

        for h in self.semh.values():
            nc.gpsimd.sem_clear(h)
        nc.all_engine_barrier()


def _barrier(S):
    evs = [(e, S.cnt[e]) for e in ('pe', 'dve', 'act') if S.cnt[e] > 0]
    for q, nq in N_DMA_SEMS.items():
        n = S.ndma[q]
        for slot in range(min(n, nq)):
            rounds = (n - 1 - slot) // nq + 1
            evs.append((('d', q, slot), 16 * rounds))
    for eng in ('pe', 'dve', 'act', 'sp'):
        waits = S._filter(eng, evs)
        if waits:
            S.prog[eng].append((waits, None, None))


def I(name, **kw):
    return lambda e: getattr(e, name)(**kw)


D = 1024
DFF = 2816
T = 1536
NT = 12
SEQS = [(0, 8, True), (8, 2, False), (10, 2, False)]
EPS = 1e-6
GN_EPS = 1e-5
NEG = -30000.0
SM = {}
_o = 0
for _n, _w in [('cT', 16), ('bada', 72), ('g1', 8), ('g2', 8), ('g3', 8), ('gf', 8), ('conv', 32), ('dlogit', 8),
               ('bi', 8), ('bf', 8), ('retgn', 8), ('mgn', 4), ('m0', 8)]:
    SM[_n] = (_o, _o + _w)
    _o += _w
SM_W = _o
CW = 7 * 128 + 2 * 512


def build_program(dbg=None):
    nc = bass.Bass("TRN2", target_bir_lowering=False)

    def din(name, shape):
        return nc.dram_tensor(name, list(shape), F32, kind="ExternalInput").ap()

    def dout(name, shape):
        return nc.dram_tensor(name, list(shape), F32, kind="ExternalOutput").ap()

    x_d = din("x", [T, D])
    smalls_d = din("smalls", [128, SM_W])
    consts_d = din("consts", [128, CW])
    sret_d = din("sret", [128, 8, 256])
    smc_d = din("smc", [128, 8, 129])
    w_ada = din("w_ada", [D, 9 * D]).rearrange("(kc p) n -> p kc n", p=128)
    w1a = din("w1_ffn1", [D, DFF]).rearrange("(kc p) n -> p kc n", p=128)
    w3a = din("w3_ffn1", [D, DFF]).rearrange("(kc p) n -> p kc n", p=128)
    w2a = din("w2_ffn1", [DFF, D]).rearrange("(kc p) n -> p kc n", p=128)
    w1b = din("w1_ffn2", [D, DFF]).rearrange("(kc p) n -> p kc n", p=128)
    w3b = din("w3_ffn2", [D, DFF]).rearrange("(kc p) n -> p kc n", p=128)
    w2b = din("w2_ffn2", [DFF, D]).rearrange("(kc p) n -> p kc n", p=128)
    w_in = din("w_in", [D, 7184]).rearrange("(kc p) n -> p kc n", p=128)
    w_ru = din("w_ret_up", [1024, D]).rearrange("(kc p) n -> p kc n", p=128)
    w_mu = din("w_m_up", [512, D]).rearrange("(kc p) n -> p kc n", p=128)
    w_o = din("w_out", [D, D]).rearrange("(kc p) n -> p kc n", p=128)
    y_d = dout("y", [T, D])
    osr_d = dout("o_sr", [128, 16, 256])
    osc_d = dout("o_sc", [128, 16, 129])
    osm_d = dout("o_sm", [1, 16])
    dbg_outs = {}

    with ExitStack() as es:
        es.enter_context(nc.allow_low_precision("bf16 matmul operands, fp32 accumulation"))
        S = Sched(nc, es)
        xT = S.sb("xT", [128, 8, T], F32)
        uT = S.sb("uT", [128, 8, T], BF16)
        ring = [S.sb("ring%d" % i, [128, 4096], BF16) for i in range(4)]
        consts = S.sb("consts", [128, CW], F32)
        smalls = S.sb("smalls", [128, SM_W], F32)
        modT = S.sb("modT", [128, 72, 2], F32)
        modA = S.sb("modA", [128, 3, 8, 2], F32)
        modG = S.sb("modG", [128, 3, 8, 2], F32)
        cv = S.sb("cv", [128, 4], F32)
        ones_ms = S.sb("ones_ms", [128, 128], F32)
        ones1 = S.sb("ones1", [128, 128], F32)
        identb = S.sb("identb", [128, 128], BF16)
        P = [S.ps("P%d" % i) for i in range(8)]
        Pb = [p[:].bitcast(BF16) for p in P]
        ident = consts[:, 0:128]
        Um = consts[:, 128:256]
        Lm = consts[:, 256:384]
        negU = consts[:, 384:512]
        negL = consts[:, 512:640]
        iota_row = consts[:, 640:768]
        diffm = consts[:, 768:896]
        cos_t = consts[:, 896:896 + 512].rearrange("p (t f) -> p t f", t=8)
        sin_t = consts[:, 896 + 512:896 + 1024].rearrange("p (t f) -> p t f", t=8)

        def sm(name, a=None, b=None):
            lo, hi = SM[name]
            if a is None:
                return smalls[:, lo:hi]
            return smalls[:, lo + a:lo + b]

        ring_i = [0]

        def ring_slot():
            i = ring_i[0] % 4
            ring_i[0] += 1
            return i

        def dump(name, ap, shape, keys):
            if dbg is None or name not in dbg:
                return
            d = dout("dbg_" + name, shape)
            dbg_outs[name] = d
            S.dma('sp', d, ap, rd=keys, wr=[('dbgo', name)])

        S.dma('sp', smalls[:], smalls_d, wr=['smalls'])
        S.dma('sp', consts[:], consts_d, wr=['consts'])
        S.op('dve', I('memset', ap=cv[:, 0:1], constant=EPS), wr=['cv0'])
        S.op('dve', I('memset', ap=cv[:, 1:2], constant=GN_EPS), wr=['cv1'])
        S.op('dve', I('memset', ap=cv[:, 2:3], constant=1.0), wr=['cv2'])
        S.op('dve', I('memset', ap=cv[:, 3:4], constant=0.0), wr=['cv'])
        S.op('dve', I('memset', ap=ones_ms[:], constant=1.0 / 1024.0), wr=['ones_ms'])
        S.op('dve', I('memset', ap=ones1[:], constant=1.0), wr=['ones1'])
        S.op('dve', I('tensor_copy', out=identb[:], in_=ident), rd=['consts'], wr=['identb'])
        CVK = ['cv0', 'cv1', 'cv2', 'cv']

        ph0 = es.enter_context(ExitStack())
        scT = ph0.enter_context(nc.sbuf_tensor("scT", [128, 8, 2], BF16))
        S.op('act', I('activation', out=scT[:].rearrange("p a b -> p (a b)"), in_=sm('cT'), func=AF.Silu),
             rd=['smalls'], wr=['scT'])
        xin = [ph0.enter_context(nc.sbuf_tensor("xin%d" % i, [128, D], F32)) for i in range(2)]

        adaw = [ph0.enter_context(nc.sbuf_tensor("adaw%d" % i, [128, 4096], BF16)) for i in range(2)]

        def mod_block(blk):
            sl = blk % 2
            wv_ = adaw[sl][:].rearrange("p (k n) -> p k n", k=8)
            S.dma('pool', wv_, w_ada[:, :, blk * 512:(blk + 1) * 512], wr=[('adaw', sl)])
            for q in range(4):
                ch = blk * 4 + q
                for kc in range(8):
                    S.op('pe', I('matmul', out=P[7][:, ch * 2:ch * 2 + 2], lhsT=wv_[:, kc, q * 128:(q + 1) * 128],
                                 rhs=scT[:, kc, :], start=(kc == 0), stop=(kc == 7)),
                         rd=[('adaw', sl), 'scT'], wr=[('P', 7)], inc=(kc == 7 and q == 3))

        junkx = ph0.enter_context(nc.sbuf_tensor("junkx", [128, D], F32))
        ssc = ph0.enter_context(nc.sbuf_tensor("ssc", [128, NT], F32))
        dgx = [ph0.enter_context(nc.sbuf_tensor("dgx%d" % i, [128, 128], F32)) for i in range(2)]

        def x_tile(t):
            xs = xin[t % 2]
            S.dma('sp', xs[:], x_d[t * 128:(t + 1) * 128, :], wr=[('xin', t % 2)])
            S.op('dve', I('tensor_tensor', out=junkx[:], in0=xs[:], in1=xs[:], op=ALU.mult), rd=[('xin', t % 2)], wr=['junkx'])
            S.op('dve', I('tensor_reduce', out=ssc[:, t:t + 1], in_=junkx[:], axis=AX.X, op=ALU.add), rd=['junkx'], wr=[('ssc', t)])
            S.op('dve', I('tensor_scalar', out=dgx[t % 2][:], in0=ident, scalar1=ssc[:, t:t + 1], scalar2=None, op0=ALU.mult),
                 rd=[('ssc', t), 'consts'], wr=[('dgx', t % 2)])
            S.op('pe', I('matmul', out=P[4 + t // 4][:, (t % 4) * 128:(t % 4 + 1) * 128], lhsT=ones_ms[:], rhs=dgx[t % 2][:], start=True, stop=True),
                 rd=[('dgx', t % 2), 'ones_ms'], wr=[('P', 4 + t // 4)])
            for half in range(2):
                for q in range(4):
                    dc = half * 4 + q
                    S.op('pe', I('transpose', out=P[half][:, q * 128:(q + 1) * 128],
                                 in_=xs[:, dc * 128:(dc + 1) * 128], identity=ident),
                         rd=[('xin', t % 2), 'consts'], wr=[('P', half)], inc=(q == 3))
                dst = xT[:, half * 4:(half + 1) * 4, t * 128:(t + 1) * 128]
                src = P[half][:].rearrange("p (a b) -> p a b", a=4)
                wk = [('x', half * 4 + q, t // 4) for q in range(4)]
                if half == 0:
                    S.op('dve', I('tensor_copy', out=dst, in_=src), rd=[('P', 0)], wr=wk)
                else:
                    S.op('act', I('activation', out=dst, in_=src, func=AF.Copy), rd=[('P', 1)], wr=wk)

        def mod_finish(n0, n1):
            c0, c1 = 24 * n0, 24 * n1
            S.op('dve', I('tensor_tensor', out=modT[:, c0:c1, :], in0=P[7][:, 2 * c0:2 * c1].rearrange("p (c s) -> p c s", s=2),
                          in1=sm('bada', c0, c1).unsqueeze(2).to_broadcast([128, c1 - c0, 2]), op=ALU.add),
                 rd=[('P', 7), 'smalls'], wr=['modT'])
            for n in range(n0, n1):
                gname = ['g1', 'g2', 'g3'][n]
                S.op('dve', I('tensor_scalar', out=modA[:, n], in0=modT[:, (3 * n + 1) * 8:(3 * n + 2) * 8, :],
                              scalar1=1.0, scalar2=None, op0=ALU.add), rd=['modT'], wr=['modA'])
                S.op('dve', I('tensor_tensor', out=modA[:, n], in0=modA[:, n],
                              in1=sm(gname).unsqueeze(2).to_broadcast([128, 8, 2]), op=ALU.mult),
                     rd=['modA', 'smalls'], wr=['modA'])
                S.op('dve', I('tensor_scalar', out=modG[:, n], in0=modT[:, (3 * n + 2) * 8:(3 * n + 3) * 8, :],
                              scalar1=(1.0 if n == 1 else 0.5), scalar2=None, op0=ALU.mult), rd=['modT'], wr=['modG'])

        def mod_part(c0, c1):
            S.op('dve', I('tensor_tensor', out=modT[:, c0:c1, :], in0=P[7][:, 2 * c0:2 * c1].rearrange("p (c s) -> p c s", s=2),
                          in1=sm('bada', c0, c1).unsqueeze(2).to_broadcast([128, c1 - c0, 2]), op=ALU.add),
                 rd=[('P', 7), 'smalls'], wr=['modT'])

        for blk in range(4):
            mod_block(blk)
            x_tile(3 * blk)
            x_tile(3 * blk + 1)
            x_tile(3 * blk + 2)
        mod_part(0, 16)
        S.op('dve', I('tensor_scalar', out=modA[:, 0], in0=modT[:, 8:16, :], scalar1=1.0, scalar2=None, op0=ALU.add), rd=['modT'], wr=['modA'])
        S.op('dve', I('tensor_tensor', out=modA[:, 0], in0=modA[:, 0], in1=sm('g1').unsqueeze(2).to_broadcast([128, 8, 2]), op=ALU.mult),
             rd=['modA', 'smalls'], wr=['modA'])

        def mod_gate0():
            mod_block(5)
            mod_part(16, 24)
            S.op('dve', I('tensor_scalar', out=modG[:, 0], in0=modT[:, 16:24, :], scalar1=0.5, scalar2=None, op0=ALU.mult), rd=['modT'], wr=['modG'])

        mod_extra = [(lambda: mod_block(4)), mod_gate0] + [(lambda bb=bb: mod_block(bb)) for bb in range(6, 18)]
        MODK = ['modT', 'modA', 'modG']

        def norm_phase(ph, n, final_cb=None, banks=(5, 6, 7), pre=False):
            if final_cb is None:
                SGS = [(0, 1024, 0), (1024, 512, 1)]
            else:
                SGS = [(0, 512, 0), (512, 512, 0), (1024, 512, 1)]
            sq = [ph.enter_context(nc.sbuf_tensor("sq%d_%d" % (n, i), [128, 1024], F32)) for i in range(2)] if not pre else None
            rstd = ph.enter_context(nc.sbuf_tensor("rstd%d" % n, [128, T], F32))
            tm = [ph.enter_context(nc.sbuf_tensor("tm%d_%d" % (n, i), [128, 1024], F32)) for i in range(2)]
            kq = [0]

            def stats(o, w):
                nch = w // 512
                g0 = o // 512
                for dc in range(8 if not pre else 0):
                    s = kq[0] % 2
                    kq[0] += 1
                    S.op('act', I('activation', out=sq[s][:, 0:w], in_=xT[:, dc, o:o + w], func=AF.Square),
                         rd=[('x', dc, g0 + c) for c in range(nch)], wr=[('sq', s)])
                    for c in range(nch):
                        pb = banks[(g0 + c) % 3]
                        S.op('pe', I('matmul', out=P[pb][:], lhsT=ones_ms[:], rhs=sq[s][:, c * 512:(c + 1) * 512], start=(dc == 0), stop=(dc == 7)),
                             rd=[('sq', s), 'ones_ms'], wr=[('P', pb)], inc=True)
                for c in range(nch):
                    pb = banks[(g0 + c) % 3]
                    S.op('act', I('activation', out=rstd[:, o + c * 512:o + (c + 1) * 512], in_=P[pb][:], func=AF.Ln, bias=cv[:, 0:1], scale=1.0),
                         rd=[('P', pb)] + CVK, wr=[('rstd', g0 + c)])
                S.op('act', I('activation', out=rstd[:, o:o + w], in_=rstd[:, o:o + w], func=AF.Exp, scale=-0.5), rd=[('rstd', g0 + c) for c in range(nch)],
                     wr=[('rstd', g0 + c) for c in range(nch)])

            def apply(o, w, st):
                nch = w // 512
                g0 = o // 512
                for dc in range(8):
                    s = kq[0] % 2
                    kq[0] += 1
                    S.op('dve', I('tensor_tensor', out=tm[s][:, 0:w], in0=xT[:, dc, o:o + w], in1=rstd[:, o:o + w], op=ALU.mult),
                         rd=[('x', dc, g0 + c) for c in range(nch)] + [('rstd', g0 + c) for c in range(nch)], wr=[('tm', s)])
                    if final_cb is None:
                        S.op('act', I('activation', out=uT[:, dc, o:o + w], in_=tm[s][:, 0:w], func=AF.Identity,
                                      bias=modT[:, 3 * n * 8 + dc, st:st + 1], scale=modA[:, n, dc, st:st + 1]),
                             rd=[('tm', s)] + MODK, wr=[('u', dc, g0 + c) for c in range(nch)])
                    else:
                        final_cb(g0, dc, tm[s][:, 0:512], ('tm', s))

            stats(SGS[0][0], SGS[0][1])
            for i_, (o, w, st) in enumerate(SGS):
                if i_ + 1 < len(SGS):
                    stats(SGS[i_ + 1][0], SGS[i_ + 1][1])
                apply(o, w, st)

        stat_k = [0]

        def emit_stat(sqb, dc, g, banks, acc):
            gs = slice(g * 512, (g + 1) * 512)
            if dc == 0:
                S.op('act', I('activation', out=acc[g][:], in_=xT[:, dc, gs], func=AF.Square), rd=[('x', dc, g)], wr=[('acc', g)])
            else:
                s = stat_k[0] % 2
                stat_k[0] += 1
                S.op('act', I('activation', out=sqb[s][:], in_=xT[:, dc, gs], func=AF.Square), rd=[('x', dc, g)], wr=[('sqb', s)])
                S.op('dve', I('tensor_tensor', out=acc[g][:], in0=acc[g][:], in1=sqb[s][:], op=ALU.add), rd=[('acc', g), ('sqb', s)], wr=[('acc', g)])
            if dc == 7:
                S.op('pe', I('matmul', out=P[banks[g]][:], lhsT=ones_ms[:], rhs=acc[g][:], start=True, stop=True),
                     rd=[('acc', g), 'ones_ms'], wr=[('P', banks[g])], inc=True)

        def ffn_phase(ph, n, w1, w3, w2, extra=(), stat_banks=None):
            extra = list(extra)
            sqb = [ph.enter_context(nc.sbuf_tensor("sqb%d_%d" % (n, i), [128, 512], F32)) for i in range(2)]
            pstat = []
            accb = [ph.enter_context(nc.sbuf_tensor("accb%d_%d" % (n, i), [128, 512], F32)) for i in range(3)] if stat_banks is not None else None
            hT = ph.enter_context(nc.sbuf_tensor("hT%d" % n, [128, 11, T], BF16))
            sa = [ph.enter_context(nc.sbuf_tensor("sa%d_%d" % (n, i), [128, 512], F32)) for i in range(2)]
            k = 0
            for half in range(2):
                for (off, wdt) in [(0, 512), (512, 512), (1024, 384)]:
                    c0 = half * 1408 + off
                    s1 = ring_slot()
                    v1 = ring[s1][:].rearrange("p (k n) -> p k n", k=8)
                    S.dma('pool', v1[:, :, 0:wdt], w1[:, :, c0:c0 + wdt], wr=[('ring', s1)])
                    s3 = ring_slot()
                    v3 = ring[s3][:].rearrange("p (k n) -> p k n", k=8)
                    S.dma('pool', v3[:, :, 0:wdt], w3[:, :, c0:c0 + wdt], wr=[('ring', s3)])
                    pending_extra = extra.pop(0) if extra else None
                    for q in range(wdt // 128):
                        fl = (off + q * 128) // 128
                        for g in range(3):
                            gs = slice(g * 512, (g + 1) * 512)
                            pi = 2 * (k % 2)
                            s = k % 2
                            k += 1
                            for kc in range(8):
                                S.op('pe', I('matmul', out=P[pi][:], lhsT=v1[:, kc, q * 128:(q + 1) * 128], rhs=uT[:, kc, gs],
                                             start=(kc == 0), stop=(kc == 7)),
                                     rd=[('ring', s1), ('u', kc, g)], wr=[('P', pi)], inc=(kc == 7))
                            for kc in range(8):
                                S.op('pe', I('matmul', out=P[pi + 1][:], lhsT=v3[:, kc, q * 128:(q + 1) * 128], rhs=uT[:, kc, gs],
                                             start=(kc == 0), stop=(kc == 7)),
                                     rd=[('ring', s3), ('u', kc, g)], wr=[('P', pi + 1)], inc=(kc == 7))
                            S.op('act', I('activation', out=sa[s][:], in_=P[pi][:], func=AF.Silu), rd=[('P', pi)], wr=[('sa', s)])
                            S.op('dve', I('tensor_tensor', out=hT[:, fl, gs], in0=sa[s][:], in1=P[pi + 1][:], op=ALU.mult),
                                 rd=[('sa', s), ('P', pi + 1)], wr=[('h', fl, g)])
                    if pending_extra is not None:
                        pending_extra()
                for cb in range(4):
                    sl = ring_slot()
                    v2 = ring[sl][:, 0:11 * 256].rearrange("p (k n) -> p k n", k=11)
                    S.dma('pool', v2, w2[:, half * 11:(half + 1) * 11, cb * 256:(cb + 1) * 256], wr=[('ring', sl)])
                    pend2 = extra.pop(0) if extra else None
                    for q in range(2):
                        dc = cb * 2 + q
                        for g in range(3):
                            gs = slice(g * 512, (g + 1) * 512)
                            st = 0 if g < 2 else 1
                            pi = 4 + (k % 2)
                            k += 1
                            for fl in range(11):
                                S.op('pe', I('matmul', out=P[pi][:], lhsT=v2[:, fl, q * 128:(q + 1) * 128], rhs=hT[:, fl, gs],
                                             start=(fl == 0), stop=(fl == 10)),
                                     rd=[('ring', sl), ('h', fl, g)], wr=[('P', pi)], inc=(fl == 10))
                            S.op('dve', I('scalar_tensor_tensor', out=xT[:, dc, gs], in0=P[pi][:], scalar=modG[:, n, dc, st:st + 1],
                                          in1=xT[:, dc, gs], op0=ALU.mult, op1=ALU.add),
                                 rd=[('P', pi), ('x', dc, g)] + MODK, wr=[('x', dc, g)])
                            if half == 1 and stat_banks is not None:
                                pstat.append((dc, g))
                                if len(pstat) > 3:
                                    emit_stat(sqb, *pstat.pop(0), stat_banks, accb)
                    if pend2 is not None:
                        pend2()
            while extra:
                extra.pop(0)()
            while pstat:
                emit_stat(sqb, *pstat.pop(0), stat_banks, accb)

        with ExitStack() as ph:
            norm_phase(ph, 0, banks=(4, 5, 6), pre=True)
            ffn_phase(ph, 0, w1a, w3a, w2a, extra=mod_extra, stat_banks=(0, 1, 2))
            mod_finish(1, 3)
            _barrier(S)
        ph0.close()

        SCK = 128.0 ** -0.5
        mxs = es.enter_context(ExitStack())
        rT = mxs.enter_context(nc.sbuf_tensor("rT", [128, 8, T], BF16))
        hmT = mxs.enter_context(nc.sbuf_tensor("hmT", [128, 4, T], BF16))
        with ExitStack() as ph:
            norm_phase(ph, 1, banks=(0, 1, 2), pre=True)
            _barrier(S)

        tctr = [0]

        def alt():
            tctr[0] += 1
            return 'dve' if tctr[0] % 2 == 0 else 'act'

        def copy_op(eng, out, in_, rd, wr):
            if eng == 'dve':
                S.op('dve', I('tensor_copy', out=out, in_=in_), rd=rd, wr=wr)
            else:
                S.op('act', I('activation', out=out, in_=in_, func=AF.Copy), rd=rd, wr=wr)

        def group_norm_stats(st, src_ap, src_keys, width, junk, tag):
            inv = 1.0 / width
            S.op('dve', I('tensor_reduce', out=st[:, 0:1], in_=src_ap, axis=AX.X, op=ALU.add), rd=src_keys, wr=[(tag, 0)])
            S.op('act', I('activation', out=junk, in_=src_ap, func=AF.Square), rd=src_keys, wr=[(tag, 'junk')])
            S.op('dve', I('tensor_reduce', out=st[:, 1:2], in_=junk, axis=AX.X, op=ALU.add), rd=[(tag, 'junk')], wr=[(tag, 1)])
            S.op('dve', I('tensor_scalar', out=st[:, 2:3], in0=st[:, 0:1], scalar1=inv, scalar2=None, op0=ALU.mult),
                 rd=[(tag, 0)], wr=[(tag, 2)])
            S.op('dve', I('tensor_tensor', out=st[:, 3:4], in0=st[:, 2:3], in1=st[:, 2:3], op=ALU.mult), rd=[(tag, 2)], wr=[(tag, 3)])
            S.op('dve', I('scalar_tensor_tensor', out=st[:, 4:5], in0=st[:, 1:2], scalar=inv, in1=st[:, 3:4],
                          op0=ALU.mult, op1=ALU.subtract), rd=[(tag, 1), (tag, 3)], wr=[(tag, 4)])
            S.op('act', I('activation', out=st[:, 5:6], in_=st[:, 4:5], func=AF.Sqrt, bias=cv[:, 1:2], scale=1.0),
                 rd=[(tag, 4)] + CVK, wr=[(tag, 5)])
            S.op('dve', I('reciprocal', out=st[:, 5:6], in_=st[:, 5:6]), rd=[(tag, 5)], wr=[(tag, 5)])
            S.op('dve', I('scalar_tensor_tensor', out=st[:, 6:7], in0=st[:, 2:3], scalar=-1.0, in1=st[:, 5:6],
                          op0=ALU.mult, op1=ALU.mult), rd=[(tag, 2), (tag, 5)], wr=[(tag, 6)])

        with ExitStack() as ph:
            def sbt(name, shape, dt):
                return ph.enter_context(nc.sbuf_tensor(name, shape, dt))
            lg = sbt("r_lg", [128, 8], F32)
            nlg = sbt("r_nlg", [128, 8], F32)
            lg127 = sbt("r_lg127", [128, 8], F32)
            lg128 = sbt("r_lg128", [128, 8], F32)
            gch = sbt("r_gch", [128, 8], F32)
            wkt = sbt("r_wkt", [128, 8], F32)
            Mh = sbt("r_Mh", [128, 4, 128], F32)
            Wq = sbt("r_Wq", [128, 8, 128], F32)
            ta = sbt("r_ta", [128, 128], F32)
            tb = sbt("r_tb", [128, 128], F32)
            qktok = sbt("r_qktok", [128, NT, 2, 128], BF16)
            qkT = sbt("r_qkT", [128, 2, T], BF16)
            rv = sbt("r_rv", [128, NT, 256], BF16)
            rgs = sbt("r_rgs", [128, NT, 256], BF16)
            Sst = sbt("r_Sst", [128, 2, 2, 256], F32)
            Sbst = sbt("r_Sbst", [128, 8, 256], BF16)
            Sfb = [sbt("r_Sfb%d" % i, [128, 256], BF16) for i in range(2)]
            kw = [sbt("r_kw%d" % i, [128, 128], BF16) for i in range(2)]
            qf = [sbt("r_qf%d" % i, [128, 128], BF16) for i in range(2)]
            qb = [sbt("r_qb%d" % i, [128, 128], BF16) for i in range(2)]
            sTm = [sbt("r_sTm%d" % i, [128, 128], BF16) for i in range(2)]
            rt = [sbt("r_rt%d" % i, [128, 2, 64], F32) for i in range(4)]
            oall = sbt("r_oall", [128, 8, 256], F32)
            junk = sbt("r_junk", [128, 2, 256], F32)
            st = sbt("r_st", [128, 8, 8], F32)
            DK = ['lg', 'nlg', 'lg127', 'lg128', 'consts']

            def decay_tables():
                S.op('act', I('activation', out=lg[:], in_=sm('dlogit'), func=AF.Exp, scale=-1.0), rd=['smalls'], wr=['lg'])
                S.op('act', I('activation', out=lg[:], in_=lg[:], func=AF.Ln, bias=cv[:, 2:3], scale=1.0), rd=['lg'] + CVK, wr=['lg'])
                S.op('dve', I('tensor_scalar', out=nlg[:], in0=lg[:], scalar1=1.0, scalar2=None, op0=ALU.mult), rd=['lg'], wr=['nlg'])
                S.op('dve', I('tensor_scalar', out=lg[:], in0=nlg[:], scalar1=-1.0, scalar2=None, op0=ALU.mult), rd=['nlg'], wr=['lg'])
                S.op('dve', I('tensor_scalar', out=lg127[:], in0=lg[:], scalar1=127.0, scalar2=None, op0=ALU.mult), rd=['lg'], wr=['lg127'])
                S.op('dve', I('tensor_scalar', out=lg128[:], in0=lg[:], scalar1=128.0, scalar2=None, op0=ALU.mult), rd=['lg'], wr=['lg128'])
                S.op('act', I('activation', out=gch[:], in_=lg128[:], func=AF.Exp), rd=['lg128'], wr=['dec_g'])
                DK = ['lg', 'nlg', 'lg127', 'lg128', 'consts']
                for h in range(4):
                    S.op('dve', I('tensor_scalar', out=ta[:], in0=diffm, scalar1=lg[:, h:h + 1], scalar2=0.0, op0=ALU.mult, op1=ALU.min),
                         rd=DK, wr=['ta'])
                    S.op('act', I('activation', out=ta[:], in_=ta[:], func=AF.Exp), rd=['ta'], wr=['ta'])
                    S.op('dve', I('tensor_tensor', out=ta[:], in0=ta[:], in1=Um, op=ALU.mult), rd=['ta', 'consts'], wr=['ta'])
                    S.op('dve', I('tensor_scalar', out=tb[:], in0=diffm, scalar1=nlg[:, 4 + h:5 + h], scalar2=0.0, op0=ALU.mult, op1=ALU.min),
                         rd=DK, wr=['tb'])
                    S.op('act', I('activation', out=tb[:], in_=tb[:], func=AF.Exp), rd=['tb'], wr=['tb'])
                    S.op('dve', I('tensor_tensor', out=tb[:], in0=tb[:], in1=Lm, op=ALU.mult), rd=['tb', 'consts'], wr=['tb'])
                    S.op('dve', I('tensor_tensor', out=ta[:], in0=ta[:], in1=tb[:], op=ALU.add), rd=['ta', 'tb'], wr=['ta'])
                    S.op('dve', I('tensor_scalar', out=Mh[:, h, :], in0=ta[:], scalar1=SCK, scalar2=None, op0=ALU.mult), rd=['ta'], wr=['dec_M'])
                    S.op('act', I('activation', out=Wq[:, h, :], in_=iota_row, func=AF.Exp, bias=lg[:, h:h + 1], scale=lg[:, h:h + 1]),
                         rd=DK, wr=['dec_Wq'])
                    S.op('act', I('activation', out=Wq[:, 4 + h, :], in_=iota_row, func=AF.Exp, bias=lg128[:, 4 + h:5 + h], scale=nlg[:, 4 + h:5 + h]),
                         rd=DK, wr=['dec_Wq'])
                    S.op('act', I('activation', out=wkt[:, h:h + 1], in_=diffm[:, 0:1], func=AF.Exp, bias=lg127[:, h:h + 1], scale=lg[:, h:h + 1]),
                         rd=DK, wr=['dec_wk'])
                    S.op('act', I('activation', out=wkt[:, 4 + h:5 + h], in_=diffm[:, 0:1], func=AF.Exp, scale=nlg[:, 4 + h:5 + h]),
                         rd=DK, wr=['dec_wk'])
                S.op('dve', I('tensor_scalar', out=wkt[:], in0=wkt[:], scalar1=SCK, scalar2=None, op0=ALU.mult), rd=['dec_wk'], wr=['dec_wk'])

            DEC = ['dec_g', 'dec_M', 'dec_Wq', 'dec_wk']

            kk = [0]

            def proj_r(h):
                slA = ring_slot()
                vA = ring[slA][:, 0:2048].rearrange("p (k n) -> p k n", k=8)
                S.dma('pool', vA[:, :, 0:128], w_in[:, :, h * 128:(h + 1) * 128], wr=[('ring', slA)])
                S.dma('pool', vA[:, :, 128:256], w_in[:, :, 512 + h * 128:512 + (h + 1) * 128], wr=[('ring', slA)])
                slB = ring_slot()
                vB = ring[slB][:].rearrange("p (k n) -> p k n", k=8)
                S.dma('pool', vB[:, :, 0:256], w_in[:, :, 1024 + h * 256:1024 + (h + 1) * 256], wr=[('ring', slB)])
                S.dma('pool', vB[:, :, 256:512], w_in[:, :, 2048 + h * 256:2048 + (h + 1) * 256], wr=[('ring', slB)])
                for t in range(NT):
                    ts_ = slice(t * 128, (t + 1) * 128)
                    pq = t % 2
                    pv = 2 if t % 2 == 0 else 6
                    for kc in range(8):
                        S.op('pe', I('matmul', out=P[pq][:, 0:256], lhsT=uT[:, kc, ts_], rhs=vA[:, kc, :], start=(kc == 0), stop=(kc == 7)),
                             rd=[('ring', slA), ('u', kc, t // 4)], wr=[('P', pq)], inc=(kc == 7))
                    for kc in range(8):
                        S.op('pe', I('matmul', out=P[pv][:], lhsT=uT[:, kc, ts_], rhs=vB[:, kc, :], start=(kc == 0), stop=(kc == 7)),
                             rd=[('ring', slB), ('u', kc, t // 4)], wr=[('P', pv)], inc=(kc == 7))
                    if t < 8:
                        X = P[pq][:, 0:256].rearrange("p (a b) -> p a b", a=2)
                        x1 = X[:, :, 0:64]
                        x2 = X[:, :, 64:128]
                        cb_ = cos_t[:, t, :].unsqueeze(1).to_broadcast([128, 2, 64])
                        sb_ = sin_t[:, t, :].unsqueeze(1).to_broadcast([128, 2, 64])
                        S.op('dve', I('tensor_tensor', out=rt[0][:], in0=x1, in1=cb_, op=ALU.mult), rd=[('P', pq), 'consts'], wr=[('rt', 0)])
                        S.op('dve', I('tensor_tensor', out=rt[1][:], in0=x2, in1=sb_, op=ALU.mult), rd=[('P', pq), 'consts'], wr=[('rt', 1)])
                        S.op('dve', I('tensor_tensor', out=rt[2][:], in0=x2, in1=cb_, op=ALU.mult), rd=[('P', pq), 'consts'], wr=[('rt', 2)])
                        S.op('dve', I('tensor_tensor', out=rt[3][:], in0=x1, in1=sb_, op=ALU.mult), rd=[('P', pq), 'consts'], wr=[('rt', 3)])
                        S.op('dve', I('tensor_tensor', out=qktok[:, t, :, 0:64], in0=rt[0][:], in1=rt[1][:], op=ALU.subtract),
                             rd=[('rt', 0), ('rt', 1)], wr=[('qktok', t, 0)])
                        S.op('dve', I('tensor_tensor', out=qktok[:, t, :, 64:128], in0=rt[2][:], in1=rt[3][:], op=ALU.add),
                             rd=[('rt', 2), ('rt', 3)], wr=[('qktok', t, 1)])
                    else:
                        S.op('act', I('activation', out=qktok[:, t].rearrange("p a b -> p (a b)"), in_=P[pq][:, 0:256], func=AF.Copy),
                             rd=[('P', pq)], wr=[('qktok', t, 0), ('qktok', t, 1)])
                    S.op('act', I('activation', out=rv[:, t, :], in_=P[pv][:, 0:256], func=AF.Copy), rd=[('P', pv)], wr=[('rv', t)])
                    S.op('act', I('activation', out=rgs[:, t, :], in_=P[pv][:, 256:512], func=AF.Silu), rd=[('P', pv)], wr=[('rgs', t)])
                    pt = 3 if t % 2 == 0 else 7
                    for c in range(2):
                        S.op('pe', I('transpose', out=Pb[pt][:, c * 128:(c + 1) * 128], in_=qktok[:, t, c, :], identity=identb[:]),
                             rd=[('qktok', t, 0), ('qktok', t, 1), 'identb'], wr=[('P', pt)], inc=(c == 1))
                    copy_op('dve' if t % 2 == 0 else 'act', qkT[:, :, ts_], Pb[pt][:, 0:256].rearrange("p (a b) -> p a b", a=2), [('P', pt)], [('qkT', t)])

            def rest_r(h):
                for si, (t0, N, samp) in enumerate(SEQS):
                    p_ = si - 1
                    ebase = 0 if samp else 8
                    sver = [0, 0]
                    if samp:
                        S.dma('sp', Sst[:, 0, 0, :], sret_d[:, h, :], wr=[('S', 0, 0)])
                        S.dma('sp', Sst[:, 1, 0, :], sret_d[:, 4 + h, :], wr=[('S', 1, 0)])
                    else:
                        S.op('dve', I('memset', ap=Sst[:, 0, 0, :], constant=0.0), wr=[('S', 0, 0)])
                        S.op('dve', I('memset', ap=Sst[:, 1, 0, :], constant=0.0), wr=[('S', 1, 0)])

                    def kv_mm(t, d):
                        s = kk[0] % 2
                        kk[0] += 1
                        di = d * 4 + h
                        S.op('dve', I('tensor_scalar', out=kw[s][:], in0=qktok[:, t, 1, :], scalar1=wkt[:, di:di + 1], scalar2=None, op0=ALU.mult),
                             rd=[('qktok', t, 0), ('qktok', t, 1)] + DEC, wr=[('kw', s)])
                        S.op('pe', I('matmul', out=P[5 + s][:, 0:256], lhsT=kw[s][:], rhs=rv[:, t, :], start=True, stop=True),
                             rd=[('kw', s), ('rv', t)], wr=[('P', 5 + s)])
                        return s

                    def s_update(d, s):
                        di = d * 4 + h
                        cu = sver[d]
                        S.op('dve', I('scalar_tensor_tensor', out=Sst[:, d, 1 - cu, :], in0=Sst[:, d, cu, :], scalar=gch[:, di:di + 1], in1=P[5 + s][:, 0:256],
                                      op0=ALU.mult, op1=ALU.add), rd=[('S', d, cu), ('P', 5 + s)] + DEC, wr=[('S', d, 1 - cu)])
                        sver[d] = 1 - cu

                    order = list(reversed(range(N)))
                    pend = kv_mm(t0 + order[0], 1)
                    for oi, n in enumerate(order):
                        cur = pend
                        if oi + 1 < N:
                            pend = kv_mm(t0 + order[oi + 1], 1)
                        S.op('act', I('activation', out=Sbst[:, n, :], in_=Sst[:, 1, sver[1], :], func=AF.Copy), rd=[('S', 1, sver[1])], wr=[('Sbst', n)])
                        s_update(1, cur)
                    if not samp:
                        S.dma('sp', osr_d[:, p_ * 8 + 4 + h, :], Sst[:, 1, sver[1], :], rd=[('S', 1, sver[1])], wr=[('osr', p_, 1, h)])

                    def indep(n):
                        t = t0 + n
                        ts_ = slice(t * 128, (t + 1) * 128)
                        s = n % 2
                        S.op('dve', I('tensor_tensor', out=qf[s][:], in0=qkT[:, 0, ts_], in1=Wq[:, h, :], op=ALU.mult),
                             rd=[('qkT', t)] + DEC, wr=[('qf', s)])
                        S.op('dve', I('tensor_tensor', out=qb[s][:], in0=qkT[:, 0, ts_], in1=Wq[:, 4 + h, :], op=ALU.mult),
                             rd=[('qkT', t)] + DEC, wr=[('qb', s)])
                        S.op('pe', I('matmul', out=P[3 + s][:, 0:128], lhsT=qkT[:, 1, ts_], rhs=qkT[:, 0, ts_], start=True, stop=True),
                             rd=[('qkT', t)], wr=[('P', 3 + s)])
                        S.op('dve', I('tensor_tensor', out=sTm[s][:], in0=P[3 + s][:, 0:128], in1=Mh[:, h, :], op=ALU.mult),
                             rd=[('P', 3 + s)] + DEC, wr=[('sTm', s)])
                        return kv_mm(t, 0)

                    pend = indep(0)
                    for n in range(N):
                        t = t0 + n
                        s = n % 2
                        cur = pend
                        if n + 1 < N:
                            pend = indep(n + 1)
                        S.op('act', I('activation', out=Sfb[s][:], in_=Sst[:, 0, sver[0], :], func=AF.Copy), rd=[('S', 0, sver[0])], wr=[('Sfb', s)])
                        po = 0 if s == 0 else 7
                        pk = ('P', 0) if s == 0 else ('P', 7)
                        S.op('pe', I('matmul', out=P[po][:, 0:256], lhsT=qf[s][:], rhs=Sfb[s][:], start=True, stop=False),
                             rd=[('qf', s), ('Sfb', s)], wr=[pk], inc=False)
                        S.op('pe', I('matmul', out=P[po][:, 0:256], lhsT=qb[s][:], rhs=Sbst[:, n, :], start=False, stop=False),
                             rd=[('qb', s), ('Sbst', n)], wr=[pk], inc=False)
                        S.op('pe', I('matmul', out=P[po][:, 0:256], lhsT=sTm[s][:], rhs=rv[:, t, :], start=False, stop=True),
                             rd=[('sTm', s), ('rv', t)], wr=[pk], inc=True)
                        S.op('act', I('activation', out=oall[:, t - ebase, :], in_=P[po][:, 0:256], func=AF.Copy), rd=[pk], wr=[('oall', t - ebase)])
                        s_update(0, cur)
                    if not samp:
                        S.dma('sp', osr_d[:, p_ * 8 + h, :], Sst[:, 0, sver[0], :], rd=[('S', 0, sver[0])], wr=[('osr', p_, 0, h)])

                    if si == 1:
                        continue
                    ea, eb_ = (0, 8) if samp else (8, 12)
                    ne = eb_ - ea
                    OK_ = [('oall', j) for j in range(ne)]
                    QK_ = [('qktok', t, c) for t in range(ea, eb_) for c in range(2)]
                    inv = 1.0 / 256.0
                    S.op('dve', I('tensor_reduce', out=st[:, 0, 0:ne], in_=oall[:, 0:ne, :], axis=AX.X, op=ALU.add), rd=OK_, wr=[('rst', 0)])
                    for j in range(0, ne, 2):
                        S.op('act', I('activation', out=junk[:].rearrange("p a b -> p (a b)"), in_=oall[:, j:j + 2, :].rearrange("p a b -> p (a b)"), func=AF.Square),
                             rd=OK_, wr=['rjunk'])
                        S.op('dve', I('tensor_reduce', out=st[:, 1, j:j + 2], in_=junk[:], axis=AX.X, op=ALU.add), rd=['rjunk'], wr=[('rst', 1)])
                    S.op('dve', I('tensor_scalar', out=st[:, 2, 0:ne], in0=st[:, 0, 0:ne], scalar1=inv, scalar2=None, op0=ALU.mult), rd=[('rst', 0)], wr=[('rst', 2)])
                    S.op('dve', I('tensor_tensor', out=st[:, 3, 0:ne], in0=st[:, 2, 0:ne], in1=st[:, 2, 0:ne], op=ALU.mult), rd=[('rst', 2)], wr=[('rst', 3)])
                    S.op('dve', I('scalar_tensor_tensor', out=st[:, 4, 0:ne], in0=st[:, 1, 0:ne], scalar=inv, in1=st[:, 3, 0:ne], op0=ALU.mult, op1=ALU.subtract),
                         rd=[('rst', 1), ('rst', 3)], wr=[('rst', 4)])
                    S.op('act', I('activation', out=st[:, 5, 0:ne], in_=st[:, 4, 0:ne], func=AF.Sqrt, bias=cv[:, 1:2], scale=1.0), rd=[('rst', 4)] + CVK, wr=[('rst', 5)])
                    S.op('dve', I('reciprocal', out=st[:, 5, 0:ne], in_=st[:, 5, 0:ne]), rd=[('rst', 5)], wr=[('rst', 5)])
                    S.op('dve', I('scalar_tensor_tensor', out=st[:, 6, 0:ne], in0=st[:, 2, 0:ne], scalar=-1.0, in1=st[:, 5, 0:ne], op0=ALU.mult, op1=ALU.mult),
                         rd=[('rst', 2), ('rst', 5)], wr=[('rst', 6)])
                    for j in range(ne):
                        S.op('act', I('activation', out=oall[:, j, :], in_=oall[:, j, :], func=AF.Identity, bias=st[:, 6, j:j + 1], scale=st[:, 5, j:j + 1]),
                             rd=[('oall', j), ('rst', 5), ('rst', 6)], wr=[('oall', j)])
                    rbfv = qktok[:, ea:eb_].rearrange("p t c f -> p t (c f)")
                    S.op('dve', I('tensor_tensor', out=rbfv, in0=oall[:, 0:ne, :], in1=rgs[:, ea:eb_, :], op=ALU.mult),
                         rd=OK_ + [('rgs', t) for t in range(ea, eb_)] + QK_, wr=QK_)
                    for c in range(2):
                        pe_ = 1 + c
                        for t in range(ea, eb_):
                            S.op('pe', I('transpose', out=Pb[pe_][:, (t - ea) * 128:(t - ea + 1) * 128], in_=qktok[:, t, c, :], identity=identb[:]),
                                 rd=[('qktok', t, 0), ('qktok', t, 1), 'identb'], wr=[('P', pe_)], inc=(t == eb_ - 1))
                        S.op('act', I('activation', out=rT[:, 2 * h + c, ea * 128:eb_ * 128], in_=Pb[pe_][:, 0:ne * 128], func=AF.Identity,
                                      scale=sm('retgn', 2 * h + c, 2 * h + c + 1)),
                             rd=[('P', pe_), 'smalls'], wr=[('rT', 2 * h + c, 0), ('rT', 2 * h + c, 1), ('rT', 2 * h + c, 2)])
            proj_r(0)
            decay_tables()
            for h in range(4):
                rest_r(h)
                if h + 1 < 4:
                    proj_r(h + 1)
            _barrier(S)
        with ExitStack() as ph:
            def sbt(name, shape, dt):
                return ph.enter_context(nc.sbuf_tensor(name, shape, dt))
            wloc = sbt("m_wloc", [128, NT, 8], F32)
            wa2 = sbt("m_wa2", [128, NT, 8], F32)
            a1 = sbt("m_a1", [128, NT, 8], F32)
            a2 = sbt("m_a2", [128, NT, 8], F32)
            enm = sbt("m_enm", [128, NT, 8], F32)
            wint = sbt("m_wint", [128, NT, 8], F32)
            flr = sbt("m_flr", [128, NT, 8], F32)
            mfin = sbt("m_mfin", [128, 2, 8], F32)
            xpre = sbt("m_xpre", [128, T], F32)
            ycv = sbt("m_ycv", [128, T], F32)
            qmT = sbt("m_qmT", [128, T], BF16)
            kmT = sbt("m_kmT", [128, T], BF16)
            kmtok = sbt("m_kmtok", [128, NT, 128], BF16)
            vext = sbt("m_vext", [128, NT, 136], BF16)
            mos = sbt("m_mos", [128, 2, NT, 128], BF16)
            ndall = sbt("m_ndall", [128, NT, 2, 130], F32)
            Cst = sbt("m_Cst", [128, 2, 2, 130], F32)
            Cbf = sbt("m_Cbf", [128, 2, 2, 136], BF16)
            sTb = [sbt("m_sT%d" % i, [128, 128], BF16) for i in range(3)]
            eb = sbt("m_eb", [128, 1, NT, 2], F32)
            st = sbt("m_st", [128, 8, NT], F32)
            gp = ExitStack()

            def sbg(name, shape, dt):
                return gp.enter_context(nc.sbuf_tensor(name, shape, dt))
            ig = sbg("m_ig", [128, NT, 8], F32)
            lf = sbg("m_lf", [128, NT, 8], F32)
            btok = sbg("m_btok", [128, NT, 8], F32)
            tot = sbg("m_tot", [128, NT, 8], F32)
            atok = sbg("m_atok", [128, NT, 8], F32)
            amax = sbg("m_amax", [128, NT, 8], F32)
            ml = sbg("m_ml", [128, NT, 8], F32)
            mprev = sbg("m_mprev", [128, NT, 8], F32)
            mnew = sbg("m_mnew", [128, NT, 8], F32)
            pmtok = sbg("m_pmtok", [128, NT, 8], F32)
            mx = sbg("m_mx", [128, NT, 8], F32)
            aT = sbg("m_aT", [128, 128], F32)
            pfa = sbg("m_pfa", [128, 128], F32)
            pfb = sbg("m_pfb", [128, 128], F32)
            tmp4 = sbg("m_tmp4", [128, 4], F32)
            amaxc = sbg("m_amaxc", [128, 1], F32)
            diagA = sbg("m_diagA", [128, 96], F32)

            def f2(a):
                return a[:].rearrange("p t c -> p (t c)")
            GK = ['gates']

            def gate_prep():
                slG = ring_slot()
                vG = ring[slG][:, 0:128].rearrange("p (k n) -> p k n", k=8)
                S.dma('pool', vG, w_in[:, :, 5120:5136], wr=[('ring', slG)])
                for t in range(NT):
                    for kc in range(8):
                        S.op('pe', I('matmul', out=P[0][:, t * 16:(t + 1) * 16], lhsT=uT[:, kc, t * 128:(t + 1) * 128], rhs=vG[:, kc, :],
                                     start=(kc == 0), stop=(kc == 7)), rd=[('ring', slG), ('u', kc, t // 4)], wr=[('P', 0)], inc=(kc == 7 and t == NT - 1))
                G3 = P[0][:, 0:192].rearrange("p (t c) -> p t c", c=16)
                S.op('dve', I('tensor_tensor', out=ig[:], in0=G3[:, :, 0:8], in1=sm('bi').unsqueeze(1).to_broadcast([128, NT, 8]), op=ALU.add),
                     rd=[('P', 0), 'smalls'], wr=GK)
                S.op('dve', I('tensor_tensor', out=lf[:], in0=G3[:, :, 8:16], in1=sm('bf').unsqueeze(1).to_broadcast([128, NT, 8]), op=ALU.add),
                     rd=[('P', 0), 'smalls'], wr=GK)
                S.op('act', I('activation', out=f2(lf), in_=f2(lf), func=AF.Exp, scale=-1.0), rd=GK, wr=GK)
                S.op('act', I('activation', out=f2(lf), in_=f2(lf), func=AF.Ln, bias=cv[:, 2:3], scale=1.0), rd=GK + CVK, wr=GK)
                S.op('dve', I('tensor_scalar', out=f2(lf), in0=f2(lf), scalar1=-1.0, scalar2=None, op0=ALU.mult), rd=GK, wr=GK)
                S.op('pe', I('matmul', out=P[1][:, 0:96], lhsT=Um, rhs=f2(lf), start=True, stop=True), rd=GK + ['consts'], wr=[('P', 1)])
                S.op('pe', I('matmul', out=P[1][:, 96:192], lhsT=Lm, rhs=f2(lf), start=True, stop=True), rd=GK + ['consts'], wr=[('P', 1)])
                S.op('pe', I('matmul', out=P[1][:, 192:288], lhsT=ones1[:], rhs=f2(lf), start=True, stop=True), rd=GK + ['ones1'], wr=[('P', 1)])
                cF = P[1][:, 0:96].rearrange("p (t c) -> p t c", c=8)
                cB = P[1][:, 96:192].rearrange("p (t c) -> p t c", c=8)
                S.op('dve', I('tensor_copy', out=btok[:, :, 0:4], in_=cF[:, :, 0:4]), rd=[('P', 1)], wr=GK)
                S.op('dve', I('tensor_copy', out=btok[:, :, 4:8], in_=cB[:, :, 4:8]), rd=[('P', 1)], wr=GK)
                S.op('dve', I('tensor_copy', out=f2(tot), in_=P[1][:, 192:288]), rd=[('P', 1)], wr=GK)
                S.op('dve', I('tensor_tensor', out=atok[:], in0=ig[:], in1=btok[:], op=ALU.subtract), rd=GK, wr=GK)
                S.op('pe', I('transpose', out=P[2][0:96, 0:128], in_=f2(atok), identity=ident), rd=GK + ['consts'], wr=[('P', 2)])
                S.op('dve', I('tensor_reduce', out=amaxc[0:96, :], in_=P[2][0:96, 0:128], axis=AX.X, op=ALU.max), rd=[('P', 2)], wr=GK)
                S.op('dve', I('tensor_scalar', out=diagA[0:96, :], in0=consts[0:96, 0:96], scalar1=amaxc[0:96, 0:1], scalar2=None, op0=ALU.mult),
                     rd=GK + ['consts'], wr=GK)
                S.op('pe', I('matmul', out=P[2][:, 128:224], lhsT=ones1[0:96, :], rhs=diagA[0:96, :], start=True, stop=True),
                     rd=GK + ['ones1'], wr=[('P', 2)])
                S.op('dve', I('tensor_copy', out=aT[0:96, :], in_=P[2][0:96, 0:128]), rd=[('P', 2)], wr=['aT'])
                for dirn in range(2):
                    cur, ck = aT, 'aT'
                    for si_, sh in enumerate([1, 2, 4, 8, 16, 32, 64]):
                        nxt, nk = (pfa, 'pfa') if si_ % 2 == 0 else (pfb, 'pfb')
                        if dirn == 0:
                            S.op('dve', I('tensor_tensor', out=nxt[0:96, sh:128], in0=cur[0:96, sh:128], in1=cur[0:96, 0:128 - sh], op=ALU.max), rd=[ck], wr=[nk])
                            S.op('dve', I('tensor_copy', out=nxt[0:96, 0:sh], in_=cur[0:96, 0:sh]), rd=[ck], wr=[nk])
                        else:
                            S.op('dve', I('tensor_tensor', out=nxt[0:96, 0:128 - sh], in0=cur[0:96, 0:128 - sh], in1=cur[0:96, sh:128], op=ALU.max), rd=[ck], wr=[nk])
                            S.op('dve', I('tensor_copy', out=nxt[0:96, 128 - sh:128], in_=cur[0:96, 128 - sh:128]), rd=[ck], wr=[nk])
                        cur, ck = nxt, nk
                    S.op('pe', I('transpose', out=P[1][:, 288 + dirn * 96:288 + (dirn + 1) * 96], in_=cur[0:96, :], identity=consts[0:96, 0:96]),
                         rd=[ck, 'consts'], wr=[('P', 1)])
                    pv_ = P[1][:, 288 + dirn * 96:288 + (dirn + 1) * 96].rearrange("p (t c) -> p t c", c=8)
                    S.op('dve', I('tensor_copy', out=pmtok[:, :, dirn * 4:dirn * 4 + 4], in_=pv_[:, :, dirn * 4:dirn * 4 + 4]), rd=[('P', 1)], wr=GK)
                S.op('dve', I('tensor_copy', out=f2(amax), in_=P[2][:, 128:224]), rd=[('P', 2)], wr=GK)
                S.op('dve', I('tensor_tensor', out=ml[:], in0=tot[:], in1=amax[:], op=ALU.add), rd=GK, wr=GK)
                S.op('dve', I('tensor_tensor', out=wloc[:], in0=atok[:], in1=amax[:], op=ALU.subtract), rd=GK, wr=GK)
                S.op('act', I('activation', out=f2(wloc), in_=f2(wloc), func=AF.Exp), rd=GK, wr=GK)
                S.op('dve', I('tensor_scalar', out=f2(wloc), in0=f2(wloc), scalar1=SCK, scalar2=None, op0=ALU.mult), rd=GK, wr=GK)
                for si, (t0, N, samp) in enumerate(SEQS):
                    for d in range(2):
                        cs = slice(d * 4, d * 4 + 4)
                        order = list(range(N)) if d == 0 else list(reversed(range(N)))
                        for oi, n in enumerate(order):
                            t = t0 + n
                            if oi == 0:
                                if samp:
                                    S.op('dve', I('tensor_copy', out=mprev[:, t, cs], in_=sm('m0', d * 4, d * 4 + 4)), rd=GK + ['smalls'], wr=GK)
                                else:
                                    S.op('dve', I('memset', ap=mprev[:, t, cs], constant=0.0), rd=GK, wr=GK)
                            S.op('dve', I('tensor_tensor', out=tmp4[:], in0=tot[:, t, cs], in1=mprev[:, t, cs], op=ALU.add), rd=GK, wr=GK)
                            S.op('dve', I('tensor_tensor', out=mnew[:, t, cs], in0=tmp4[:], in1=ml[:, t, cs], op=ALU.max), rd=GK, wr=GK)
                            if oi + 1 < N:
                                S.op('dve', I('tensor_copy', out=mprev[:, t0 + order[oi + 1], cs], in_=mnew[:, t, cs]), rd=GK, wr=GK)
                            elif not samp:
                                S.op('dve', I('tensor_copy', out=mfin[:, si - 1, cs], in_=mnew[:, t, cs]), rd=GK, wr=GK)
                S.op('dve', I('tensor_tensor', out=a1[:], in0=tot[:], in1=mprev[:], op=ALU.add), rd=GK, wr=GK)
                S.op('dve', I('tensor_tensor', out=a1[:], in0=a1[:], in1=mnew[:], op=ALU.subtract), rd=GK, wr=GK)
                S.op('act', I('activation', out=f2(a1), in_=f2(a1), func=AF.Exp), rd=GK, wr=GK)
                S.op('dve', I('tensor_tensor', out=a2[:], in0=ml[:], in1=mnew[:], op=ALU.subtract), rd=GK, wr=GK)
                S.op('act', I('activation', out=f2(a2), in_=f2(a2), func=AF.Exp), rd=GK, wr=GK)
                S.op('dve', I('tensor_tensor', out=mx[:], in0=pmtok[:], in1=mprev[:], op=ALU.max), rd=GK, wr=GK)
                S.op('dve', I('tensor_tensor', out=enm[:], in0=amax[:], in1=mx[:], op=ALU.subtract), rd=GK, wr=GK)
                S.op('act', I('activation', out=f2(enm), in_=f2(enm), func=AF.Exp), rd=GK, wr=GK)
                S.op('dve', I('tensor_tensor', out=wint[:], in0=mprev[:], in1=mx[:], op=ALU.subtract), rd=GK, wr=GK)
                S.op('act', I('activation', out=f2(wint), in_=f2(wint), func=AF.Exp), rd=GK, wr=GK)
                S.op('dve', I('tensor_tensor', out=flr[:], in0=btok[:], in1=mx[:], op=ALU.add), rd=GK, wr=GK)
                S.op('act', I('activation', out=f2(flr), in_=f2(flr), func=AF.Exp, scale=-1.0), rd=GK, wr=GK)
                S.op('dve', I('tensor_tensor', out=wa2[:], in0=wloc[:], in1=a2[:], op=ALU.mult), rd=GK, wr=GK)
                S.dma('sp', osm_d, mfin[0:1].rearrange("p a b -> p (a b)"), rd=GK, wr=['osm'])
                S.op('dve', I('memset', ap=vext[:, :, 128:129], constant=1.0), wr=['vone'])
                S.op('dve', I('memset', ap=vext[:, :, 129:130], constant=0.0), wr=['vzero'])


            SEGS = [(0, 1024), (1024, 1280), (1280, 1536)]
            def proj(h):
                slQ = ring_slot()
                vQ = ring[slQ][:, 0:2048].rearrange("p (k n) -> p k n", k=8)
                S.dma('pool', vQ[:, :, 0:128], w_in[:, :, 3072 + h * 128:3072 + (h + 1) * 128], wr=[('ring', slQ)])
                S.dma('pool', vQ[:, :, 128:256], w_in[:, :, 3584 + h * 128:3584 + (h + 1) * 128], wr=[('ring', slQ)])
                slV = ring_slot()
                vV = ring[slV][:, 0:2048].rearrange("p (k n) -> p k n", k=8)
                S.dma('pool', vV[:, :, 0:128], w_in[:, :, 4096 + h * 128:4096 + (h + 1) * 128], wr=[('ring', slV)])
                S.dma('pool', vV[:, :, 128:256], w_in[:, :, 4608 + h * 128:4608 + (h + 1) * 128], wr=[('ring', slV)])
                for c in range(2):
                    for g in range(3):
                        gs = slice(g * 512, (g + 1) * 512)
                        for kc in range(8):
                            S.op('pe', I('matmul', out=P[g % 2][:], lhsT=vQ[:, kc, c * 128:(c + 1) * 128], rhs=uT[:, kc, gs], start=(kc == 0), stop=(kc == 7)),
                                 rd=[('ring', slQ), ('u', kc, g)], wr=[('P', g % 2)], inc=(kc == 7))
                        copy_op('act' if g % 2 == 0 else 'dve', xpre[:, gs], P[g % 2][:], [('P', g % 2)], ['xpre'])
                    ch = c * 4 + h
                    cw = lambda j: sm('conv', ch * 4 + j, ch * 4 + j + 1)
                    for (s0, e0) in SEGS:
                        S.op('act', I('activation', out=ycv[:, s0:e0], in_=xpre[:, s0:e0], func=AF.Identity, bias=cw(3), scale=cw(1)),
                             rd=['xpre', 'smalls'], wr=['ycv'])
                        S.op('dve', I('scalar_tensor_tensor', out=ycv[:, s0 + 1:e0], in0=xpre[:, s0:e0 - 1], scalar=cw(0), in1=ycv[:, s0 + 1:e0],
                                      op0=ALU.mult, op1=ALU.add), rd=['xpre', 'ycv', 'smalls'], wr=['ycv'])
                        S.op('dve', I('scalar_tensor_tensor', out=ycv[:, s0:e0 - 1], in0=xpre[:, s0 + 1:e0], scalar=cw(2), in1=ycv[:, s0:e0 - 1],
                                      op0=ALU.mult, op1=ALU.add), rd=['xpre', 'ycv', 'smalls'], wr=['ycv'])
                    if c == 0:
                        S.op('act', I('activation', out=qmT[:], in_=ycv[:], func=AF.Silu), rd=['ycv'], wr=['qmT'])
                    else:
                        S.op('act', I('activation', out=kmT[:], in_=ycv[:], func=AF.Silu), rd=['ycv'], wr=['kmT'])
                for (ta_, tb_) in [(0, 8), (8, 12)]:
                    for t in range(ta_, tb_):
                        S.op('pe', I('transpose', out=Pb[2][:, (t - ta_) * 128:(t - ta_ + 1) * 128], in_=kmT[:, t * 128:(t + 1) * 128], identity=identb[:]),
                             rd=['kmT', 'identb'], wr=[('P', 2)], inc=(t == tb_ - 1))
                    S.op('dve', I('tensor_copy', out=kmtok[:, ta_:tb_, :], in_=Pb[2][:, 0:(tb_ - ta_) * 128].rearrange("p (a b) -> p a b", b=128)),
                         rd=[('P', 2)], wr=['kmtok'])
                for tp in range(NT // 2):
                    pvo = [3, 7, 4, 5][tp % 4]
                    for tt_ in range(2):
                        t = tp * 2 + tt_
                        for kc in range(8):
                            S.op('pe', I('matmul', out=P[pvo][:, tt_ * 256:(tt_ + 1) * 256], lhsT=uT[:, kc, t * 128:(t + 1) * 128], rhs=vV[:, kc, :], start=(kc == 0), stop=(kc == 7)),
                                 rd=[('ring', slV), ('u', kc, t // 4)], wr=[('P', pvo)], inc=(kc == 7 and tt_ == 1))
                    pv4 = P[pvo][:].rearrange("p (a b c) -> p a b c", a=2, b=2)
                    S.op('act', I('activation', out=vext[:, tp * 2:tp * 2 + 2, 0:128], in_=pv4[:, :, 0, :], func=AF.Copy), rd=[('P', pvo)],
                         wr=[('vext', tp * 2), ('vext', tp * 2 + 1)])
                    S.op('act', I('activation', out=mos[:, h % 2, tp * 2:tp * 2 + 2, :], in_=pv4[:, :, 1, :], func=AF.Sigmoid), rd=[('P', pvo)],
                         wr=[('mos', h % 2, tp * 2), ('mos', h % 2, tp * 2 + 1)])

            def loops(h):
                for si, (t0, N, samp) in enumerate(SEQS):
                    p_ = si - 1
                    cver = [0, 0]
                    for d in (1, 0):
                        dh = d * 4 + h
                        if samp:
                            S.op('dve', I('memset', ap=Cst[:, d, 0, 129:130], constant=0.0), wr=[('C', d, 0)])
                            S.dma('sp', Cst[:, d, 0, 0:129], smc_d[:, dh, :], rd=[('C', d, 0)], wr=[('C', d, 0)])
                        else:
                            S.op('dve', I('memset', ap=Cst[:, d, 0, :], constant=0.0), wr=[('C', d, 0)])
                        S.op('act', I('activation', out=Cbf[:, d, 0, 0:130], in_=Cst[:, d, 0, :], func=AF.Copy), rd=[('C', d, 0)], wr=[('Cbf', d, 0)])
                    ordB = [(1, n) for n in reversed(range(N))]
                    ordF = [(0, n) for n in range(N)]
                    steps = [x for pr in zip(ordB, ordF) for x in pr]
                    ns = len(steps)
                    for j, (d, n) in enumerate(steps):
                        t = t0 + n
                        dh = d * 4 + h
                        ts_ = slice(t * 128, (t + 1) * 128)
                        s3 = j % 3
                        s2 = j % 2
                        maskm = Lm if d == 1 else Um
                        qb_ = [0, 1, 4][s3]
                        ib_ = [2, 3, 5][s3]
                        S.op('pe', I('matmul', out=P[qb_][:, 0:128], lhsT=kmT[:, ts_], rhs=qmT[:, ts_], start=True, stop=True),
                             rd=['kmT', 'qmT'], wr=[('P', qb_)])
                        S.op('dve', I('scalar_tensor_tensor', out=sTb[s3][:], in0=P[qb_][:, 0:128], scalar=wloc[:, t, dh:dh + 1], in1=maskm,
                                      op0=ALU.mult, op1=ALU.mult), rd=[('P', qb_), 'consts'] + GK, wr=[('sTb', s3)])
                        S.op('pe', I('matmul', out=P[ib_][:, 0:130], lhsT=sTb[s3][:], rhs=vext[:, t, 0:130], start=True, stop=True),
                             rd=[('sTb', s3), ('vext', t), 'vone', 'vzero'], wr=[('P', ib_)])
                        S.op('act', I('activation', out=ndall[:, t, d, :], in_=P[ib_][:, 0:130], func=AF.Identity, scale=enm[:, t, dh:dh + 1]),
                             rd=[('P', ib_)] + GK, wr=[('nd', t, d)])
                        S.op('dve', I('tensor_scalar', out=wvst[:, j, 0:130], in0=vext[:, t, 0:130], scalar1=wa2[:, t, dh:dh + 1], scalar2=None, op0=ALU.mult),
                             rd=[('vext', t), 'vone', 'vzero'] + GK, wr=[('wvst', j)])

                    def pe_cloc(j):
                        d, n = steps[j]
                        S.op('pe', I('matmul', out=P[4 + j % 2][:, 0:130], lhsT=kmtok[:, t0 + n, :], rhs=wvst[:, j, 0:130], start=True, stop=True),
                             rd=['kmtok', ('wvst', j)], wr=[('P', 4 + j % 2)])

                    CRB = [6, 7, 1]

                    def pe_cross(j):
                        d, n = steps[j]
                        t = t0 + n
                        cb_ = CRB[j % 3]
                        S.op('pe', I('matmul', out=P[cb_][:, 0:130], lhsT=qmT[:, t * 128:(t + 1) * 128], rhs=Cbf[:, d, cver[d], 0:130], start=True, stop=True),
                             rd=['qmT', ('Cbf', d, cver[d])], wr=[('P', cb_)])

                    pe_cloc(0)
                    pe_cross(0)
                    if ns > 1:
                        pe_cross(1)
                    for j, (d, n) in enumerate(steps):
                        t = t0 + n
                        dh = d * 4 + h
                        if j + 1 < ns:
                            pe_cloc(j + 1)
                        cu = cver[d]
                        S.op('dve', I('scalar_tensor_tensor', out=Cst[:, d, 1 - cu, :], in0=Cst[:, d, cu, :], scalar=a1[:, t, dh:dh + 1], in1=P[4 + j % 2][:, 0:130],
                                      op0=ALU.mult, op1=ALU.add), rd=[('P', 4 + j % 2), ('C', d, cu)] + GK, wr=[('C', d, 1 - cu)])
                        S.op('act', I('activation', out=Cbf[:, d, 1 - cu, 0:130], in_=Cst[:, d, 1 - cu, :], func=AF.Copy), rd=[('C', d, 1 - cu)], wr=[('Cbf', d, 1 - cu)])
                        cver[d] = 1 - cu
                        if j + 2 < ns:
                            pe_cross(j + 2)
                        cb_ = CRB[j % 3]
                        S.op('dve', I('scalar_tensor_tensor', out=ndall[:, t, d, :], in0=P[cb_][:, 0:130], scalar=wint[:, t, dh:dh + 1], in1=ndall[:, t, d, :],
                                      op0=ALU.mult, op1=ALU.add), rd=[('P', cb_), ('nd', t, d)] + GK, wr=[('nd', t, d)])
                    if not samp:
                        for d in range(2):
                            S.dma('sp', osc_d[:, p_ * 8 + d * 4 + h, :], Cst[:, d, cver[d], 0:129], rd=[('C', d, cver[d])], wr=[('osc', p_, d * 4 + h)])

            def epilogue(h):
                junk = ndall[:, :, 1, 0:128]
                hmf = ndall[:, :, 0, 0:128]
                hbf = wvst[:, 0:NT, 0:128]
                WVK = [('wvst', j) for j in range(16)]
                NDK = [('nd', t, d) for t in range(NT) for d in range(2)]
                inv = 1.0 / 128.0
                den = ndall[:, :, :, 128]
                S.op('dve', I('scalar_tensor_tensor', out=eb[:, 0], in0=den, scalar=-1.0, in1=den, op0=ALU.mult, op1=ALU.max), rd=NDK, wr=[('eb', 0)])
                fl2 = flr[:].rearrange("p t (d q) -> p t d q", d=2)[:, :, :, h]
                S.op('dve', I('tensor_tensor', out=eb[:, 0], in0=eb[:, 0], in1=fl2, op=ALU.max), rd=[('eb', 0)] + GK, wr=[('eb', 0)])
                S.op('dve', I('reciprocal', out=eb[:, 0], in_=eb[:, 0]), rd=[('eb', 0)], wr=[('eb', 0)])
                for d in range(2):
                    S.op('dve', I('tensor_tensor', out=ndall[:, :, d, 0:128], in0=ndall[:, :, d, 0:128],
                                  in1=eb[:, 0, :, d].unsqueeze(2).to_broadcast([128, NT, 128]), op=ALU.mult), rd=NDK + [('eb', 0)], wr=NDK)
                S.op('dve', I('tensor_tensor', out=hmf, in0=ndall[:, :, 0, 0:128], in1=ndall[:, :, 1, 0:128], op=ALU.add), rd=NDK, wr=NDK)
                S.op('dve', I('tensor_tensor', out=hmf, in0=hmf, in1=mos[:, h % 2], op=ALU.mult), rd=NDK + [('mos', h % 2, t) for t in range(NT)], wr=NDK)
                S.op('dve', I('tensor_reduce', out=st[:, 0, :], in_=hmf, axis=AX.X, op=ALU.add), rd=NDK, wr=[('mst', 0)])
                S.op('act', I('activation', out=junk, in_=hmf, func=AF.Square), rd=NDK, wr=NDK)
                S.op('dve', I('tensor_reduce', out=st[:, 1, :], in_=junk, axis=AX.X, op=ALU.add), rd=NDK, wr=[('mst', 1)])
                S.op('dve', I('tensor_scalar', out=st[:, 2, :], in0=st[:, 0, :], scalar1=inv, scalar2=None, op0=ALU.mult), rd=[('mst', 0)], wr=[('mst', 2)])
                S.op('dve', I('tensor_tensor', out=st[:, 3, :], in0=st[:, 2, :], in1=st[:, 2, :], op=ALU.mult), rd=[('mst', 2)], wr=[('mst', 3)])
                S.op('dve', I('scalar_tensor_tensor', out=st[:, 4, :], in0=st[:, 1, :], scalar=inv, in1=st[:, 3, :], op0=ALU.mult, op1=ALU.subtract),
                     rd=[('mst', 1), ('mst', 3)], wr=[('mst', 4)])
                S.op('act', I('activation', out=st[:, 5, :], in_=st[:, 4, :], func=AF.Sqrt, bias=cv[:, 1:2], scale=1.0), rd=[('mst', 4)] + CVK, wr=[('mst', 5)])
                S.op('dve', I('reciprocal', out=st[:, 5, :], in_=st[:, 5, :]), rd=[('mst', 5)], wr=[('mst', 5)])
                S.op('dve', I('scalar_tensor_tensor', out=st[:, 6, :], in0=st[:, 2, :], scalar=-1.0, in1=st[:, 5, :], op0=ALU.mult, op1=ALU.mult),
                     rd=[('mst', 2), ('mst', 5)], wr=[('mst', 6)])
                for t in range(NT):
                    S.op('act', I('activation', out=hbf[:, t, :], in_=hmf[:, t, :], func=AF.Identity, bias=st[:, 6, t:t + 1], scale=st[:, 5, t:t + 1]),
                         rd=NDK + [('mst', 5), ('mst', 6)], wr=WVK)
                for bi_, (ta_, tb_) in enumerate([(0, 8), (8, 12)]):
                    pe_ = bi_
                    for t in range(ta_, tb_):
                        S.op('pe', I('transpose', out=Pb[pe_][:, (t - ta_) * 128:(t - ta_ + 1) * 128], in_=hbf[:, t, :], identity=identb[:]),
                             rd=WVK + ['identb'], wr=[('P', pe_)], inc=(t == tb_ - 1))
                    S.op('act', I('activation', out=hmT[:, h, ta_ * 128:tb_ * 128], in_=Pb[pe_][:, 0:(tb_ - ta_) * 128], func=AF.Identity,
                                  scale=sm('mgn', h, h + 1)), rd=[('P', pe_), 'smalls'], wr=[('hmT', h, 0), ('hmT', h, 1), ('hmT', h, 2)])
            proj(0)
            gate_prep()
            _barrier(S)
            gp.close()
            wvst = sbt("m_wvst", [128, 16, 136], BF16)
            for h in range(4):
                loops(h)
                if h + 1 < 4:
                    proj(h + 1)
                epilogue(h)
            _barrier(S)

        with ExitStack() as ph:
            merged = ph.enter_context(nc.sbuf_tensor("merged", [128, 8, T], BF16))
            s0t = [ph.enter_context(nc.sbuf_tensor("s0t%d" % i, [128, 512], F32)) for i in range(2)]
            s1t = [ph.enter_context(nc.sbuf_tensor("s1t%d" % i, [128, 512], F32)) for i in range(2)]
            k = 0
            for dc in range(8):
                sl = ring_slot()
                vru = ring[sl][:, 0:1024].rearrange("p (k n) -> p k n", k=8)
                vmu = ring[sl][:, 1024:1536].rearrange("p (k n) -> p k n", k=4)
                vb0 = ring[sl][:, 1536:2560].rearrange("p (k n) -> p k n", k=8)
                vb1 = ring[sl][:, 2560:3584].rearrange("p (k n) -> p k n", k=8)
                cs_ = slice(dc * 128, (dc + 1) * 128)
                S.dma('pool', vru, w_ru[:, :, cs_], wr=[('ring', sl)])
                S.dma('pool', vmu, w_mu[:, :, cs_], wr=[('ring', sl)])
                S.dma('pool', vb0, w_in[:, :, 5136 + dc * 128:5136 + (dc + 1) * 128], wr=[('ring', sl)])
                S.dma('pool', vb1, w_in[:, :, 6160 + dc * 128:6160 + (dc + 1) * 128], wr=[('ring', sl)])
                for g in range(3):
                    gs = slice(g * 512, (g + 1) * 512)
                    pb = 4 * (k % 2)
                    s = k % 2
                    k += 1
                    for kc in range(8):
                        S.op('pe', I('matmul', out=P[pb][:], lhsT=vru[:, kc, :], rhs=rT[:, kc, gs], start=(kc == 0), stop=(kc == 7)),
                             rd=[('ring', sl), ('rT', kc, g)], wr=[('P', pb)], inc=(kc == 7))
                    for kc in range(4):
                        S.op('pe', I('matmul', out=P[pb + 1][:], lhsT=vmu[:, kc, :], rhs=hmT[:, kc, gs], start=(kc == 0), stop=(kc == 3)),
                             rd=[('ring', sl), ('hmT', kc, g)], wr=[('P', pb + 1)], inc=(kc == 3))
                    for kc in range(8):
                        S.op('pe', I('matmul', out=P[pb + 2][:], lhsT=vb0[:, kc, :], rhs=uT[:, kc, gs], start=(kc == 0), stop=(kc == 7)),
                             rd=[('ring', sl), ('u', kc, g)], wr=[('P', pb + 2)], inc=(kc == 7))
                    for kc in range(8):
                        S.op('pe', I('matmul', out=P[pb + 3][:], lhsT=vb1[:, kc, :], rhs=uT[:, kc, gs], start=(kc == 0), stop=(kc == 7)),
                             rd=[('ring', sl), ('u', kc, g)], wr=[('P', pb + 3)], inc=(kc == 7))
                    S.op('act', I('activation', out=s0t[s][:], in_=P[pb + 2][:], func=AF.Sigmoid), rd=[('P', pb + 2)], wr=[('s0t', s)])
                    S.op('act', I('activation', out=s1t[s][:], in_=P[pb + 3][:], func=AF.Sigmoid), rd=[('P', pb + 3)], wr=[('s1t', s)])
                    S.op('dve', I('tensor_tensor', out=s0t[s][:], in0=s0t[s][:], in1=P[pb][:], op=ALU.mult), rd=[('s0t', s), ('P', pb)], wr=[('s0t', s)])
                    S.op('dve', I('tensor_tensor', out=s1t[s][:], in0=s1t[s][:], in1=P[pb + 1][:], op=ALU.mult), rd=[('s1t', s), ('P', pb + 1)], wr=[('s1t', s)])
                    S.op('dve', I('tensor_tensor', out=merged[:, dc, gs], in0=s0t[s][:], in1=s1t[s][:], op=ALU.add),
                         rd=[('s0t', s), ('s1t', s)], wr=[('mg', dc, g)])
            pstat_m = []
            accm = [ph.enter_context(nc.sbuf_tensor("accm%d" % i, [128, 512], F32)) for i in range(3)]
            for dc in range(8):
                sl = ring_slot()
                vo = ring[sl][:, 0:1024].rearrange("p (k n) -> p k n", k=8)
                S.dma('pool', vo, w_o[:, :, dc * 128:(dc + 1) * 128], wr=[('ring', sl)])
                for g in range(3):
                    gs = slice(g * 512, (g + 1) * 512)
                    st_ = 0 if g < 2 else 1
                    pi = k % 2
                    k += 1
                    for kc in range(8):
                        S.op('pe', I('matmul', out=P[pi][:], lhsT=vo[:, kc, :], rhs=merged[:, kc, gs], start=(kc == 0), stop=(kc == 7)),
                             rd=[('ring', sl), ('mg', kc, g)], wr=[('P', pi)], inc=(kc == 7))
                    S.op('dve', I('scalar_tensor_tensor', out=xT[:, dc, gs], in0=P[pi][:], scalar=modG[:, 1, dc, st_:st_ + 1], in1=xT[:, dc, gs],
                                  op0=ALU.mult, op1=ALU.add), rd=[('P', pi), ('x', dc, g)] + MODK, wr=[('x', dc, g)])
                    pstat_m.append((dc, g))
                    if len(pstat_m) > 3:
                        emit_stat(s0t, *pstat_m.pop(0), (5, 6, 7), accm)
            while pstat_m:
                emit_stat(s0t, *pstat_m.pop(0), (5, 6, 7), accm)
            _barrier(S)
        mxs.close()

        with ExitStack() as ph:
            norm_phase(ph, 2, banks=(5, 6, 7), pre=True)
            ffn_phase(ph, 2, w1b, w3b, w2b, stat_banks=(2, 3, 6))
            _barrier(S)

        with ExitStack() as ph:
            yT = [ph.enter_context(nc.sbuf_tensor("yT%d" % i, [128, 8, 512], F32)) for i in range(2)]
            yo = [ph.enter_context(nc.sbuf_tensor("yo%d" % i, [128, D], F32)) for i in range(3)]
            cnt = [0]
            pend_tr = []

            def trans_group(g):
                yb = yT[g % 2]
                for tt in range(4):
                    t = g * 4 + tt
                    o = yo[cnt[0] % 3]
                    ok = ('yo', cnt[0] % 3)
                    pb0 = 0 if cnt[0] % 2 == 0 else 4
                    cnt[0] += 1
                    for half in range(2):
                        pbk = pb0 + half
                        for q in range(4):
                            d2 = half * 4 + q
                            S.op('pe', I('transpose', out=P[pbk][:, q * 128:(q + 1) * 128],
                                         in_=yb[:, d2, tt * 128:(tt + 1) * 128], identity=ident),
                                 rd=[('yT', g % 2, d2), 'consts'], wr=[('P', pbk)], inc=(q == 3))
                        if half == 0:
                            S.op('dve', I('tensor_copy', out=o[:, 0:512], in_=P[pbk][:]), rd=[('P', pbk)], wr=[ok + (0,)])
                        else:
                            S.op('act', I('activation', out=o[:, 512:1024], in_=P[pbk][:], func=AF.Copy), rd=[('P', pbk)], wr=[ok + (1,)])
                    S.dma('sp', y_d[t * 128:(t + 1) * 128, :], o[:], rd=[ok + (0,), ok + (1,)], wr=[('y', t)])

            def fin_cb(g, dc, tmb, tmk):
                S.op('act', I('activation', out=yT[g % 2][:, dc, :], in_=tmb, func=AF.Copy, scale=sm('gf', dc, dc + 1)),
                     rd=[tmk, 'smalls'], wr=[('yT', g % 2, dc)])
                if dc == 7:
                    pend_tr.append(g)
                    if len(pend_tr) > 1:
                        trans_group(pend_tr.pop(0))

            norm_phase(ph, 3, final_cb=fin_cb, banks=(2, 3, 6), pre=True)
            while pend_tr:
                trans_group(pend_tr.pop(0))
            outs = [('y', t) for t in range(NT)] + [('dbgo', k) for k in dbg_outs] + ['osm'] + [('osr', p, d, h) for p in range(2) for d in range(2) for h in range(4)] + [('osc', p, dh) for p in range(2) for dh in range(8)]
            S.finish(outs)
    return nc, list(dbg_outs.keys())


def _consts():
    a = np.arange(128)
    ident = np.eye(128, dtype=np.float32)
    U = (a[None, :] >= a[:, None]).astype(np.float32)
    L = (a[None, :] <= a[:, None]).astype(np.float32)
    negU = np.where(U > 0, 0.0, NEG).astype(np.float32)
    negL = np.where(L > 0, 0.0, NEG).astype(np.float32)
    iota_row = np.broadcast_to(a[None, :].astype(np.float32), (128, 128))
    diff = (a[None, :] - a[:, None]).astype(np.float32)
    Lq = 1024
    r = np.repeat(np.arange(Lq // 64, dtype=np.float32), 64)
    col = (np.arange(Lq) % 64).astype(np.float32)
    n_f = 32
    freqs = (np.float32(10000.0) ** (-np.arange(n_f, dtype=np.float32) / np.float32(n_f))).astype(np.float32)
    ang = np.concatenate([r[:, None] * freqs, col[:, None] * freqs], axis=-1).astype(np.float32)
    cos = np.cos(ang).astype(np.float32).reshape(8, 128, 64).transpose(1, 0, 2).reshape(128, 512)
    sin = np.sin(ang).astype(np.float32).reshape(8, 128, 64).transpose(1, 0, 2).reshape(128, 512)
    return np.ascontiguousarray(np.concatenate([ident, U, L, negU, negL, iota_row, diff, cos, sin], axis=1), dtype=np.float32)


def _pp(v, nch):
    return np.ascontiguousarray(np.asarray(v, dtype=np.float32).reshape(nch, 128).T)


def _bc(v):
    v = np.asarray(v, dtype=np.float32).reshape(-1)
    return np.ascontiguousarray(np.broadcast_to(v[None, :], (128, v.size)))


_PROG = {}


def kernel(**inputs):
    f = {k: np.asarray(v) for k, v in inputs.items()}
    if 'prog' not in _PROG:
        _PROG['prog'] = build_program()
    nc, _ = _PROG['prog']
    consts = _consts()
    shared = {
        "consts": consts,
        "w_ada": np.ascontiguousarray(f['w_ada'][0]),
        "w1_ffn1": np.ascontiguousarray(f['w1_ffn1'][0]), "w3_ffn1": np.ascontiguousarray(f['w3_ffn1'][0]),
        "w2_ffn1": np.ascontiguousarray(f['w2_ffn1'][0]),
        "w1_ffn2": np.ascontiguousarray(f['w1_ffn2'][0]), "w3_ffn2": np.ascontiguousarray(f['w3_ffn2'][0]),
        "w2_ffn2": np.ascontiguousarray(f['w2_ffn2'][0]),
        "w_in": np.ascontiguousarray(f['w_in'][0]), "w_ret_up": np.ascontiguousarray(f['w_ret_up'][0]),
        "w_m_up": np.ascontiguousarray(f['w_m_up'][0]), "w_out": np.ascontiguousarray(f['w_out'][0]),
    }
    conv = np.concatenate([f['conv_w'][0], f['conv_b'][0][None, :]], axis=0)
    conv_pp = np.ascontiguousarray(conv.reshape(4, 8, 128).transpose(2, 1, 0).reshape(128, 32))
    in_maps = []
    for b in range(8):
        x = np.concatenate([f['x_sample'][b], f['x_prompt'][2 * b], f['x_prompt'][2 * b + 1]], axis=0)
        cT = np.stack([_pp(f['c'][b], 8), _pp(f['c_ctx'], 8)], axis=2).reshape(128, 16)
        smalls = np.concatenate([
            cT, _pp(f['b_ada'][0], 72), _pp(f['norm_ffn1'][0], 8), _pp(f['norm_mix'][0], 8), _pp(f['norm_ffn2'][0], 8),
            _pp(f['norm_final'], 8), conv_pp, _bc(f['ret_decay_logit'][0]), _bc(f['b_igate'][0]), _bc(f['b_fgate'][0]),
            _pp(f['ret_gn'][0], 8), _pp(f['m_gn'][0], 4), _bc(f['state_mlstm_m'][b, 0])], axis=1)
        sret = f['state_ret'][b, 0].reshape(8, 128, 256).transpose(1, 0, 2)
        smc = np.concatenate([f['state_mlstm_C'][b, 0].reshape(8, 128, 128).transpose(2, 0, 1),
                              f['state_mlstm_n'][b, 0].reshape(8, 128).T[:, :, None]], axis=2)
        m = dict(shared)
        m.update({"x": np.ascontiguousarray(x, dtype=np.float32), "smalls": np.ascontiguousarray(smalls, dtype=np.float32),
                  "sret": np.ascontiguousarray(sret, dtype=np.float32), "smc": np.ascontiguousarray(smc, dtype=np.float32)})
        in_maps.append(m)
    res = run_bass_kernel_spmd(nc, in_maps, core_ids=list(range(8)))
    _PROG['last'] = res
    y_s = np.zeros((8, 1024, 1024), np.float32)
    y_p = np.zeros((16, 256, 1024), np.float32)
    n_sr = np.zeros((16, 1, 2, 4, 128, 256), np.float32)
    n_C = np.zeros((16, 1, 2, 4, 128, 128), np.float32)
    n_n = np.zeros((16, 1, 2, 4, 128), np.float32)
    n_m = np.zeros((16, 1, 2, 4), np.float32)
    for b in range(8):
        r = res.results[b]
        y = r["y"]
        y_s[b] = y[0:1024]
        y_p[2 * b] = y[1024:1280]
        y_p[2 * b + 1] = y[1280:1536]
        osr = r["o_sr"].reshape(128, 2, 2, 4, 256)
        osc = r["o_sc"].reshape(128, 2, 2, 4, 129)
        osm = r["o_sm"].reshape(2, 2, 4)
        for p in range(2):
            n_sr[2 * b + p, 0] = osr[:, p].transpose(1, 2, 0, 3)
            n_C[2 * b + p, 0] = osc[:, p, :, :, 0:128].transpose(1, 2, 3, 0)
            n_n[2 * b + p, 0] = osc[:, p, :, :, 128].transpose(1, 2, 0)
            n_m[2 * b + p, 0] = osm[p]
    return (y_p, y_s, n_sr, n_C, n_n, n_m)
```

```python
import numpy as np
import ml_dtypes
from contextlib import ExitStack
import concourse.bass as bass
import concourse.mybir as mybir
from concourse.bass_utils import run_bass_kernel_spmd

F32 = mybir.dt.float32
BF16 = mybir.dt.bfloat16
AF = mybir.ActivationFunctionType
ALU = mybir.AluOpType
AX = mybir.AxisListType

N_DMA_SEMS = {'sp': 32, 'pool': 8}


class Sched:
    ENG = ('pe', 'dve', 'act', 'pool', 'sp')

    def __init__(self, nc, es):
        self.nc = nc
        self.es = es
        self.semh = {}
        for e in self.ENG:
            self.semh[e] = es.enter_context(nc.semaphore("s_" + e))
        for q, n in N_DMA_SEMS.items():
            for i in range(n):
                self.semh[('d', q, i)] = es.enter_context(nc.semaphore("s_d%s%d" % (q, i)))
        self.prog = {e: [] for e in self.ENG}
        self.cnt = {e: 0 for e in self.ENG}
        self.seen = {e: {} for e in self.ENG}
        self.lastw = {}
        self.reads = {}
        self.pend = {e: ([], []) for e in self.ENG}
        self.ndma = {q: 0 for q in N_DMA_SEMS}
        self.nps = 0

    def sb(self, name, shape, dtype):
        return self.es.enter_context(self.nc.sbuf_tensor("sb_" + name, shape, dtype))

    def ps(self, name):
        self.nps += 1
        return self.es.enter_context(self.nc.psum_tensor(name, [128, 512], F32))

    def _deps(self, eng, rd, wr):
        ev = []
        for k in rd:
            if k in self.lastw:
                ev.append(self.lastw[k])
        for k in wr:
            if k in self.lastw:
                ev.append(self.lastw[k])
            for r in self.reads.get(k, ()):
                ev.append(r)
        return ev

    def _filter(self, eng, evs):
        best = {}
        for (s, v) in evs:
            if s == 'pe' and eng == 'pe':
                continue
            if self.seen[eng].get(s, 0) >= v:
                continue
            if best.get(s, 0) < v:
                best[s] = v
        for s, v in best.items():
            self.seen[eng][s] = v
        return list(best.items())

    def _register(self, ev, rd, wr):
        for k in rd:
            self.reads.setdefault(k, []).append(ev)
        for k in wr:
            self.lastw[k] = ev
            self.reads[k] = []

    def op(self, eng, fn, rd=(), wr=(), inc=True):
        waits = self._filter(eng, self._deps(eng, rd, wr))
        if inc:
            self.cnt[eng] += 1
            ev = (eng, self.cnt[eng])
            prd, pwr = self.pend[eng]
            self._register(ev, list(prd) + list(rd), list(pwr) + list(wr))
            self.pend[eng] = ([], [])
            self.prog[eng].append((waits, fn, (eng, 1)))
            return ev
        else:
            self.pend[eng][0].extend(rd)
            self.pend[eng][1].extend(wr)
            self.prog[eng].append((waits, fn, None))
            return None

    def dma(self, q, out, in_, rd=(), wr=()):
        nq = N_DMA_SEMS[q]
        slot = self.ndma[q] % nq
        rnd = self.ndma[q] // nq
        self.ndma[q] += 1
        evs = self._deps(q, rd, wr)
        if rnd > 0:
            evs.append((('d', q, slot), 16 * rnd))
        waits = self._filter(q, evs)
        ev = (('d', q, slot), 16 * (rnd + 1))
        self._register(ev, rd, wr)
        self.prog[q].append((waits, lambda e: e.dma_start(out=out, in_=in_), (('d', q, slot), 16)))
        return ev

    def finish(self, out_keys):
        evs = [self.lastw[k] for k in out_keys if k in self.lastw]
        waits = self._filter('sp', evs)
        self.prog['sp'].append((waits, None, None))
        nc = self.nc

        def replay(name, eng):
            for waits, fn, inc in self.prog[name]:
                for s, v in waits:
                    eng.wait_ge(self.semh[s], v)
                if fn is None:
                    continue
                ins = fn(eng)
                if inc is not None:
                    ins.then_inc(self.semh[inc[0]], inc[1])

        with nc.Block() as block:
            @block.tensor
            def _(e):
                replay('pe', e)

            @block.vector
            def _(e):
                replay('dve', e)

            @block.scalar
            def _(e):
                replay('act', e)

            @block.gpsimd
            def _(e):
                replay('pool', e)

            @block.sync
            def _(e):
                replay('sp', e)

        for h in self.semh.values():
            nc.gpsimd.sem_clear(h)
        nc.all_engine_barrier()


def _barrier(S):
    evs = [(e, S.cnt[e]) for e in ('pe', 'dve', 'act') if S.cnt[e] > 0]
    for q, nq in N_DMA_SEMS.items():
        n = S.ndma[q]
        for slot in range(min(n, nq)):
            rounds = (n - 1 - slot) // nq + 1
            evs.append((('d', q, slot), 16 * rounds))
    for eng in ('pe', 'dve', 'act', 'sp'):
        waits = S._filter(eng, evs)
        if waits:
            S.prog[eng].append((waits, None, None))


def I(name, **kw):
    return lambda e: getattr(e, name)(**kw)


D = 1024
DFF = 2816
T = 1536
NT = 12
SEQS = [(0, 8, True), (8, 2, False), (10, 2, False)]
EPS = 1e-6
GN_EPS = 1e-5
NEG = -30000.0
SM = {}
_o = 0
for _n, _w in [('cT', 16), ('bada', 72), ('g1', 8), ('g2', 8), ('g3', 8), ('gf', 8), ('conv', 32), ('dlogit', 8),
               ('bi', 8), ('bf', 8), ('retgn', 8), ('mgn', 4), ('m0', 8)]:
    SM[_n] = (_o, _o + _w)
    _o += _w
SM_W = _o
CW = 7 * 128 + 2 * 512


def build_program(dbg=None):
    nc = bass.Bass("TRN2", target_bir_lowering=False)

    def din(name, shape):
        return nc.dram_tensor(name, list(shape), F32, kind="ExternalInput").ap()

    def dout(name, shape):
        return nc.dram_tensor(name, list(shape), F32, kind="ExternalOutput").ap()

    x_d = din("x", [T, D])
    smalls_d = din("smalls", [128, SM_W])
    consts_d = din("consts", [128, CW])
    sret_d = din("sret", [128, 8, 256])
    smc_d = din("smc", [128, 8, 129])
    w_ada = din("w_ada", [D, 9 * D]).rearrange("(kc p) n -> p kc n", p=128)
    w1a = din("w1_ffn1", [D, DFF]).rearrange("(kc p) n -> p kc n", p=128)
    w3a = din("w3_ffn1", [D, DFF]).rearrange("(kc p) n -> p kc n", p=128)
    w2a = din("w2_ffn1", [DFF, D]).rearrange("(kc p) n -> p kc n", p=128)
    w1b = din("w1_ffn2", [D, DFF]).rearrange("(kc p) n -> p kc n", p=128)
    w3b = din("w3_ffn2", [D, DFF]).rearrange("(kc p) n -> p kc n", p=128)
    w2b = din("w2_ffn2", [DFF, D]).rearrange("(kc p) n -> p kc n", p=128)
    w_in = din("w_in", [D, 7184]).rearrange("(kc p) n -> p kc n", p=128)
    w_ru = din("w_ret_up", [1024, D]).rearrange("(kc p) n -> p kc n", p=128)
    w_mu = din("w_m_up", [512, D]).rearrange("(kc p) n -> p kc n", p=128)
    w_o = din("w_out", [D, D]).rearrange("(kc p) n -> p kc n", p=128)
    y_d = dout("y", [T, D])
    osr_d = dout("o_sr", [128, 16, 256])
    osc_d = dout("o_sc", [128, 16, 129])
    osm_d = dout("o_sm", [1, 16])
    dbg_outs = {}

    with ExitStack() as es:
        es.enter_context(nc.allow_low_precision("bf16 matmul operands, fp32 accumulation"))
        S = Sched(nc, es)
        xT = S.sb("xT", [128, 8, T], F32)
        uT = S.sb("uT", [128, 8, T], BF16)
        ring = [S.sb("ring%d" % i, [128, 4096], BF16) for i in range(4)]
        consts = S.sb("consts", [128, CW], F32)
        smalls = S.sb("smalls", [128, SM_W], F32)
        modT = S.sb("modT", [128, 72, 2], F32)
        modA = S.sb("modA", [128, 3, 8, 2], F32)
        modG = S.sb("modG", [128, 3, 8, 2], F32)
        cv = S.sb("cv", [128, 4], F32)
        ones_ms = S.sb("ones_ms", [128, 128], F32)
        ones1 = S.sb("ones1", [128, 128], F32)
        identb = S.sb("identb", [128, 128], BF16)
        P = [S.ps("P%d" % i) for i in range(8)]
        Pb = [p[:].bitcast(BF16) for p in P]
        ident = consts[:, 0:128]
        Um = consts[:, 128:256]
        Lm = consts[:, 256:384]
        negU = consts[:, 384:512]
        negL = consts[:, 512:640]
        iota_row = consts[:, 640:768]
        diffm = consts[:, 768:896]
        cos_t = consts[:, 896:896 + 512].rearrange("p (t f) -> p t f", t=8)
        sin_t = consts[:, 896 + 512:896 + 1024].rearrange("p (t f) -> p t f", t=8)

        def sm(name, a=None, b=None):
            lo, hi = SM[name]
            if a is None:
                return smalls[:, lo:hi]
            return smalls[:, lo + a:lo + b]

        ring_i = [0]

        def ring_slot():
            i = ring_i[0] % 4
            ring_i[0] += 1
            return i

        def dump(name, ap, shape, keys):
            if dbg is None or name not in dbg:
                return
            d = dout("dbg_" + name, shape)
            dbg_outs[name] = d
            S.dma('sp', d, ap, rd=keys, wr=[('dbgo', name)])

        S.dma('sp', smalls[:], smalls_d, wr=['smalls'])
        S.dma('sp', consts[:], consts_d, wr=['consts'])
        S.op('dve', I('memset', ap=cv[:, 0:1], constant=EPS), wr=['cv0'])
        S.op('dve', I('memset', ap=cv[:, 1:2], constant=GN_EPS), wr=['cv1'])
        S.op('dve', I('memset', ap=cv[:, 2:3], constant=1.0), wr=['cv2'])
        S.op('dve', I('memset', ap=cv[:, 3:4], constant=0.0), wr=['cv'])
        S.op('dve', I('memset', ap=ones_ms[:], constant=1.0 / 1024.0), wr=['ones_ms'])
        S.op('dve', I('memset', ap=ones1[:], constant=1.0), wr=['ones1'])
        S.op('dve', I('tensor_copy', out=identb[:], in_=ident), rd=['consts'], wr=['identb'])
        CVK = ['cv0', 'cv1', 'cv2', 'cv']

        ph0 = es.enter_context(ExitStack())
        scT = ph0.enter_context(nc.sbuf_tensor("scT", [128, 8, 2], BF16))
        S.op('act', I('activation', out=scT[:].rearrange("p a b -> p (a b)"), in_=sm('cT'), func=AF.Silu),
             rd=['smalls'], wr=['scT'])
        xin = [ph0.enter_context(nc.sbuf_tensor("xin%d" % i, [128, D], F32)) for i in range(2)]

        adaw = [ph0.enter_context(nc.sbuf_tensor("adaw%d" % i, [128, 4096], BF16)) for i in range(2)]

        def mod_block(blk):
            sl = blk % 2
            wv_ = adaw[sl][:].rearrange("p (k n) -> p k n", k=8)
            S.dma('pool', wv_, w_ada[:, :, blk * 512:(blk + 1) * 512], wr=[('adaw', sl)])
            for q in range(4):
                ch = blk * 4 + q
                for kc in range(8):
                    S.op('pe', I('matmul', out=P[7][:, ch * 2:ch * 2 + 2], lhsT=wv_[:, kc, q * 128:(q + 1) * 128],
                                 rhs=scT[:, kc, :], start=(kc == 0), stop=(kc == 7)),
                         rd=[('adaw', sl), 'scT'], wr=[('P', 7)], inc=(kc == 7 and q == 3))

        junkx = ph0.enter_context(nc.sbuf_tensor("junkx", [128, D], F32))
        ssc = ph0.enter_context(nc.sbuf_tensor("ssc", [128, NT], F32))
        dgx = [ph0.enter_context(nc.sbuf_tensor("dgx%d" % i, [128, 128], F32)) for i in range(2)]

        def x_tile(t):
            xs = xin[t % 2]
            S.dma('sp', xs[:], x_d[t * 128:(t + 1) * 128, :], wr=[('xin', t % 2)])
            S.op('dve', I('tensor_tensor', out=junkx[:], in0=xs[:], in1=xs[:], op=ALU.mult), rd=[('xin', t % 2)], wr=['junkx'])
            S.op('dve', I('tensor_reduce', out=ssc[:, t:t + 1], in_=junkx[:], axis=AX.X, op=ALU.add), rd=['junkx'], wr=[('ssc', t)])
            S.op('dve', I('tensor_scalar', out=dgx[t % 2][:], in0=ident, scalar1=ssc[:, t:t + 1], scalar2=None, op0=ALU.mult),
                 rd=[('ssc', t), 'consts'], wr=[('dgx', t % 2)])
            S.op('pe', I('matmul', out=P[4 + t // 4][:, (t % 4) * 128:(t % 4 + 1) * 128], lhsT=ones_ms[:], rhs=dgx[t % 2][:], start=True, stop=True),
                 rd=[('dgx', t % 2), 'ones_ms'], wr=[('P', 4 + t // 4)])
            for half in range(2):
                for q in range(4):
                    dc = half * 4 + q
                    S.op('pe', I('transpose', out=P[half][:, q * 128:(q + 1) * 128],
                                 in_=xs[:, dc * 128:(dc + 1) * 128], identity=ident),
                         rd=[('xin', t % 2), 'consts'], wr=[('P', half)], inc=(q == 3))
                dst = xT[:, half * 4:(half + 1) * 4, t * 128:(t + 1) * 128]
                src = P[half][:].rearrange("p (a b) -> p a b", a=4)
                wk = [('x', half * 4 + q, t // 4) for q in range(4)]
                if half == 0:
                    S.op('dve', I('tensor_copy', out=dst, in_=src), rd=[('P', 0)], wr=wk)
                else:
                    S.op('act', I('activation', out=dst, in_=src, func=AF.Copy), rd=[('P', 1)], wr=wk)

        def mod_finish(n0, n1):
            c0, c1 = 24 * n0, 24 * n1
            S.op('dve', I('tensor_tensor', out=modT[:, c0:c1, :], in0=P[7][:, 2 * c0:2 * c1].rearrange("p (c s) -> p c s", s=2),
                          in1=sm('bada', c0, c1).unsqueeze(2).to_broadcast([128, c1 - c0, 2]), op=ALU.add),
                 rd=[('P', 7), 'smalls'], wr=['modT'])
            for n in range(n0, n1):
                gname = ['g1', 'g2', 'g3'][n]
                S.op('dve', I('tensor_scalar', out=modA[:, n], in0=modT[:, (3 * n + 1) * 8:(3 * n + 2) * 8, :],
                              scalar1=1.0, scalar2=None, op0=ALU.add), rd=['modT'], wr=['modA'])
                S.op('dve', I('tensor_tensor', out=modA[:, n], in0=modA[:, n],
                              in1=sm(gname).unsqueeze(2).to_broadcast([128, 8, 2]), op=ALU.mult),
                     rd=['modA', 'smalls'], wr=['modA'])
                S.op('dve', I('tensor_scalar', out=modG[:, n], in0=modT[:, (3 * n + 2) * 8:(3 * n + 3) * 8, :],
                              scalar1=(1.0 if n == 1 else 0.5), scalar2=None, op0=ALU.mult), rd=['modT'], wr=['modG'])

        def mod_part(c0, c1):
            S.op('dve', I('tensor_tensor', out=modT[:, c0:c1, :], in0=P[7][:, 2 * c0:2 * c1].rearrange("p (c s) -> p c s", s=2),
                          in1=sm('bada', c0, c1).unsqueeze(2).to_broadcast([128, c1 - c0, 2]), op=ALU.add),
                 rd=[('P', 7), 'smalls'], wr=['modT'])

        for blk in range(4):
            mod_block(blk)
            x_tile(3 * blk)
            x_tile(3 * blk + 1)
            x_tile(3 * blk + 2)
        mod_part(0, 16)
        S.op('dve', I('tensor_scalar', out=modA[:, 0], in0=modT[:, 8:16, :], scalar1=1.0, scalar2=None, op0=ALU.add), rd=['modT'], wr=['modA'])
        S.op('dve', I('tensor_tensor', out=modA[:, 0], in0=modA[:, 0], in1=sm('g1').unsqueeze(2).to_broadcast([128, 8, 2]), op=ALU.mult),
             rd=['modA', 'smalls'], wr=['modA'])

        def mod_gate0():
            mod_block(5)
            mod_part(16, 24)
            S.op('dve', I('tensor_scalar', out=modG[:, 0], in0=modT[:, 16:24, :], scalar1=0.5, scalar2=None, op0=ALU.mult), rd=['modT'], wr=['modG'])

        mod_extra = [(lambda: mod_block(4)), mod_gate0] + [(lambda bb=bb: mod_block(bb)) for bb in range(6, 18)]
        MODK = ['modT', 'modA', 'modG']

        def norm_phase(ph, n, final_cb=None, banks=(5, 6, 7), pre=False):
            if final_cb is None:
                SGS = [(0, 1024, 0), (1024, 512, 1)]
            else:
                SGS = [(0, 512, 0), (512, 512, 0), (1024, 512, 1)]
            sq = [ph.enter_context(nc.sbuf_tensor("sq%d_%d" % (n, i), [128, 1024], F32)) for i in range(2)] if not pre else None
            rstd = ph.enter_context(nc.sbuf_tensor("rstd%d" % n, [128, T], F32))
            tm = [ph.enter_context(nc.sbuf_tensor("tm%d_%d" % (n, i), [128, 1024], F32)) for i in range(2)]
            kq = [0]

            def stats(o, w):
                nch = w // 512
                g0 = o // 512
                for dc in range(8 if not pre else 0):
                    s = kq[0] % 2
                    kq[0] += 1
                    S.op('act', I('activation', out=sq[s][:, 0:w], in_=xT[:, dc, o:o + w], func=AF.Square),
                         rd=[('x', dc, g0 + c) for c in range(nch)], wr=[('sq', s)])
                    for c in range(nch):
                        pb = banks[(g0 + c) % 3]
                        S.op('pe', I('matmul', out=P[pb][:], lhsT=ones_ms[:], rhs=sq[s][:, c * 512:(c + 1) * 512], start=(dc == 0), stop=(dc == 7)),
                             rd=[('sq', s), 'ones_ms'], wr=[('P', pb)], inc=True)
                for c in range(nch):
                    pb = banks[(g0 + c) % 3]
                    S.op('act', I('activation', out=rstd[:, o + c * 512:o + (c + 1) * 512], in_=P[pb][:], func=AF.Ln, bias=cv[:, 0:1], scale=1.0),
                         rd=[('P', pb)] + CVK, wr=[('rstd', g0 + c)])
                S.op('act', I('activation', out=rstd[:, o:o + w], in_=rstd[:, o:o + w], func=AF.Exp, scale=-0.5), rd=[('rstd', g0 + c) for c in range(nch)],
                     wr=[('rstd', g0 + c) for c in range(nch)])

            def apply(o, w, st):
                nch = w // 512
                g0 = o // 512
                for dc in range(8):
                    s = kq[0] % 2
                    kq[0] += 1
                    S.op('dve', I('tensor_tensor', out=tm[s][:, 0:w], in0=xT[:, dc, o:o + w], in1=rstd[:, o:o + w], op=ALU.mult),
                         rd=[('x', dc, g0 + c) for c in range(nch)] + [('rstd', g0 + c) for c in range(nch)], wr=[('tm', s)])
                    if final_cb is None:
                        S.op('act', I('activation', out=uT[:, dc, o:o + w], in_=tm[s][:, 0:w], func=AF.Identity,
                                      bias=modT[:, 3 * n * 8 + dc, st:st + 1], scale=modA[:, n, dc, st:st + 1]),
                             rd=[('tm', s)] + MODK, wr=[('u', dc, g0 + c) for c in range(nch)])
                    else:
                        final_cb(g0, dc, tm[s][:, 0:512], ('tm', s))

            stats(SGS[0][0], SGS[0][1])
            for i_, (o, w, st) in enumerate(SGS):
                if i_ + 1 < len(SGS):
                    stats(SGS[i_ + 1][0], SGS[i_ + 1][1])
                apply(o, w, st)

        stat_k = [0]

        def emit_stat(sqb, dc, g, banks, acc):
            gs = slice(g * 512, (g + 1) * 512)
            if dc == 0:
                S.op('act', I('activation', out=acc[g][:], in_=xT[:, dc, gs], func=AF.Square), rd=[('x', dc, g)], wr=[('acc', g)])
            else:
                s = stat_k[0] % 2
                stat_k[0] += 1
                S.op('act', I('activation', out=sqb[s][:], in_=xT[:, dc, gs], func=AF.Square), rd=[('x', dc, g)], wr=[('sqb', s)])
                S.op('dve', I('tensor_tensor', out=acc[g][:], in0=acc[g][:], in1=sqb[s][:], op=ALU.add), rd=[('acc', g), ('sqb', s)], wr=[('acc', g)])
            if dc == 7:
                S.op('pe', I('matmul', out=P[banks[g]][:], lhsT=ones_ms[:], rhs=acc[g][:], start=True, stop=True),
                     rd=[('acc', g), 'ones_ms'], wr=[('P', banks[g])], inc=True)

        def ffn_phase(ph, n, w1, w3, w2, extra=(), stat_banks=None):
            extra = list(extra)
            sqb = [ph.enter_context(nc.sbuf_tensor("sqb%d_%d" % (n, i), [128, 512], F32)) for i in range(2)]
            pstat = []
            accb = [ph.enter_context(nc.sbuf_tensor("accb%d_%d" % (n, i), [128, 512], F32)) for i in range(3)] if stat_banks is not None else None
            hT = ph.enter_context(nc.sbuf_tensor("hT%d" % n, [128, 11, T], BF16))
            sa = [ph.enter_context(nc.sbuf_tensor("sa%d_%d" % (n, i), [128, 512], F32)) for i in range(2)]
            k = 0
            for half in range(2):
                for (off, wdt) in [(0, 512), (512, 512), (1024, 384)]:
                    c0 = half * 1408 + off
                    s1 = ring_slot()
                    v1 = ring[s1][:].rearrange("p (k n) -> p k n", k=8)
                    S.dma('pool', v1[:, :, 0:wdt], w1[:, :, c0:c0 + wdt], wr=[('ring', s1)])
                    s3 = ring_slot()
                    v3 = ring[s3][:].rearrange("p (k n) -> p k n", k=8)
                    S.dma('pool', v3[:, :, 0:wdt], w3[:, :, c0:c0 + wdt], wr=[('ring', s3)])
                    pending_extra = extra.pop(0) if extra else None
                    for q in range(wdt // 128):
                        fl = (off + q * 128) // 128
                        for g in range(3):
                            gs = slice(g * 512, (g + 1) * 512)
                            pi = 2 * (k % 2)
                            s = k % 2
                            k += 1
                            for kc in range(8):
                                S.op('pe', I('matmul', out=P[pi][:], lhsT=v1[:, kc, q * 128:(q + 1) * 128], rhs=uT[:, kc, gs],
                                             start=(kc == 0), stop=(kc == 7)),
                                     rd=[('ring', s1), ('u', kc, g)], wr=[('P', pi)], inc=(kc == 7))
                            for kc in range(8):
                                S.op('pe', I('matmul', out=P[pi + 1][:], lhsT=v3[:, kc, q * 128:(q + 1) * 128], rhs=uT[:, kc, gs],
                                             start=(kc == 0), stop=(kc == 7)),
                                     rd=[('ring', s3), ('u', kc, g)], wr=[('P', pi + 1)], inc=(kc == 7))
                            S.op('act', I('activation', out=sa[s][:], in_=P[pi][:], func=AF.Silu), rd=[('P', pi)], wr=[('sa', s)])
                            S.op('dve', I('tensor_tensor', out=hT[:, fl, gs], in0=sa[s][:], in1=P[pi + 1][:], op=ALU.mult),
                                 rd=[('sa', s), ('P', pi + 1)], wr=[('h', fl, g)])
                    if pending_extra is not None:
                        pending_extra()
                for cb in range(4):
                    sl = ring_slot()
                    v2 = ring[sl][:, 0:11 * 256].rearrange("p (k n) -> p k n", k=11)
                    S.dma('pool', v2, w2[:, half * 11:(half + 1) * 11, cb * 256:(cb + 1) * 256], wr=[('ring', sl)])
                    pend2 = extra.pop(0) if extra else None
                    for q in range(2):
                        dc = cb * 2 + q
                        for g in range(3):
                            gs = slice(g * 512, (g + 1) * 512)
                            st = 0 if g < 2 else 1
                            pi = 4 + (k % 2)
                            k += 1
                            for fl in range(11):
                                S.op('pe', I('matmul', out=P[pi][:], lhsT=v2[:, fl, q * 128:(q + 1) * 128], rhs=hT[:, fl, gs],
                                             start=(fl == 0), stop=(fl == 10)),
                                     rd=[('ring', sl), ('h', fl, g)], wr=[('P', pi)], inc=(fl == 10))
                            S.op('dve', I('scalar_tensor_tensor', out=xT[:, dc, gs], in0=P[pi][:], scalar=modG[:, n, dc, st:st + 1],
                                          in1=xT[:, dc, gs], op0=ALU.mult, op1=ALU.add),
                                 rd=[('P', pi), ('x', dc, g)] + MODK, wr=[('x', dc, g)])
                            if half == 1 and stat_banks is not None:
                                pstat.append((dc, g))
                                if len(pstat) > 3:
                                    emit_stat(sqb, *pstat.pop(0), stat_banks, accb)
                    if pend2 is not None:
                        pend2()
            while extra:
                extra.pop(0)()
            while pstat:
                emit_stat(sqb, *pstat.pop(0), stat_banks, accb)

        with ExitStack() as ph:
            norm_phase(ph, 0, banks=(4, 5, 6), pre=True)
            ffn_phase(ph, 0, w1a, w3a, w2a, extra=mod_extra, stat_banks=(0, 1, 2))
            mod_finish(1, 3)
            _barrier(S)
        ph0.close()

        SCK = 128.0 ** -0.5
        mxs = es.enter_context(ExitStack())
        rT = mxs.enter_context(nc.sbuf_tensor("rT", [128, 8, T], BF16))
        hmT = mxs.enter_context(nc.sbuf_tensor("hmT", [128, 4, T], BF16))
        with ExitStack() as ph:
            norm_phase(ph, 1, banks=(0, 1, 2), pre=True)
            _barrier(S)

        tctr = [0]

        def alt():
            tctr[0] += 1
            return 'dve' if tctr[0] % 2 == 0 else 'act'

        def copy_op(eng, out, in_, rd, wr):
            if eng == 'dve':
                S.op('dve', I('tensor_copy', out=out, in_=in_), rd=rd, wr=wr)
            else:
                S.op('act', I('activation', out=out, in_=in_, func=AF.Copy), rd=rd, wr=wr)

        def group_norm_stats(st, src_ap, src_keys, width, junk, tag):
            inv = 1.0 / width
            S.op('dve', I('tensor_reduce', out=st[:, 0:1], in_=src_ap, axis=AX.X, op=ALU.add), rd=src_keys, wr=[(tag, 0)])
            S.op('act', I('activation', out=junk, in_=src_ap, func=AF.Square), rd=src_keys, wr=[(tag, 'junk')])
            S.op('dve', I('tensor_reduce', out=st[:, 1:2], in_=junk, axis=AX.X, op=ALU.add), rd=[(tag, 'junk')], wr=[(tag, 1)])
            S.op('dve', I('tensor_scalar', out=st[:, 2:3], in0=st[:, 0:1], scalar1=inv, scalar2=None, op0=ALU.mult),
                 rd=[(tag, 0)], wr=[(tag, 2)])
            S.op('dve', I('tensor_tensor', out=st[:, 3:4], in0=st[:, 2:3], in1=st[:, 2:3], op=ALU.mult), rd=[(tag, 2)], wr=[(tag, 3)])
            S.op('dve', I('scalar_tensor_tensor', out=st[:, 4:5], in0=st[:, 1:2], scalar=inv, in1=st[:, 3:4],
                          op0=ALU.mult, op1=ALU.subtract), rd=[(tag, 1), (tag, 3)], wr=[(tag, 4)])
            S.op('act', I('activation', out=st[:, 5:6], in_=st[:, 4:5], func=AF.Sqrt, bias=cv[:, 1:2], scale=1.0),
                 rd=[(tag, 4)] + CVK, wr=[(tag, 5)])
            S.op('dve', I('reciprocal', out=st[:, 5:6], in_=st[:, 5:6]), rd=[(tag, 5)], wr=[(tag, 5)])
            S.op('dve', I('scalar_tensor_tensor', out=st[:, 6:7], in0=st[:, 2:3], scalar=-1.0, in1=st[:, 5:6],
                          op0=ALU.mult, op1=ALU.mult), rd=[(tag, 2), (tag, 5)], wr=[(tag, 6)])

        with ExitStack() as ph:
            def sbt(name, shape, dt):
                return ph.enter_context(nc.sbuf_tensor(name, shape, dt))
            lg = sbt("r_lg", [128, 8], F32)
            nlg = sbt("r_nlg", [128, 8], F32)
            lg127 = sbt("r_lg127", [128, 8], F32)
            lg128 = sbt("r_lg128", [128, 8], F32)
            gch = sbt("r_gch", [128, 8], F32)
            wkt = sbt("r_wkt", [128, 8], F32)
            Mh = sbt("r_Mh", [128, 4, 128], F32)
            Wq = sbt("r_Wq", [128, 8, 128], F32)
            ta = sbt("r_ta", [128, 128], F32)
            tb = sbt("r_tb", [128, 128], F32)
            qktok = sbt("r_qktok", [128, NT, 2, 128], BF16)
            qkT = sbt("r_qkT", [128, 2, T], BF16)
            rv = sbt("r_rv", [128, NT, 256], BF16)
            rgs = sbt("r_rgs", [128, NT, 256], BF16)
            Sst = sbt("r_Sst", [128, 2, 2, 256], F32)
            Sbst = sbt("r_Sbst", [128, 8, 256], BF16)
            Sfb = [sbt("r_Sfb%d" % i, [128, 256], BF16) for i in range(2)]
            kw = [sbt("r_kw%d" % i, [128, 128], BF16) for i in range(2)]
            qfb = [sbt("r_qfb%d" % i, [128, 2, 128], BF16) for i in range(2)]
            sTm = [sbt("r_sTm%d" % i, [128, 128], BF16) for i in range(2)]
            rt = [sbt("r_rt%d" % i, [128, 2, 64], F32) for i in range(4)]
            oall = sbt("r_oall", [128, 8, 256], F32)
            junk = sbt("r_junk", [128, 2, 256], F32)
            st = sbt("r_st", [128, 8, 8], F32)
            DK = ['lg', 'nlg', 'lg127', 'lg128', 'consts']

            def decay_tables():
                S.op('act', I('activation', out=lg[:], in_=sm('dlogit'), func=AF.Exp, scale=-1.0), rd=['smalls'], wr=['lg'])
                S.op('act', I('activation', out=lg[:], in_=lg[:], func=AF.Ln, bias=cv[:, 2:3], scale=1.0), rd=['lg'] + CVK, wr=['lg'])
                S.op('dve', I('tensor_scalar', out=nlg[:], in0=lg[:], scalar1=1.0, scalar2=None, op0=ALU.mult), rd=['lg'], wr=['nlg'])
                S.op('dve', I('tensor_scalar', out=lg[:], in0=nlg[:], scalar1=-1.0, scalar2=None, op0=ALU.mult), rd=['nlg'], wr=['lg'])
                S.op('dve', I('tensor_scalar', out=lg127[:], in0=lg[:], scalar1=127.0, scalar2=None, op0=ALU.mult), rd=['lg'], wr=['lg127'])
                S.op('dve', I('tensor_scalar', out=lg128[:], in0=lg[:], scalar1=128.0, scalar2=None, op0=ALU.mult), rd=['lg'], wr=['lg128'])
                S.op('act', I('activation', out=gch[:], in_=lg128[:], func=AF.Exp), rd=['lg128'], wr=['dec_g'])
                DK = ['lg', 'nlg', 'lg127', 'lg128', 'consts']
                for h in range(4):
                    S.op('dve', I('tensor_scalar', out=ta[:], in0=diffm, scalar1=lg[:, h:h + 1], scalar2=0.0, op0=ALU.mult, op1=ALU.min),
                         rd=DK, wr=['ta'])
                    S.op('act', I('activation', out=ta[:], in_=ta[:], func=AF.Exp), rd=['ta'], wr=['ta'])
                    S.op('dve', I('tensor_tensor', out=ta[:], in0=ta[:], in1=Um, op=ALU.mult), rd=['ta', 'consts'], wr=['ta'])
                    S.op('dve', I('tensor_scalar', out=tb[:], in0=diffm, scalar1=nlg[:, 4 + h:5 + h], scalar2=0.0, op0=ALU.mult, op1=ALU.min),
                         rd=DK, wr=['tb'])
                    S.op('act', I('activation', out=tb[:], in_=tb[:], func=AF.Exp), rd=['tb'], wr=['tb'])
                    S.op('dve', I('tensor_tensor', out=tb[:], in0=tb[:], in1=Lm, op=ALU.mult), rd=['tb', 'consts'], wr=['tb'])
                    S.op('dve', I('tensor_tensor', out=ta[:], in0=ta[:], in1=tb[:], op=ALU.add), rd=['ta', 'tb'], wr=['ta'])
                    S.op('dve', I('tensor_scalar', out=Mh[:, h, :], in0=ta[:], scalar1=SCK, scalar2=None, op0=ALU.mult), rd=['ta'], wr=['dec_M'])
                    S.op('act', I('activation', out=Wq[:, h, :], in_=iota_row, func=AF.Exp, bias=lg[:, h:h + 1], scale=lg[:, h:h + 1]),
                         rd=DK, wr=['dec_Wq'])
                    S.op('act', I('activation', out=Wq[:, 4 + h, :], in_=iota_row, func=AF.Exp, bias=lg128[:, 4 + h:5 + h], scale=nlg[:, 4 + h:5 + h]),
                         rd=DK, wr=['dec_Wq'])
                    S.op('act', I('activation', out=wkt[:, h:h + 1], in_=diffm[:, 0:1], func=AF.Exp, bias=lg127[:, h:h + 1], scale=lg[:, h:h + 1]),
                         rd=DK, wr=['dec_wk'])
                    S.op('act', I('activation', out=wkt[:, 4 + h:5 + h], in_=diffm[:, 0:1], func=AF.Exp, scale=nlg[:, 4 + h:5 + h]),
                         rd=DK, wr=['dec_wk'])
                S.op('dve', I('tensor_scalar', out=wkt[:], in0=wkt[:], scalar1=SCK, scalar2=None, op0=ALU.mult), rd=['dec_wk'], wr=['dec_wk'])

            DEC = ['dec_g', 'dec_M', 'dec_Wq', 'dec_wk']

            kk = [0]

            def proj_r(h):
                slA = ring_slot()
                vA = ring[slA][:, 0:2048].rearrange("p (k n) -> p k n", k=8)
                S.dma('pool', vA[:, :, 0:128], w_in[:, :, h * 128:(h + 1) * 128], wr=[('ring', slA)])
                S.dma('pool', vA[:, :, 128:256], w_in[:, :, 512 + h * 128:512 + (h + 1) * 128], wr=[('ring', slA)])
                slB = ring_slot()
                vB = ring[slB][:].rearrange("p (k n) -> p k n", k=8)
                S.dma('pool', vB[:, :, 0:256], w_in[:, :, 1024 + h * 256:1024 + (h + 1) * 256], wr=[('ring', slB)])
                S.dma('pool', vB[:, :, 256:512], w_in[:, :, 2048 + h * 256:2048 + (h + 1) * 256], wr=[('ring', slB)])
                for t in range(NT):
                    ts_ = slice(t * 128, (t + 1) * 128)
                    pq = t % 2
                    pv = 2 if t % 2 == 0 else 6
                    for kc in range(8):
                        S.op('pe', I('matmul', out=P[pq][:, 0:256], lhsT=uT[:, kc, ts_], rhs=vA[:, kc, :], start=(kc == 0), stop=(kc == 7)),
                             rd=[('ring', slA), ('u', kc, t // 4)], wr=[('P', pq)], inc=(kc == 7))
                    for kc in range(8):
                        S.op('pe', I('matmul', out=P[pv][:], lhsT=uT[:, kc, ts_], rhs=vB[:, kc, :], start=(kc == 0), stop=(kc == 7)),
                             rd=[('ring', slB), ('u', kc, t // 4)], wr=[('P', pv)], inc=(kc == 7))
                    if t < 8:
                        X = P[pq][:, 0:256].rearrange("p (a b) -> p a b", a=2)
                        x1 = X[:, :, 0:64]
                        x2 = X[:, :, 64:128]
                        cb_ = cos_t[:, t, :].unsqueeze(1).to_broadcast([128, 2, 64])
                        sb_ = sin_t[:, t, :].unsqueeze(1).to_broadcast([128, 2, 64])
                        S.op('dve', I('tensor_tensor', out=rt[0][:], in0=x1, in1=cb_, op=ALU.mult), rd=[('P', pq), 'consts'], wr=[('rt', 0)])
                        S.op('dve', I('tensor_tensor', out=rt[1][:], in0=x2, in1=sb_, op=ALU.mult), rd=[('P', pq), 'consts'], wr=[('rt', 1)])
                        S.op('dve', I('tensor_tensor', out=rt[2][:], in0=x2, in1=cb_, op=ALU.mult), rd=[('P', pq), 'consts'], wr=[('rt', 2)])
                        S.op('dve', I('tensor_tensor', out=rt[3][:], in0=x1, in1=sb_, op=ALU.mult), rd=[('P', pq), 'consts'], wr=[('rt', 3)])
                        S.op('dve', I('tensor_tensor', out=qktok[:, t, :, 0:64], in0=rt[0][:], in1=rt[1][:], op=ALU.subtract),
                             rd=[('rt', 0), ('rt', 1)], wr=[('qktok', t, 0)])
                        S.op('dve', I('tensor_tensor', out=qktok[:, t, :, 64:128], in0=rt[2][:], in1=rt[3][:], op=ALU.add),
                             rd=[('rt', 2), ('rt', 3)], wr=[('qktok', t, 1)])
                    else:
                        S.op('act', I('activation', out=qktok[:, t].rearrange("p a b -> p (a b)"), in_=P[pq][:, 0:256], func=AF.Copy),
                             rd=[('P', pq)], wr=[('qktok', t, 0), ('qktok', t, 1)])
                    S.op('act', I('activation', out=rv[:, t, :], in_=P[pv][:, 0:256], func=AF.Copy), rd=[('P', pv)], wr=[('rv', t)])
                    S.op('act', I('activation', out=rgs[:, t, :], in_=P[pv][:, 256:512], func=AF.Silu), rd=[('P', pv)], wr=[('rgs', t)])
                    pt = 3 if t % 2 == 0 else 7
                    for c in range(2):
                        S.op('pe', I('transpose', out=Pb[pt][:, c * 128:(c + 1) * 128], in_=qktok[:, t, c, :], identity=identb[:]),
                             rd=[('qktok', t, 0), ('qktok', t, 1), 'identb'], wr=[('P', pt)], inc=(c == 1))
                    copy_op('dve' if t % 2 == 0 else 'act', qkT[:, :, ts_], Pb[pt][:, 0:256].rearrange("p (a b) -> p a b", a=2), [('P', pt)], [('qkT', t)])

            def rest_r(h):
                for si, (t0, N, samp) in enumerate(SEQS):
                    p_ = si - 1
                    ebase = 0 if samp else 8
                    sver = [0, 0]
                    if samp:
                        S.dma('sp', Sst[:, 0, 0, :], sret_d[:, h, :], wr=[('S', 0, 0)])
                        S.dma('sp', Sst[:, 1, 0, :], sret_d[:, 4 + h, :], wr=[('S', 1, 0)])
                    else:
                        S.op('dve', I('memset', ap=Sst[:, 0, 0, :], constant=0.0), wr=[('S', 0, 0)])
                        S.op('dve', I('memset', ap=Sst[:, 1, 0, :], constant=0.0), wr=[('S', 1, 0)])

                    def kv_mm(t, d):
                        s = kk[0] % 2
                        kk[0] += 1
                        di = d * 4 + h
                        S.op('dve', I('tensor_scalar', out=kw[s][:], in0=qktok[:, t, 1, :], scalar1=wkt[:, di:di + 1], scalar2=None, op0=ALU.mult),
                             rd=[('qktok', t, 0), ('qktok', t, 1)] + DEC, wr=[('kw', s)])
                        S.op('pe', I('matmul', out=P[5 + s][:, 0:256], lhsT=kw[s][:], rhs=rv[:, t, :], start=True, stop=True),
                             rd=[('kw', s), ('rv', t)], wr=[('P', 5 + s)])
                        return s

                    def s_update(d, s):
                        di = d * 4 + h
                        cu = sver[d]
                        S.op('dve', I('scalar_tensor_tensor', out=Sst[:, d, 1 - cu, :], in0=Sst[:, d, cu, :], scalar=gch[:, di:di + 1], in1=P[5 + s][:, 0:256],
                                      op0=ALU.mult, op1=ALU.add), rd=[('S', d, cu), ('P', 5 + s)] + DEC, wr=[('S', d, 1 - cu)])
                        sver[d] = 1 - cu

                    order = list(reversed(range(N)))
                    pend = kv_mm(t0 + order[0], 1)
                    for oi, n in enumerate(order):
                        cur = pend
                        if oi + 1 < N:
                            pend = kv_mm(t0 + order[oi + 1], 1)
                        S.op('act', I('activation', out=Sbst[:, n, :], in_=Sst[:, 1, sver[1], :], func=AF.Copy), rd=[('S', 1, sver[1])], wr=[('Sbst', n)])
                        s_update(1, cur)
                    if not samp:
                        S.dma('sp', osr_d[:, p_ * 8 + 4 + h, :], Sst[:, 1, sver[1], :], rd=[('S', 1, sver[1])], wr=[('osr', p_, 1, h)])

                    def indep(n):
                        t = t0 + n
                        ts_ = slice(t * 128, (t + 1) * 128)
                        s = n % 2
                        S.op('dve', I('tensor_tensor', out=qfb[s][:], in0=qkT[:, 0, ts_].unsqueeze(1).to_broadcast([128, 2, 128]),
                                      in1=Wq[:].rearrange("p (d q) f -> p d q f", d=2)[:, :, h, :], op=ALU.mult),
                             rd=[('qkT', t)] + DEC, wr=[('qf', s), ('qb', s)])
                        S.op('pe', I('matmul', out=P[3 + s][:, 0:128], lhsT=qkT[:, 1, ts_], rhs=qkT[:, 0, ts_], start=True, stop=True),
                             rd=[('qkT', t)], wr=[('P', 3 + s)])
                        S.op('dve', I('tensor_tensor', out=sTm[s][:], in0=P[3 + s][:, 0:128], in1=Mh[:, h, :], op=ALU.mult),
                             rd=[('P', 3 + s)] + DEC, wr=[('sTm', s)])
                        return kv_mm(t, 0)

                    pend = indep(0)
                    for n in range(N):
                        t = t0 + n
                        s = n % 2
                        cur = pend
                        if n + 1 < N:
                            pend = indep(n + 1)
                        S.op('act', I('activation', out=Sfb[s][:], in_=Sst[:, 0, sver[0], :], func=AF.Copy), rd=[('S', 0, sver[0])], wr=[('Sfb', s)])
                        po = 0 if s == 0 else 7
                        pk = ('P', 0) if s == 0 else ('P', 7)
                        S.op('pe', I('matmul', out=P[po][:, 0:256], lhsT=qfb[s][:, 0, :], rhs=Sfb[s][:], start=True, stop=False),
                             rd=[('qf', s), ('Sfb', s)], wr=[pk], inc=False)
                        S.op('pe', I('matmul', out=P[po][:, 0:256], lhsT=qfb[s][:, 1, :], rhs=Sbst[:, n, :], start=False, stop=False),
                             rd=[('qb', s), ('Sbst', n)], wr=[pk], inc=False)
                        S.op('pe', I('matmul', out=P[po][:, 0:256], lhsT=sTm[s][:], rhs=rv[:, t, :], start=False, stop=True),
                             rd=[('sTm', s), ('rv', t)], wr=[pk], inc=True)
                        S.op('act', I('activation', out=oall[:, t - ebase, :], in_=P[po][:, 0:256], func=AF.Copy), rd=[pk], wr=[('oall', t - ebase)])
                        s_update(0, cur)
                    if not samp:
                        S.dma('sp', osr_d[:, p_ * 8 + h, :], Sst[:, 0, sver[0], :], rd=[('S', 0, sver[0])], wr=[('osr', p_, 0, h)])

                    if si == 1:
                        continue
                    ea, eb_ = (0, 8) if samp else (8, 12)
                    ne = eb_ - ea
                    OK_ = [('oall', j) for j in range(ne)]
                    QK_ = [('qktok', t, c) for t in range(ea, eb_) for c in range(2)]
                    inv = 1.0 / 256.0
                    S.op('dve', I('tensor_reduce', out=st[:, 0, 0:ne], in_=oall[:, 0:ne, :], axis=AX.X, op=ALU.add), rd=OK_, wr=[('rst', 0)])
                    for j in range(0, ne, 2):
                        S.op('act', I('activation', out=junk[:].rearrange("p a b -> p (a b)"), in_=oall[:, j:j + 2, :].rearrange("p a b -> p (a b)"), func=AF.Square),
                             rd=OK_, wr=['rjunk'])
                        S.op('dve', I('tensor_reduce', out=st[:, 1, j:j + 2], in_=junk[:], axis=AX.X, op=ALU.add), rd=['rjunk'], wr=[('rst', 1)])
                    S.op('dve', I('tensor_scalar', out=st[:, 2, 0:ne], in0=st[:, 0, 0:ne], scalar1=inv, scalar2=None, op0=ALU.mult), rd=[('rst', 0)], wr=[('rst', 2)])
                    S.op('dve', I('tensor_tensor', out=st[:, 3, 0:ne], in0=st[:, 2, 0:ne], in1=st[:, 2, 0:ne], op=ALU.mult), rd=[('rst', 2)], wr=[('rst', 3)])
                    S.op('dve', I('scalar_tensor_tensor', out=st[:, 4, 0:ne], in0=st[:, 1, 0:ne], scalar=inv, in1=st[:, 3, 0:ne], op0=ALU.mult, op1=ALU.subtract),
                         rd=[('rst', 1), ('rst', 3)], wr=[('rst', 4)])
                    S.op('act', I('activation', out=st[:, 5, 0:ne], in_=st[:, 4, 0:ne], func=AF.Sqrt, bias=cv[:, 1:2], scale=1.0), rd=[('rst', 4)] + CVK, wr=[('rst', 5)])
                    S.op('dve', I('reciprocal', out=st[:, 5, 0:ne], in_=st[:, 5, 0:ne]), rd=[('rst', 5)], wr=[('rst', 5)])
                    S.op('dve', I('scalar_tensor_tensor', out=st[:, 6, 0:ne], in0=st[:, 2, 0:ne], scalar=-1.0, in1=st[:, 5, 0:ne], op0=ALU.mult, op1=ALU.mult),
                         rd=[('rst', 2), ('rst', 5)], wr=[('rst', 6)])
                    for j in range(ne):
                        S.op('act', I('activation', out=oall[:, j, :], in_=oall[:, j, :], func=AF.Identity, bias=st[:, 6, j:j + 1], scale=st[:, 5, j:j + 1]),
                             rd=[('oall', j), ('rst', 5), ('rst', 6)], wr=[('oall', j)])
                    rbfv = qktok[:, ea:eb_].rearrange("p t c f -> p t (c f)")
                    S.op('dve', I('tensor_tensor', out=rbfv, in0=oall[:, 0:ne, :], in1=rgs[:, ea:eb_, :], op=ALU.mult),
                         rd=OK_ + [('rgs', t) for t in range(ea, eb_)] + QK_, wr=QK_)
                    for c in range(2):
                        pe_ = 1 + c
                        for t in range(ea, eb_):
                            S.op('pe', I('transpose', out=Pb[pe_][:, (t - ea) * 128:(t - ea + 1) * 128], in_=qktok[:, t, c, :], identity=identb[:]),
                                 rd=[('qktok', t, 0), ('qktok', t, 1), 'identb'], wr=[('P', pe_)], inc=(t == eb_ - 1))
                        S.op('act', I('activation', out=rT[:, 2 * h + c, ea * 128:eb_ * 128], in_=Pb[pe_][:, 0:ne * 128], func=AF.Identity,
                                      scale=sm('retgn', 2 * h + c, 2 * h + c + 1)),
                             rd=[('P', pe_), 'smalls'], wr=[('rT', 2 * h + c, 0), ('rT', 2 * h + c, 1), ('rT', 2 * h + c, 2)])
            proj_r(0)
            decay_tables()
            for h in range(4):
                rest_r(h)
                if h + 1 < 4:
                    proj_r(h + 1)
            _barrier(S)
        with ExitStack() as ph:
            def sbt(name, shape, dt):
                return ph.enter_context(nc.sbuf_tensor(name, shape, dt))
            wloc = sbt("m_wloc", [128, NT, 8], F32)
            wa2 = sbt("m_wa2", [128, NT, 8], F32)
            a1 = sbt("m_a1", [128, NT, 8], F32)
            a2 = sbt("m_a2", [128, NT, 8], F32)
            enm = sbt("m_enm", [128, NT, 8], F32)
            wint = sbt("m_wint", [128, NT, 8], F32)
            flr = sbt("m_flr", [128, NT, 8], F32)
            mfin = sbt("m_mfin", [128, 2, 8], F32)
            xpre = sbt("m_xpre", [128, T], F32)
            ycv = sbt("m_ycv", [128, T], F32)
            qmT = sbt("m_qmT", [128, T], BF16)
            kmT = sbt("m_kmT", [128, T], BF16)
            kmtok = sbt("m_kmtok", [128, NT, 128], BF16)
            vext = sbt("m_vext", [128, NT, 136], BF16)
            mos = sbt("m_mos", [128, 2, NT, 128], BF16)
            ndall = sbt("m_ndall", [128, NT, 2, 130], F32)
            Cst = sbt("m_Cst", [128, 2, 2, 130], F32)
            Cbf = sbt("m_Cbf", [128, 2, 2, 136], BF16)
            sTb = [sbt("m_sT%d" % i, [128, 128], BF16) for i in range(3)]
            eb = sbt("m_eb", [128, 1, NT, 2], F32)
            st = sbt("m_st", [128, 8, NT], F32)
            gp = ExitStack()

            def sbg(name, shape, dt):
                return gp.enter_context(nc.sbuf_tensor(name, shape, dt))
            ig = sbg("m_ig", [128, NT, 8], F32)
            lf = sbg("m_lf", [128, NT, 8], F32)
            btok = sbg("m_btok", [128, NT, 8], F32)
            tot = sbg("m_tot", [128, NT, 8], F32)
            atok = sbg("m_atok", [128, NT, 8], F32)
            amax = sbg("m_amax", [128, NT, 8], F32)
            ml = sbg("m_ml", [128, NT, 8], F32)
            mprev = sbg("m_mprev", [128, NT, 8], F32)
            mnew = sbg("m_mnew", [128, NT, 8], F32)
            pmtok = sbg("m_pmtok", [128, NT, 8], F32)
            mx = sbg("m_mx", [128, NT, 8], F32)
            aT = sbg("m_aT", [128, 128], F32)
            pfa = sbg("m_pfa", [128, 128], F32)
            pfb = sbg("m_pfb", [128, 128], F32)
            tmp4 = sbg("m_tmp4", [128, 4], F32)
            amaxc = sbg("m_amaxc", [128, 1], F32)
            diagA = sbg("m_diagA", [128, 96], F32)

            def f2(a):
                return a[:].rearrange("p t c -> p (t c)")
            GK = ['gates']

            def gate_prep():
                slG = ring_slot()
                vG = ring[slG][:, 0:128].rearrange("p (k n) -> p k n", k=8)
                S.dma('pool', vG, w_in[:, :, 5120:5136], wr=[('ring', slG)])
                for t in range(NT):
                    for kc in range(8):
                        S.op('pe', I('matmul', out=P[0][:, t * 16:(t + 1) * 16], lhsT=uT[:, kc, t * 128:(t + 1) * 128], rhs=vG[:, kc, :],
                                     start=(kc == 0), stop=(kc == 7)), rd=[('ring', slG), ('u', kc, t // 4)], wr=[('P', 0)], inc=(kc == 7 and t == NT - 1))
                G3 = P[0][:, 0:192].rearrange("p (t c) -> p t c", c=16)
                S.op('dve', I('tensor_tensor', out=ig[:], in0=G3[:, :, 0:8], in1=sm('bi').unsqueeze(1).to_broadcast([128, NT, 8]), op=ALU.add),
                     rd=[('P', 0), 'smalls'], wr=GK)
                S.op('dve', I('tensor_tensor', out=lf[:], in0=G3[:, :, 8:16], in1=sm('bf').unsqueeze(1).to_broadcast([128, NT, 8]), op=ALU.add),
                     rd=[('P', 0), 'smalls'], wr=GK)
                S.op('act', I('activation', out=f2(lf), in_=f2(lf), func=AF.Exp, scale=-1.0), rd=GK, wr=GK)
                S.op('act', I('activation', out=f2(lf), in_=f2(lf), func=AF.Ln, bias=cv[:, 2:3], scale=1.0), rd=GK + CVK, wr=GK)
                S.op('dve', I('tensor_scalar', out=f2(lf), in0=f2(lf), scalar1=-1.0, scalar2=None, op0=ALU.mult), rd=GK, wr=GK)
                S.op('pe', I('matmul', out=P[1][:, 0:96], lhsT=Um, rhs=f2(lf), start=True, stop=True), rd=GK + ['consts'], wr=[('P', 1)])
                S.op('pe', I('matmul', out=P[1][:, 96:192], lhsT=Lm, rhs=f2(lf), start=True, stop=True), rd=GK + ['consts'], wr=[('P', 1)])
                S.op('pe', I('matmul', out=P[1][:, 192:288], lhsT=ones1[:], rhs=f2(lf), start=True, stop=True), rd=GK + ['ones1'], wr=[('P', 1)])
                cF = P[1][:, 0:96].rearrange("p (t c) -> p t c", c=8)
                cB = P[1][:, 96:192].rearrange("p (t c) -> p t c", c=8)
                S.op('dve', I('tensor_copy', out=btok[:, :, 0:4], in_=cF[:, :, 0:4]), rd=[('P', 1)], wr=GK)
                S.op('dve', I('tensor_copy', out=btok[:, :, 4:8], in_=cB[:, :, 4:8]), rd=[('P', 1)], wr=GK)
                S.op('dve', I('tensor_copy', out=f2(tot), in_=P[1][:, 192:288]), rd=[('P', 1)], wr=GK)
                S.op('dve', I('tensor_tensor', out=atok[:], in0=ig[:], in1=btok[:], op=ALU.subtract), rd=GK, wr=GK)
                S.op('pe', I('transpose', out=P[2][0:96, 0:128], in_=f2(atok), identity=ident), rd=GK + ['consts'], wr=[('P', 2)])
                S.op('dve', I('tensor_reduce', out=amaxc[0:96, :], in_=P[2][0:96, 0:128], axis=AX.X, op=ALU.max), rd=[('P', 2)], wr=GK)
                S.op('dve', I('tensor_scalar', out=diagA[0:96, :], in0=consts[0:96, 0:96], scalar1=amaxc[0:96, 0:1], scalar2=None, op0=ALU.mult),
                     rd=GK + ['consts'], wr=GK)
                S.op('pe', I('matmul', out=P[2][:, 128:224], lhsT=ones1[0:96, :], rhs=diagA[0:96, :], start=True, stop=True),
                     rd=GK + ['ones1'], wr=[('P', 2)])
                S.op('dve', I('tensor_copy', out=aT[0:96, :], in_=P[2][0:96, 0:128]), rd=[('P', 2)], wr=['aT'])
                for dirn in range(2):
                    cur, ck = aT, 'aT'
                    for si_, sh in enumerate([1, 2, 4, 8, 16, 32, 64]):
                        nxt, nk = (pfa, 'pfa') if si_ % 2 == 0 else (pfb, 'pfb')
                        if dirn == 0:
                            S.op('dve', I('tensor_tensor', out=nxt[0:96, sh:128], in0=cur[0:96, sh:128], in1=cur[0:96, 0:128 - sh], op=ALU.max), rd=[ck], wr=[nk])
                            S.op('dve', I('tensor_copy', out=nxt[0:96, 0:sh], in_=cur[0:96, 0:sh]), rd=[ck], wr=[nk])
                        else:
                            S.op('dve', I('tensor_tensor', out=nxt[0:96, 0:128 - sh], in0=cur[0:96, 0:128 - sh], in1=cur[0:96, sh:128], op=ALU.max), rd=[ck], wr=[nk])
                            S.op('dve', I('tensor_copy', out=nxt[0:96, 128 - sh:128], in_=cur[0:96, 128 - sh:128]), rd=[ck], wr=[nk])
                        cur, ck = nxt, nk
                    S.op('pe', I('transpose', out=P[1][:, 288 + dirn * 96:288 + (dirn + 1) * 96], in_=cur[0:96, :], identity=consts[0:96, 0:96]),
                         rd=[ck, 'consts'], wr=[('P', 1)])
                    pv_ = P[1][:, 288 + dirn * 96:288 + (dirn + 1) * 96].rearrange("p (t c) -> p t c", c=8)
                    S.op('dve', I('tensor_copy', out=pmtok[:, :, dirn * 4:dirn * 4 + 4], in_=pv_[:, :, dirn * 4:dirn * 4 + 4]), rd=[('P', 1)], wr=GK)
                S.op('dve', I('tensor_copy', out=f2(amax), in_=P[2][:, 128:224]), rd=[('P', 2)], wr=GK)
                S.op('dve', I('tensor_tensor', out=ml[:], in0=tot[:], in1=amax[:], op=ALU.add), rd=GK, wr=GK)
                S.op('dve', I('tensor_tensor', out=wloc[:], in0=atok[:], in1=amax[:], op=ALU.subtract), rd=GK, wr=GK)
                S.op('act', I('activation', out=f2(wloc), in_=f2(wloc), func=AF.Exp), rd=GK, wr=GK)
                S.op('dve', I('tensor_scalar', out=f2(wloc), in0=f2(wloc), scalar1=SCK, scalar2=None, op0=ALU.mult), rd=GK, wr=GK)
                for si, (t0, N, samp) in enumerate(SEQS):
                    for d in range(2):
                        cs = slice(d * 4, d * 4 + 4)
                        order = list(range(N)) if d == 0 else list(reversed(range(N)))
                        for oi, n in enumerate(order):
                            t = t0 + n
                            if oi == 0:
                                if samp:
                                    S.op('dve', I('tensor_copy', out=mprev[:, t, cs], in_=sm('m0', d * 4, d * 4 + 4)), rd=GK + ['smalls'], wr=GK)
                                else:
                                    S.op('dve', I('memset', ap=mprev[:, t, cs], constant=0.0), rd=GK, wr=GK)
                            S.op('dve', I('tensor_tensor', out=tmp4[:], in0=tot[:, t, cs], in1=mprev[:, t, cs], op=ALU.add), rd=GK, wr=GK)
                            S.op('dve', I('tensor_tensor', out=mnew[:, t, cs], in0=tmp4[:], in1=ml[:, t, cs], op=ALU.max), rd=GK, wr=GK)
                            if oi + 1 < N:
                                S.op('dve', I('tensor_copy', out=mprev[:, t0 + order[oi + 1], cs], in_=mnew[:, t, cs]), rd=GK, wr=GK)
                            elif not samp:
                                S.op('dve', I('tensor_copy', out=mfin[:, si - 1, cs], in_=mnew[:, t, cs]), rd=GK, wr=GK)
                S.op('dve', I('tensor_tensor', out=a1[:], in0=tot[:], in1=mprev[:], op=ALU.add), rd=GK, wr=GK)
                S.op('dve', I('tensor_tensor', out=a1[:], in0=a1[:], in1=mnew[:], op=ALU.subtract), rd=GK, wr=GK)
                S.op('act', I('activation', out=f2(a1), in_=f2(a1), func=AF.Exp), rd=GK, wr=GK)
                S.op('dve', I('tensor_tensor', out=a2[:], in0=ml[:], in1=mnew[:], op=ALU.subtract), rd=GK, wr=GK)
                S.op('act', I('activation', out=f2(a2), in_=f2(a2), func=AF.Exp), rd=GK, wr=GK)
                S.op('dve', I('tensor_tensor', out=mx[:], in0=pmtok[:], in1=mprev[:], op=ALU.max), rd=GK, wr=GK)
                S.op('dve', I('tensor_tensor', out=enm[:], in0=amax[:], in1=mx[:], op=ALU.subtract), rd=GK, wr=GK)
                S.op('act', I('activation', out=f2(enm), in_=f2(enm), func=AF.Exp), rd=GK, wr=GK)
                S.op('dve', I('tensor_tensor', out=wint[:], in0=mprev[:], in1=mx[:], op=ALU.subtract), rd=GK, wr=GK)
                S.op('act', I('activation', out=f2(wint), in_=f2(wint), func=AF.Exp), rd=GK, wr=GK)
                S.op('dve', I('tensor_tensor', out=flr[:], in0=btok[:], in1=mx[:], op=ALU.add), rd=GK, wr=GK)
                S.op('act', I('activation', out=f2(flr), in_=f2(flr), func=AF.Exp, scale=-1.0), rd=GK, wr=GK)
                S.op('dve', I('tensor_tensor', out=wa2[:], in0=wloc[:], in1=a2[:], op=ALU.mult), rd=GK, wr=GK)
                S.dma('sp', osm_d, mfin[0:1].rearrange("p a b -> p (a b)"), rd=GK, wr=['osm'])
                S.op('dve', I('memset', ap=vext[:, :, 128:129], constant=1.0), wr=['vone'])
                S.op('dve', I('memset', ap=vext[:, :, 129:130], constant=0.0), wr=['vzero'])


            SEGS = [(0, 1024), (1024, 1280), (1280, 1536)]
            def proj(h):
                slQ = ring_slot()
                vQ = ring[slQ][:, 0:2048].rearrange("p (k n) -> p k n", k=8)
                S.dma('pool', vQ[:, :, 0:128], w_in[:, :, 3072 + h * 128:3072 + (h + 1) * 128], wr=[('ring', slQ)])
                S.dma('pool', vQ[:, :, 128:256], w_in[:, :, 3584 + h * 128:3584 + (h + 1) * 128], wr=[('ring', slQ)])
                slV = ring_slot()
                vV = ring[slV][:, 0:2048].rearrange("p (k n) -> p k n", k=8)
                S.dma('pool', vV[:, :, 0:128], w_in[:, :, 4096 + h * 128:4096 + (h + 1) * 128], wr=[('ring', slV)])
                S.dma('pool', vV[:, :, 128:256], w_in[:, :, 4608 + h * 128:4608 + (h + 1) * 128], wr=[('ring', slV)])
                for c in range(2):
                    for g in range(3):
                        gs = slice(g * 512, (g + 1) * 512)
                        for kc in range(8):
                            S.op('pe', I('matmul', out=P[g % 2][:], lhsT=vQ[:, kc, c * 128:(c + 1) * 128], rhs=uT[:, kc, gs], start=(kc == 0), stop=(kc == 7)),
                                 rd=[('ring', slQ), ('u', kc, g)], wr=[('P', g % 2)], inc=(kc == 7))
                        copy_op('act' if g % 2 == 0 else 'dve', xpre[:, gs], P[g % 2][:], [('P', g % 2)], ['xpre'])
                    ch = c * 4 + h
                    cw = lambda j: sm('conv', ch * 4 + j, ch * 4 + j + 1)
                    for (s0, e0) in SEGS:
                        S.op('act', I('activation', out=ycv[:, s0:e0], in_=xpre[:, s0:e0], func=AF.Identity, bias=cw(3), scale=cw(1)),
                             rd=['xpre', 'smalls'], wr=['ycv'])
                        S.op('dve', I('scalar_tensor_tensor', out=ycv[:, s0 + 1:e0], in0=xpre[:, s0:e0 - 1], scalar=cw(0), in1=ycv[:, s0 + 1:e0],
                                      op0=ALU.mult, op1=ALU.add), rd=['xpre', 'ycv', 'smalls'], wr=['ycv'])
                        S.op('dve', I('scalar_tensor_tensor', out=ycv[:, s0:e0 - 1], in0=xpre[:, s0 + 1:e0], scalar=cw(2), in1=ycv[:, s0:e0 - 1],
                                      op0=ALU.mult, op1=ALU.add), rd=['xpre', 'ycv', 'smalls'], wr=['ycv'])
                    if c == 0:
                        S.op('act', I('activation', out=qmT[:], in_=ycv[:], func=AF.Silu), rd=['ycv'], wr=['qmT'])
                    else:
                        S.op('act', I('activation', out=kmT[:], in_=ycv[:], func=AF.Silu), rd=['ycv'], wr=['kmT'])
                for (ta_, tb_) in [(0, 8), (8, 12)]:
                    for t in range(ta_, tb_):
                        S.op('pe', I('transpose', out=Pb[2][:, (t - ta_) * 128:(t - ta_ + 1) * 128], in_=kmT[:, t * 128:(t + 1) * 128], identity=identb[:]),
                             rd=['kmT', 'identb'], wr=[('P', 2)], inc=(t == tb_ - 1))
                    S.op('dve', I('tensor_copy', out=kmtok[:, ta_:tb_, :], in_=Pb[2][:, 0:(tb_ - ta_) * 128].rearrange("p (a b) -> p a b", b=128)),
                         rd=[('P', 2)], wr=['kmtok'])
                for tp in range(NT // 2):
                    pvo = [3, 7, 4, 5][tp % 4]
                    for tt_ in range(2):
                        t = tp * 2 + tt_
                        for kc in range(8):
                            S.op('pe', I('matmul', out=P[pvo][:, tt_ * 256:(tt_ + 1) * 256], lhsT=uT[:, kc, t * 128:(t + 1) * 128], rhs=vV[:, kc, :], start=(kc == 0), stop=(kc == 7)),
                                 rd=[('ring', slV), ('u', kc, t // 4)], wr=[('P', pvo)], inc=(kc == 7 and tt_ == 1))
                    pv4 = P[pvo][:].rearrange("p (a b c) -> p a b c", a=2, b=2)
                    S.op('act', I('activation', out=vext[:, tp * 2:tp * 2 + 2, 0:128], in_=pv4[:, :, 0, :], func=AF.Copy), rd=[('P', pvo)],
                         wr=[('vext', tp * 2), ('vext', tp * 2 + 1)])
                    S.op('act', I('activation', out=mos[:, h % 2, tp * 2:tp * 2 + 2, :], in_=pv4[:, :, 1, :], func=AF.Sigmoid), rd=[('P', pvo)],
                         wr=[('mos', h % 2, tp * 2), ('mos', h % 2, tp * 2 + 1)])

            def loops(h):
                for si, (t0, N, samp) in enumerate(SEQS):
                    p_ = si - 1
                    cver = [0, 0]
                    for d in (1, 0):
                        dh = d * 4 + h
                        if samp:
                            S.op('dve', I('memset', ap=Cst[:, d, 0, 129:130], constant=0.0), wr=[('C', d, 0)])
                            S.dma('sp', Cst[:, d, 0, 0:129], smc_d[:, dh, :], rd=[('C', d, 0)], wr=[('C', d, 0)])
                        else:
                            S.op('dve', I('memset', ap=Cst[:, d, 0, :], constant=0.0), wr=[('C', d, 0)])
                        S.op('act', I('activation', out=Cbf[:, d, 0, 0:130], in_=Cst[:, d, 0, :], func=AF.Copy), rd=[('C', d, 0)], wr=[('Cbf', d, 0)])
                    ordB = [(1, n) for n in reversed(range(N))]
                    ordF = [(0, n) for n in range(N)]
                    steps = [x for pr in zip(ordB, ordF) for x in pr]
                    ns = len(steps)
                    for j, (d, n) in enumerate(steps):
                        t = t0 + n
                        dh = d * 4 + h
                        ts_ = slice(t * 128, (t + 1) * 128)
                        s3 = j % 3
                        s2 = j % 2
                        maskm = Lm if d == 1 else Um
                        qb_ = [0, 1, 4][s3]
                        ib_ = [2, 3, 5][s3]
                        S.op('pe', I('matmul', out=P[qb_][:, 0:128], lhsT=kmT[:, ts_], rhs=qmT[:, ts_], start=True, stop=True),
                             rd=['kmT', 'qmT'], wr=[('P', qb_)])
                        S.op('dve', I('scalar_tensor_tensor', out=sTb[s3][:], in0=P[qb_][:, 0:128], scalar=wloc[:, t, dh:dh + 1], in1=maskm,
                                      op0=ALU.mult, op1=ALU.mult), rd=[('P', qb_), 'consts'] + GK, wr=[('sTb', s3)])
                        S.op('pe', I('matmul', out=P[ib_][:, 0:130], lhsT=sTb[s3][:], rhs=vext[:, t, 0:130], start=True, stop=True),
                             rd=[('sTb', s3), ('vext', t), 'vone', 'vzero'], wr=[('P', ib_)])
                        S.op('act', I('activation', out=ndall[:, t, d, :], in_=P[ib_][:, 0:130], func=AF.Identity, scale=enm[:, t, dh:dh + 1]),
                             rd=[('P', ib_)] + GK, wr=[('nd', t, d)])
                        S.op('dve', I('tensor_scalar', out=wvst[:, j, 0:130], in0=vext[:, t, 0:130], scalar1=wa2[:, t, dh:dh + 1], scalar2=None, op0=ALU.mult),
                             rd=[('vext', t), 'vone', 'vzero'] + GK, wr=[('wvst', j)])

                    def pe_cloc(j):
                        d, n = steps[j]
                        S.op('pe', I('matmul', out=P[4 + j % 2][:, 0:130], lhsT=kmtok[:, t0 + n, :], rhs=wvst[:, j, 0:130], start=True, stop=True),
                             rd=['kmtok', ('wvst', j)], wr=[('P', 4 + j % 2)])

                    CRB = [6, 7, 1]

                    def pe_cross(j):
                        d, n = steps[j]
                        t = t0 + n
                        cb_ = CRB[j % 3]
                        S.op('pe', I('matmul', out=P[cb_][:, 0:130], lhsT=qmT[:, t * 128:(t + 1) * 128], rhs=Cbf[:, d, cver[d], 0:130], start=True, stop=True),
                             rd=['qmT', ('Cbf', d, cver[d])], wr=[('P', cb_)])

                    pe_cloc(0)
                    pe_cross(0)
                    if ns > 1:
                        pe_cross(1)
                    for j, (d, n) in enumerate(steps):
                        t = t0 + n
                        dh = d * 4 + h
                        if j + 1 < ns:
                            pe_cloc(j + 1)
                        cu = cver[d]
                        S.op('dve', I('scalar_tensor_tensor', out=Cst[:, d, 1 - cu, :], in0=Cst[:, d, cu, :], scalar=a1[:, t, dh:dh + 1], in1=P[4 + j % 2][:, 0:130],
                                      op0=ALU.mult, op1=ALU.add), rd=[('P', 4 + j % 2), ('C', d, cu)] + GK, wr=[('C', d, 1 - cu)])
                        S.op('act', I('activation', out=Cbf[:, d, 1 - cu, 0:130], in_=Cst[:, d, 1 - cu, :], func=AF.Copy), rd=[('C', d, 1 - cu)], wr=[('Cbf', d, 1 - cu)])
                        cver[d] = 1 - cu
                        if j + 2 < ns:
                            pe_cross(j + 2)
                        cb_ = CRB[j % 3]
                        S.op('dve', I('scalar_tensor_tensor', out=ndall[:, t, d, :], in0=P[cb_][:, 0:130], scalar=wint[:, t, dh:dh + 1], in1=ndall[:, t, d, :],
                                      op0=ALU.mult, op1=ALU.add), rd=[('P', cb_), ('nd', t, d)] + GK, wr=[('nd', t, d)])
                    if not samp:
                        for d in range(2):
                            S.dma('sp', osc_d[:, p_ * 8 + d * 4 + h, :], Cst[:, d, cver[d], 0:129], rd=[('C', d, cver[d])], wr=[('osc', p_, d * 4 + h)])

            def epilogue(h):
                junk = ndall[:, :, 1, 0:128]
                hmf = ndall[:, :, 0, 0:128]
                hbf = wvst[:, 0:NT, 0:128]
                WVK = [('wvst', j) for j in range(16)]
                NDK = [('nd', t, d) for t in range(NT) for d in range(2)]
                inv = 1.0 / 128.0
                den = ndall[:, :, :, 128]
                S.op('dve', I('scalar_tensor_tensor', out=eb[:, 0], in0=den, scalar=-1.0, in1=den, op0=ALU.mult, op1=ALU.max), rd=NDK, wr=[('eb', 0)])
                fl2 = flr[:].rearrange("p t (d q) -> p t d q", d=2)[:, :, :, h]
                S.op('dve', I('tensor_tensor', out=eb[:, 0], in0=eb[:, 0], in1=fl2, op=ALU.max), rd=[('eb', 0)] + GK, wr=[('eb', 0)])
                S.op('dve', I('reciprocal', out=eb[:, 0], in_=eb[:, 0]), rd=[('eb', 0)], wr=[('eb', 0)])
                for d in range(2):
                    S.op('dve', I('tensor_tensor', out=ndall[:, :, d, 0:128], in0=ndall[:, :, d, 0:128],
                                  in1=eb[:, 0, :, d].unsqueeze(2).to_broadcast([128, NT, 128]), op=ALU.mult), rd=NDK + [('eb', 0)], wr=NDK)
                S.op('dve', I('tensor_tensor', out=hmf, in0=ndall[:, :, 0, 0:128], in1=ndall[:, :, 1, 0:128], op=ALU.add), rd=NDK, wr=NDK)
                S.op('dve', I('tensor_tensor', out=hmf, in0=hmf, in1=mos[:, h % 2], op=ALU.mult), rd=NDK + [('mos', h % 2, t) for t in range(NT)], wr=NDK)
                S.op('dve', I('tensor_reduce', out=st[:, 0, :], in_=hmf, axis=AX.X, op=ALU.add), rd=NDK, wr=[('mst', 0)])
                S.op('act', I('activation', out=junk, in_=hmf, func=AF.Square), rd=NDK, wr=NDK)
                S.op('dve', I('tensor_reduce', out=st[:, 1, :], in_=junk, axis=AX.X, op=ALU.add), rd=NDK, wr=[('mst', 1)])
                S.op('dve', I('tensor_scalar', out=st[:, 2, :], in0=st[:, 0, :], scalar1=inv, scalar2=None, op0=ALU.mult), rd=[('mst', 0)], wr=[('mst', 2)])
                S.op('dve', I('tensor_tensor', out=st[:, 3, :], in0=st[:, 2, :], in1=st[:, 2, :], op=ALU.mult), rd=[('mst', 2)], wr=[('mst', 3)])
                S.op('dve', I('scalar_tensor_tensor', out=st[:, 4, :], in0=st[:, 1, :], scalar=inv, in1=st[:, 3, :], op0=ALU.mult, op1=ALU.subtract),
                     rd=[('mst', 1), ('mst', 3)], wr=[('mst', 4)])
                S.op('act', I('activation', out=st[:, 5, :], in_=st[:, 4, :], func=AF.Sqrt, bias=cv[:, 1:2], scale=1.0), rd=[('mst', 4)] + CVK, wr=[('mst', 5)])
                S.op('dve', I('reciprocal', out=st[:, 5, :], in_=st[:, 5, :]), rd=[('mst', 5)], wr=[('mst', 5)])
                S.op('dve', I('scalar_tensor_tensor', out=st[:, 6, :], in0=st[:, 2, :], scalar=-1.0, in1=st[:, 5, :], op0=ALU.mult, op1=ALU.mult),
                     rd=[('mst', 2), ('mst', 5)], wr=[('mst', 6)])
                for t in range(NT):
                    S.op('act', I('activation', out=hbf[:, t, :], in_=hmf[:, t, :], func=AF.Identity, bias=st[:, 6, t:t + 1], scale=st[:, 5, t:t + 1]),
                         rd=NDK + [('mst', 5), ('mst', 6)], wr=WVK)
                for bi_, (ta_, tb_) in enumerate([(0, 8), (8, 12)]):
                    pe_ = bi_
                    for t in range(ta_, tb_):
                        S.op('pe', I('transpose', out=Pb[pe_][:, (t - ta_) * 128:(t - ta_ + 1) * 128], in_=hbf[:, t, :], identity=identb[:]),
                             rd=WVK + ['identb'], wr=[('P', pe_)], inc=(t == tb_ - 1))
                    S.op('act', I('activation', out=hmT[:, h, ta_ * 128:tb_ * 128], in_=Pb[pe_][:, 0:(tb_ - ta_) * 128], func=AF.Identity,
                                  scale=sm('mgn', h, h + 1)), rd=[('P', pe_), 'smalls'], wr=[('hmT', h, 0), ('hmT', h, 1), ('hmT', h, 2)])
            proj(0)
            gate_prep()
            _barrier(S)
            gp.close()
            wvst = sbt("m_wvst", [128, 16, 136], BF16)
            for h in range(4):
                loops(h)
                if h + 1 < 4:
                    proj(h + 1)
                epilogue(h)
            _barrier(S)

        with ExitStack() as ph:
            merged = ph.enter_context(nc.sbuf_tensor("merged", [128, 8, T], BF16))
            s0t = [ph.enter_context(nc.sbuf_tensor("s0t%d" % i, [128, 512], F32)) for i in range(2)]
            s1t = [ph.enter_context(nc.sbuf_tensor("s1t%d" % i, [128, 512], F32)) for i in range(2)]
            k = 0
            for dc in range(8):
                sl = ring_slot()
                vru = ring[sl][:, 0:1024].rearrange("p (k n) -> p k n", k=8)
                vmu = ring[sl][:, 1024:1536].rearrange("p (k n) -> p k n", k=4)
                vb0 = ring[sl][:, 1536:2560].rearrange("p (k n) -> p k n", k=8)
                vb1 = ring[sl][:, 2560:3584].rearrange("p (k n) -> p k n", k=8)
                cs_ = slice(dc * 128, (dc + 1) * 128)
                S.dma('pool', vru, w_ru[:, :, cs_], wr=[('ring', sl)])
                S.dma('pool', vmu, w_mu[:, :, cs_], wr=[('ring', sl)])
                S.dma('pool', vb0, w_in[:, :, 5136 + dc * 128:5136 + (dc + 1) * 128], wr=[('ring', sl)])
                S.dma('pool', vb1, w_in[:, :, 6160 + dc * 128:6160 + (dc + 1) * 128], wr=[('ring', sl)])
                for g in range(3):
                    gs = slice(g * 512, (g + 1) * 512)
                    pb = 4 * (k % 2)
                    s = k % 2
                    k += 1
                    for kc in range(8):
                        S.op('pe', I('matmul', out=P[pb][:], lhsT=vru[:, kc, :], rhs=rT[:, kc, gs], start=(kc == 0), stop=(kc == 7)),
                             rd=[('ring', sl), ('rT', kc, g)], wr=[('P', pb)], inc=(kc == 7))
                    for kc in range(4):
                        S.op('pe', I('matmul', out=P[pb + 1][:], lhsT=vmu[:, kc, :], rhs=hmT[:, kc, gs], start=(kc == 0), stop=(kc == 3)),
                             rd=[('ring', sl), ('hmT', kc, g)], wr=[('P', pb + 1)], inc=(kc == 3))
                    for kc in range(8):
                        S.op('pe', I('matmul', out=P[pb + 2][:], lhsT=vb0[:, kc, :], rhs=uT[:, kc, gs], start=(kc == 0), stop=(kc == 7)),
                             rd=[('ring', sl), ('u', kc, g)], wr=[('P', pb + 2)], inc=(kc == 7))
                    for kc in range(8):
                        S.op('pe', I('matmul', out=P[pb + 3][:], lhsT=vb1[:, kc, :], rhs=uT[:, kc, gs], start=(kc == 0), stop=(kc == 7)),
                             rd=[('ring', sl), ('u', kc, g)], wr=[('P', pb + 3)], inc=(kc == 7))
                    S.op('act', I('activation', out=s0t[s][:], in_=P[pb + 2][:], func=AF.Sigmoid), rd=[('P', pb + 2)], wr=[('s0t', s)])
                    S.op('act', I('activation', out=s1t[s][:], in_=P[pb + 3][:], func=AF.Sigmoid), rd=[('P', pb + 3)], wr=[('s1t', s)])
                    S.op('dve', I('tensor_tensor', out=s0t[s][:], in0=s0t[s][:], in1=P[pb][:], op=ALU.mult), rd=[('s0t', s), ('P', pb)], wr=[('s0t', s)])
                    S.op('dve', I('tensor_tensor', out=s1t[s][:], in0=s1t[s][:], in1=P[pb + 1][:], op=ALU.mult), rd=[('s1t', s), ('P', pb + 1)], wr=[('s1t', s)])
                    S.op('dve', I('tensor_tensor', out=merged[:, dc, gs], in0=s0t[s][:], in1=s1t[s][:], op=ALU.add),
                         rd=[('s0t', s), ('s1t', s)], wr=[('mg', dc, g)])
            pstat_m = []
            accm = [ph.enter_context(nc.sbuf_tensor("accm%d" % i, [128, 512], F32)) for i in range(3)]
            for dc in range(8):
                sl = ring_slot()
                vo = ring[sl][:, 0:1024].rearrange("p (k n) -> p k n", k=8)
                S.dma('pool', vo, w_o[:, :, dc * 128:(dc + 1) * 128], wr=[('ring', sl)])
                for g in range(3):
                    gs = slice(g * 512, (g + 1) * 512)
                    st_ = 0 if g < 2 else 1
                    pi = k % 2
                    k += 1
                    for kc in range(8):
                        S.op('pe', I('matmul', out=P[pi][:], lhsT=vo[:, kc, :], rhs=merged[:, kc, gs], start=(kc == 0), stop=(kc == 7)),
                             rd=[('ring', sl), ('mg', kc, g)], wr=[('P', pi)], inc=(kc == 7))
                    S.op('dve', I('scalar_tensor_tensor', out=xT[:, dc, gs], in0=P[pi][:], scalar=modG[:, 1, dc, st_:st_ + 1], in1=xT[:, dc, gs],
                                  op0=ALU.mult, op1=ALU.add), rd=[('P', pi), ('x', dc, g)] + MODK, wr=[('x', dc, g)])
                    pstat_m.append((dc, g))
                    if len(pstat_m) > 3:
                        emit_stat(s0t, *pstat_m.pop(0), (5, 6, 7), accm)
            while pstat_m:
                emit_stat(s0t, *pstat_m.pop(0), (5, 6, 7), accm)
            _barrier(S)
        mxs.close()

        with ExitStack() as ph:
            norm_phase(ph, 2, banks=(5, 6, 7), pre=True)
            ffn_phase(ph, 2, w1b, w3b, w2b, stat_banks=(2, 3, 6))
            _barrier(S)

        with ExitStack() as ph:
            yT = [ph.enter_context(nc.sbuf_tensor("yT%d" % i, [128, 8, 512], F32)) for i in range(2)]
            yo = [ph.enter_context(nc.sbuf_tensor("yo%d" % i, [128, D], F32)) for i in range(3)]
            cnt = [0]
            pend_tr = []

            def trans_group(g):
                yb = yT[g % 2]
                for tt in range(4):
                    t = g * 4 + tt
                    o = yo[cnt[0] % 3]
                    ok = ('yo', cnt[0] % 3)
                    pb0 = 0 if cnt[0] % 2 == 0 else 4
                    cnt[0] += 1
                    for half in range(2):
                        pbk = pb0 + half
                        for q in range(4):
                            d2 = half * 4 + q
                            S.op('pe', I('transpose', out=P[pbk][:, q * 128:(q + 1) * 128],
                                         in_=yb[:, d2, tt * 128:(tt + 1) * 128], identity=ident),
                                 rd=[('yT', g % 2, d2), 'consts'], wr=[('P', pbk)], inc=(q == 3))
                        if half == 0:
                            S.op('dve', I('tensor_copy', out=o[:, 0:512], in_=P[pbk][:]), rd=[('P', pbk)], wr=[ok + (0,)])
                        else:
                            S.op('act', I('activation', out=o[:, 512:1024], in_=P[pbk][:], func=AF.Copy), rd=[('P', pbk)], wr=[ok + (1,)])
                    S.dma('sp', y_d[t * 128:(t + 1) * 128, :], o[:], rd=[ok + (0,), ok + (1,)], wr=[('y', t)])

            def fin_cb(g, dc, tmb, tmk):
                S.op('act', I('activation', out=yT[g % 2][:, dc, :], in_=tmb, func=AF.Copy, scale=sm('gf', dc, dc + 1)),
                     rd=[tmk, 'smalls'], wr=[('yT', g % 2, dc)])
                if dc == 7:
                    pend_tr.append(g)
                    if len(pend_tr) > 1:
                        trans_group(pend_tr.pop(0))

            norm_phase(ph, 3, final_cb=fin_cb, banks=(2, 3, 6), pre=True)
            while pend_tr:
                trans_group(pend_tr.pop(0))
            outs = [('y', t) for t in range(NT)] + [('dbgo', k) for k in dbg_outs] + ['osm'] + [('osr', p, d, h) for p in range(2) for d in range(2) for h in range(4)] + [('osc', p, dh) for p in range(2) for dh in range(8)]
            S.finish(outs)
    return nc, list(dbg_outs.keys())


def _consts():
    a = np.arange(128)
    ident = np.eye(128, dtype=np.float32)
    U = (a[None, :] >= a[:, None]).astype(np.float32)
    L = (a[None, :] <= a[:, None]).astype(np.float32)
    negU = np.where(U > 0, 0.0, NEG).astype(np.float32)
    negL = np.where(L > 0, 0.0, NEG).astype(np.float32)
    iota_row = np.broadcast_to(a[None, :].astype(np.float32), (128, 128))
    diff = (a[None, :] - a[:, None]).astype(np.float32)
    Lq = 1024
    r = np.repeat(np.arange(Lq // 64, dtype=np.float32), 64)
    col = (np.arange(Lq) % 64).astype(np.float32)
    n_f = 32
    freqs = (np.float32(10000.0) ** (-np.arange(n_f, dtype=np.float32) / np.float32(n_f))).astype(np.float32)
    ang = np.concatenate([r[:, None] * freqs, col[:, None] * freqs], axis=-1).astype(np.float32)
    cos = np.cos(ang).astype(np.float32).reshape(8, 128, 64).transpose(1, 0, 2).reshape(128, 512)
    sin = np.sin(ang).astype(np.float32).reshape(8, 128, 64).transpose(1, 0, 2).reshape(128, 512)
    return np.ascontiguousarray(np.concatenate([ident, U, L, negU, negL, iota_row, diff, cos, sin], axis=1), dtype=np.float32)


def _pp(v, nch):
    return np.ascontiguousarray(np.asarray(v, dtype=np.float32).reshape(nch, 128).T)


def _bc(v):
    v = np.asarray(v, dtype=np.float32).reshape(-1)
    return np.ascontiguousarray(np.broadcast_to(v[None, :], (128, v.size)))


_PROG = {}


def kernel(**inputs):
    f = {k: np.asarray(v) for k, v in inputs.items()}
    if 'prog' not in _PROG:
        _PROG['prog'] = build_program()
    nc, _ = _PROG['prog']
    consts = _consts()
    shared = {
        "consts": consts,
        "w_ada": np.ascontiguousarray(f['w_ada'][0]),
        "w1_ffn1": np.ascontiguousarray(f['w1_ffn1'][0]), "w3_ffn1": np.ascontiguousarray(f['w3_ffn1'][0]),
        "w2_ffn1": np.ascontiguousarray(f['w2_ffn1'][0]),
        "w1_ffn2": np.ascontiguousarray(f['w1_ffn2'][0]), "w3_ffn2": np.ascontiguousarray(f['w3_ffn2'][0]),
        "w2_ffn2": np.ascontiguousarray(f['w2_ffn2'][0]),
        "w_in": np.ascontiguousarray(f['w_in'][0]), "w_ret_up": np.ascontiguousarray(f['w_ret_up'][0]),
        "w_m_up": np.ascontiguousarray(f['w_m_up'][0]), "w_out": np.ascontiguousarray(f['w_out'][0]),
    }
    conv = np.concatenate([f['conv_w'][0], f['conv_b'][0][None, :]], axis=0)
    conv_pp = np.ascontiguousarray(conv.reshape(4, 8, 128).transpose(2, 1, 0).reshape(128, 32))
    in_maps = []
    for b in range(8):
        x = np.concatenate([f['x_sample'][b], f['x_prompt'][2 * b], f['x_prompt'][2 * b + 1]], axis=0)
        cT = np.stack([_pp(f['c'][b], 8), _pp(f['c_ctx'], 8)], axis=2).reshape(128, 16)
        smalls = np.concatenate([
            cT, _pp(f['b_ada'][0], 72), _pp(f['norm_ffn1'][0], 8), _pp(f['norm_mix'][0], 8), _pp(f['norm_ffn2'][0], 8),
            _pp(f['norm_final'], 8), conv_pp, _bc(f['ret_decay_logit'][0]), _bc(f['b_igate'][0]), _bc(f['b_fgate'][0]),
            _pp(f['ret_gn'][0], 8), _pp(f['m_gn'][0], 4), _bc(f['state_mlstm_m'][b, 0])], axis=1)
        sret = f['state_ret'][b, 0].reshape(8, 128, 256).transpose(1, 0, 2)
        smc = np.concatenate([f['state_mlstm_C'][b, 0].reshape(8, 128, 128).transpose(2, 0, 1),
                              f['state_mlstm_n'][b, 0].reshape(8, 128).T[:, :, None]], axis=2)
        m = dict(shared)
        m.update({"x": np.ascontiguousarray(x, dtype=np.float32), "smalls": np.ascontiguousarray(smalls, dtype=np.float32),
                  "sret": np.ascontiguousarray(sret, dtype=np.float32), "smc": np.ascontiguousarray(smc, dtype=np.float32)})
        in_maps.append(m)
    res = run_bass_kernel_spmd(nc, in_maps, core_ids=list(range(8)))
    _PROG['last'] = res
    y_s = np.zeros((8, 1024, 1024), np.float32)
    y_p = np.zeros((16, 256, 1024), np.float32)
    n_sr = np.zeros((16, 1, 2, 4, 128, 256), np.float32)
    n_C = np.zeros((16, 1, 2, 4, 128, 128), np.float32)
    n_n = np.zeros((16, 1, 2, 4, 128), np.float32)
    n_m = np.zeros((16, 1, 2, 4), np.float32)
    for b in range(8):
        r = res.results[b]
        y = r["y"]
        y_s[b] = y[0:1024]
        y_p[2 * b] = y[1024:1280]
        y_p[2 * b + 1] = y[1280:1536]
        osr = r["o_sr"].reshape(128, 2, 2, 4, 256)
        osc = r["o_sc"].reshape(128, 2, 2, 4, 129)
        osm = r["o_sm"].reshape(2, 2, 4)
        for p in range(2):
            n_sr[2 * b + p, 0] = osr[:, p].transpose(1, 2, 0, 3)
            n_C[2 * b + p, 0] = osc[:, p, :, :, 0:128].transpose(1, 2, 3, 0)
            n_n[2 * b + p, 0] = osc[:, p, :, :, 128].transpose(1, 2, 0)
            n_m[2 * b + p, 0] = osm[p]
    return (y_p, y_s, n_sr, n_C, n_n, n_m)
```
